# Optimizing a Trainium2 kernel written in Bass

```python
import math
import jax, jax.numpy as jnp
from jax import lax
import numpy as np

D_MODEL = 1024
BATCH = 16
SEQ = 2048
DEPTH = 2

CHUNK = 64
MEM_LEN = 256
EPS = 1e-6

FOX_HEAD_DIM = 64
FOX_HEADS = D_MODEL // 128
FOX_WIDTH = FOX_HEADS * FOX_HEAD_DIM
FOX_Q_BLOCK = 128

MLSTM_HEADS = 4
MLSTM_HEAD_DIM = D_MODEL // 8
MLSTM_WIDTH = MLSTM_HEADS * MLSTM_HEAD_DIM
MLSTM_CHUNK = CHUNK
CONV_WIDTH = 4

GMLP_GROUPS = 4
GMLP_GROUP_DIM = D_MODEL // 8
GMLP_WIDTH = GMLP_GROUPS * GMLP_GROUP_DIM
GMLP_SPAN = 128

N_BRANCH = 3
BRANCH_WIDTH = D_MODEL // 2

XATTN_HEADS = 4
XATTN_HEAD_DIM = D_MODEL // XATTN_HEADS

D_FF = 4 * D_MODEL

NORM_MIX_PRE, NORM_MIX_POST, NORM_X_PRE, NORM_X_POST, NORM_MEM, NORM_FF_PRE, NORM_FF_POST = 0, 1, 2, 3, 4, 5, 6
N_NORMS = 7

IN_SPLITS = (
    ("fox_q", FOX_WIDTH), ("fox_k", FOX_WIDTH), ("fox_v", FOX_WIDTH), ("fox_f", FOX_HEADS),
    ("ml_q", MLSTM_WIDTH), ("ml_k", MLSTM_WIDTH), ("ml_v", MLSTM_WIDTH),
    ("ml_i", MLSTM_HEADS), ("ml_f", MLSTM_HEADS), ("ml_o", MLSTM_WIDTH),
    ("g_u", GMLP_WIDTH), ("g_v", GMLP_WIDTH),
    ("gate", N_BRANCH * D_MODEL),
)
D_IN = sum(size for _, size in IN_SPLITS)

kernel_name = "hybrid_fox_mlstm_gmlp_streaming_encoder"


def rms_norm(x, g):
    x32 = x.astype(jnp.float32)
    y = x32 * lax.rsqrt(jnp.mean(x32 * x32, axis=-1, keepdims=True) + EPS)
    return y.astype(x.dtype) * g


def layer_norm(x, g):
    x32 = x.astype(jnp.float32)
    mu = jnp.mean(x32, axis=-1, keepdims=True)
    xc = x32 - mu
    y = xc * lax.rsqrt(jnp.mean(xc * xc, axis=-1, keepdims=True) + EPS)
    return y.astype(x.dtype) * g


def split_columns(z):
    parts = []
    start = 0
    for _, size in IN_SPLITS:
        parts.append(z[..., start:start + size])
        start += size
    return parts


def causal_conv(x, w):
    K = w.shape[0]
    S = x.shape[1]
    xp = jnp.pad(x, ((0, 0), (K - 1, 0), (0, 0)))
    y = w[0] * xp[:, 0:S]
    for j in range(1, K):
        y = y + w[j] * xp[:, j:j + S]
    return y


def fox_attention(q, k, v, f_pre):
    B, S, H, dh = q.shape
    F = jnp.cumsum(jax.nn.log_sigmoid(f_pre.astype(jnp.float32)), axis=1)
    F = F.transpose(0, 2, 1)
    q = q.transpose(0, 2, 1, 3) * (dh ** -0.5)
    k = k.transpose(0, 2, 1, 3)
    v = v.transpose(0, 2, 1, 3)
    outs = []
    for blk in range(S // FOX_Q_BLOCK):
        q0 = blk * FOX_Q_BLOCK
        q1 = q0 + FOX_Q_BLOCK
        s = jnp.einsum('bhqd,bhkd->bhqk', q[:, :, q0:q1], k[:, :, :q1]).astype(jnp.float32)
        s = s + F[:, :, q0:q1, None] - F[:, :, None, :q1]
        causal = jnp.arange(q0, q1)[:, None] >= jnp.arange(q1)[None, :]
        s = jnp.where(causal, s, -jnp.inf)
        p = jax.nn.softmax(s, axis=-1).astype(v.dtype)
        outs.append(jnp.einsum('bhqk,bhkd->bqhd', p, v[:, :, :q1]))
    return jnp.concatenate(outs, axis=1).reshape(B, S, H * dh)


def mlstm(q, k, v, i_pre, f_pre):
    B, S, H, d = q.shape
    L = MLSTM_CHUNK
    NC = S // L
    out_dtype = q.dtype

    def to_chunks(a):
        a = a.astype(jnp.float32).reshape((B, NC, L, H) + a.shape[3:])
        return jnp.moveaxis(a, (1, 3), (0, 2))

    qc = to_chunks(q)
    kc = to_chunks(k) * (d ** -0.5)
    vc = to_chunks(v)
    ic = to_chunks(i_pre)
    lfc = to_chunks(jax.nn.log_sigmoid(f_pre.astype(jnp.float32)))
    tril = jnp.tril(jnp.ones((L, L), dtype=bool))

    def step(carry, xs):
        C, n, m_prev = carry
        qb, kb, vb, ib, lfb = xs
        b = jnp.cumsum(lfb, axis=-1)
        D = jnp.where(tril, b[..., :, None] - b[..., None, :] + ib[..., None, :], -jnp.inf)
        inter = b + m_prev[..., None]
        m = jnp.maximum(inter, jnp.max(D, axis=-1))
        w_inter = jnp.exp(inter - m)
        P = jnp.einsum('bhtd,bhsd->bhts', qb, kb) * jnp.exp(D - m[..., None])
        num = w_inter[..., None] * jnp.einsum('bhtd,bhde->bhte', qb, C) + jnp.einsum('bhts,bhse->bhte', P, vb)
        den = w_inter * jnp.einsum('bhtd,bhd->bht', qb, n) + jnp.sum(P, axis=-1)
        h = num / jnp.maximum(jnp.abs(den), jnp.exp(-m))[..., None]
        m_new = m[..., -1]
        decay = jnp.exp(b[..., -1] + m_prev - m_new)
        w_s = jnp.exp(b[..., -1:] - b + ib - m_new[..., None])
        C_new = decay[..., None, None] * C + jnp.einsum('bhs,bhsd,bhse->bhde', w_s, kb, vb)
        n_new = decay[..., None] * n + jnp.einsum('bhs,bhsd->bhd', w_s, kb)
        return (C_new, n_new, m_new), h

    init = (jnp.zeros((B, H, d, d), jnp.float32),
            jnp.zeros((B, H, d), jnp.float32),
            jnp.zeros((B, H), jnp.float32))
    _, h = lax.scan(step, init, (qc, kc, vc, ic, lfc))
    h = jnp.moveaxis(h, (0, 2), (1, 3)).reshape(B, S, H, d)
    return h.astype(out_dtype)


def gmlp_sgu(u, v, norm_g, ws, bs):
    B, S, _ = u.shape
    u = jax.nn.gelu(u)
    v = layer_norm(jax.nn.gelu(v), norm_g)
    NG = S // GMLP_SPAN
    vg = v.reshape(B, NG, GMLP_SPAN, GMLP_GROUPS, GMLP_GROUP_DIM)
    pos_chunk = jnp.arange(GMLP_SPAN) // CHUNK
    mask = pos_chunk[:, None] >= pos_chunk[None, :]
    ws = jnp.where(mask, ws, 0)
    mixed = jnp.einsum('gts,bnsgc->bntgc', ws, vg) + bs.T[:, :, None]
    return u * mixed.reshape(B, S, GMLP_WIDTH)


def hybrid_mixer(h, w_in, b_in, conv_w, mlstm_g, gmlp_g, gmlp_ws, gmlp_bs, w_branch, w_out):
    B, S, _ = h.shape
    z = h @ w_in + b_in
    (fq, fk, fv, ff, mq, mk, mv, mi, mf, mo, gu, gv, gate) = split_columns(z)

    shp_f = (B, S, FOX_HEADS, FOX_HEAD_DIM)
    y_fox = fox_attention(fq.reshape(shp_f), fk.reshape(shp_f), fv.reshape(shp_f), ff)

    qk = jax.nn.silu(causal_conv(jnp.concatenate([mq, mk], axis=-1), conv_w))
    mq, mk = qk[..., :MLSTM_WIDTH], qk[..., MLSTM_WIDTH:]
    shp_m = (B, S, MLSTM_HEADS, MLSTM_HEAD_DIM)
    hm = mlstm(mq.reshape(shp_m), mk.reshape(shp_m), mv.reshape(shp_m), mi, mf)
    hm = rms_norm(hm, mlstm_g.reshape(MLSTM_HEADS, MLSTM_HEAD_DIM)).reshape(B, S, MLSTM_WIDTH)
    y_ml = jax.nn.sigmoid(mo) * hm

    y_g = gmlp_sgu(gu, gv, gmlp_g, gmlp_ws, gmlp_bs)

    ys = jnp.stack([y_fox, y_ml, y_g], axis=2)
    branches = jnp.einsum('bsnc,ncd->bsnd', ys, w_branch)
    gates = jax.nn.sigmoid(gate.reshape(B, S, N_BRANCH, D_MODEL))
    merged = jnp.sum(gates * branches, axis=2)
    return merged @ w_out


def cross_attention(h, mem_n, w_q, w_kv, w_o):
    B, S, _ = h.shape
    M = mem_n.shape[1]
    q = (h @ w_q).reshape(B, S, XATTN_HEADS, XATTN_HEAD_DIM)
    kv = mem_n @ w_kv
    k = kv[..., :D_MODEL].reshape(B, M, XATTN_HEADS, XATTN_HEAD_DIM)
    v = kv[..., D_MODEL:].reshape(B, M, XATTN_HEADS, XATTN_HEAD_DIM)
    s = jnp.einsum('bqhd,bkhd->bhqk', q, k).astype(jnp.float32) * (XATTN_HEAD_DIM ** -0.5)
    p = jax.nn.softmax(s, axis=-1).astype(v.dtype)
    o = jnp.einsum('bhqk,bkhd->bqhd', p, v).reshape(B, S, D_MODEL)
    return o @ w_o


def setup_inputs(seed: int = 0) -> dict:
    key = jax.random.key(seed)
    ks = jax.random.split(key, 20)
    f32 = jnp.float32

    def normal(k, shape, scale):
        return jax.random.normal(k, shape, f32) * scale

    x = normal(ks[0], (BATCH, SEQ, D_MODEL), 1.0)
    mem = normal(ks[1], (BATCH, MEM_LEN, D_MODEL), 1.0)
    norms = 1.0 + normal(ks[2], (DEPTH, N_NORMS, D_MODEL), 0.05)
    w_in = normal(ks[3], (DEPTH, D_MODEL, D_IN), D_MODEL ** -0.5)
    seg_keys = jax.random.split(ks[4], len(IN_SPLITS))
    parts = []
    for (name, size), kk in zip(IN_SPLITS, seg_keys):
        if name == "fox_f":
            parts.append(jax.random.uniform(kk, (DEPTH, size), f32, 1.0, 4.0))
        elif name == "ml_f":
            parts.append(jax.random.uniform(kk, (DEPTH, size), f32, 3.0, 6.0))
        elif name == "ml_i":
            parts.append(normal(kk, (DEPTH, size), 0.1))
        else:
            parts.append(normal(kk, (DEPTH, size), 0.02))
    b_in = jnp.concatenate(parts, axis=-1)
    conv_w = normal(ks[5], (DEPTH, CONV_WIDTH, 2 * MLSTM_WIDTH), CONV_WIDTH ** -0.5)
    mlstm_norm = 1.0 + normal(ks[6], (DEPTH, MLSTM_WIDTH), 0.05)
    gmlp_norm = 1.0 + normal(ks[7], (DEPTH, GMLP_WIDTH), 0.05)
    gmlp_ws = normal(ks[8], (DEPTH, GMLP_GROUPS, GMLP_SPAN, GMLP_SPAN), GMLP_SPAN ** -0.5)
    gmlp_bs = 1.0 + normal(ks[9], (DEPTH, GMLP_GROUPS, GMLP_SPAN), 0.1)
    w_branch = normal(ks[10], (DEPTH, N_BRANCH, BRANCH_WIDTH, D_MODEL), BRANCH_WIDTH ** -0.5)
    w_out = normal(ks[11], (DEPTH, D_MODEL, D_MODEL), D_MODEL ** -0.5)
    w_xq = normal(ks[12], (DEPTH, D_MODEL, D_MODEL), D_MODEL ** -0.5)
    w_xkv = normal(ks[13], (DEPTH, D_MODEL, 2 * D_MODEL), D_MODEL ** -0.5)
    w_xo = normal(ks[14], (DEPTH, D_MODEL, D_MODEL), D_MODEL ** -0.5)
    w_ff1 = normal(ks[15], (DEPTH, D_MODEL, D_FF), D_MODEL ** -0.5)
    w_ff2 = normal(ks[16], (DEPTH, D_FF, D_MODEL), D_FF ** -0.5)
    return {"x": x, "mem": mem, "norms": norms, "w_in": w_in, "b_in": b_in,
            "conv_w": conv_w, "mlstm_norm": mlstm_norm, "gmlp_norm": gmlp_norm,
            "gmlp_ws": gmlp_ws, "gmlp_bs": gmlp_bs, "w_branch": w_branch, "w_out": w_out,
            "w_xq": w_xq, "w_xkv": w_xkv, "w_xo": w_xo, "w_ff1": w_ff1, "w_ff2": w_ff2}


def reference(x, mem, norms, w_in, b_in, conv_w, mlstm_norm, gmlp_norm, gmlp_ws, gmlp_bs,
              w_branch, w_out, w_xq, w_xkv, w_xo, w_ff1, w_ff2):
    for l in range(DEPTH):
        g = norms[l]
        h = rms_norm(x, g[NORM_MIX_PRE])
        y = hybrid_mixer(h, w_in[l], b_in[l], conv_w[l], mlstm_norm[l], gmlp_norm[l],
                         gmlp_ws[l], gmlp_bs[l], w_branch[l], w_out[l])
        x = x + rms_norm(y, g[NORM_MIX_POST])
        h = rms_norm(x, g[NORM_X_PRE])
        mem_n = rms_norm(mem, g[NORM_MEM])
        y = cross_attention(h, mem_n, w_xq[l], w_xkv[l], w_xo[l])
        x = x + rms_norm(y, g[NORM_X_POST])
        h = rms_norm(x, g[NORM_FF_PRE])
        y = jnp.square(jax.nn.relu(h @ w_ff1[l])) @ w_ff2[l]
        x = x + rms_norm(y, g[NORM_FF_POST])
    return x
```

```python
import math
import numpy as np
import concourse.bass as bass
import concourse.mybir as mybir
from concourse.bass_utils import run_bass_kernel_spmd
from contextlib import ExitStack

F32 = mybir.dt.float32
BF16 = mybir.dt.bfloat16
AF = mybir.ActivationFunctionType
ALU = mybir.AluOpType

ENGS = ["pe", "act", "dve", "pool", "sp"]

D = 1024
DEPTH = 2
NSEQ = 2
MEM = 256
D_IN = 7696
FOX_Q, FOX_K, FOX_V, FOX_F = 0, 512, 1024, 1536
ML_Q, ML_K, ML_V, ML_I, ML_F, ML_O = 1544, 2056, 2568, 3080, 3084, 3088
G_U, G_V, GATE = 3600, 4112, 4624
EPS = 1e-6


class Tok:
    __slots__ = ("w", "r")

    def __init__(self):
        self.w = None
        self.r = []


def toks(n):
    return [Tok() for _ in range(n)]


class FW:
    def __init__(self, nc, n_dma_sems=32):
        self.nc = nc
        self.ins = {e: [] for e in ENGS}
        self.n_dma_sems = n_dma_sems
        self.dma_count = [0] * n_dma_sems
        self.dma_rr = 0
        self.es = ExitStack()
        self.uid = 0

    def sbuf(self, shape, dtype):
        self.uid += 1
        return self.es.enter_context(self.nc.sbuf_tensor(f"sb{self.uid}", list(shape), dtype))

    def psum(self, shape, dtype):
        self.uid += 1
        return self.es.enter_context(self.nc.psum_tensor(f"ps{self.uid}", list(shape), dtype))

    def _deps(self, reads, writes):
        deps = set()
        for t in reads:
            if t.w is not None:
                deps.add(t.w)
        for t in writes:
            if t.w is not None:
                deps.add(t.w)
            deps.update(t.r)
        return deps

    def _mark(self, me, reads, writes):
        for t in reads:
            if me[0] != "dmasem":
                t.r = [x for x in t.r if x[0] != me[0]]
            t.r.append(me)
        for t in writes:
            t.w = me
            t.r = []

    def op(self, eng, fn, reads=(), writes=()):
        idx = len(self.ins[eng])
        deps = self._deps(reads, writes)
        me = (eng, idx)
        self.ins[eng].append(dict(fn=fn, deps=deps, dma=None, sig=False))
        self._mark(me, reads, writes)
        return me

    def dma(self, eng, fn, reads=(), writes=()):
        half = self.n_dma_sems // 2
        if not hasattr(self, "dma_rr2"):
            self.dma_rr2 = {"sp": 0, "pool": 0, "act": 0}
        base = half if eng == "pool" else 0
        s = base + self.dma_rr2[eng] % half
        self.dma_rr2[eng] += 1
        deps = self._deps(reads, writes)
        if self.dma_count[s] > 0:
            deps.add(("dmasem", s, 16 * self.dma_count[s]))
        self.dma_count[s] += 1
        me = ("dmasem", s, 16 * self.dma_count[s])
        self.ins[eng].append(dict(fn=fn, deps=deps, dma=s, sig=False))
        self._mark(me, reads, writes)
        return me

    def barrier(self):
        deps = set()
        for e in ENGS:
            for i in range(len(self.ins[e]) - 1, -1, -1):
                if self.ins[e][i]["dma"] is None:
                    deps.add((e, i))
                    break
        for s in range(self.n_dma_sems):
            if self.dma_count[s] > 0:
                deps.add(("dmasem", s, 16 * self.dma_count[s]))
        for e in ENGS:
            self.ins[e].append(dict(fn=None, deps=set(deps), dma=None, sig=False))

    def emit(self):
        nc = self.nc
        for e in ENGS:
            for rec in self.ins[e]:
                for d in rec["deps"]:
                    if d[0] != "dmasem" and not (d[0] == "pe" and e == "pe"):
                        self.ins[d[0]][d[1]]["sig"] = True
        for e in ENGS:
            c = 0
            for rec in self.ins[e]:
                if rec["sig"]:
                    c += 1
                rec["cnt"] = c
        es = self.es
        esem = {e: es.enter_context(nc.semaphore(f"sem_{e}")) for e in ENGS}
        dsem = [es.enter_context(nc.semaphore(f"sem_dma{i}")) for i in range(self.n_dma_sems)]
        stats = {e: [0, 0] for e in ENGS}

        def replay(ename, eng):
            seen = {}
            for rec in self.ins[ename]:
                need = {}
                for d in rec["deps"]:
                    if d[0] == "dmasem":
                        key = ("d", d[1])
                        val = d[2]
                    else:
                        if d[0] == "pe" and ename == "pe":
                            continue
                        key = ("e", d[0])
                        val = self.ins[d[0]][d[1]]["cnt"]
                    if val > need.get(key, 0):
                        need[key] = val
                for key, val in need.items():
                    if seen.get(key, 0) >= val:
                        continue
                    seen[key] = val
                    sem = dsem[key[1]] if key[0] == "d" else esem[key[1]]
                    eng.wait_ge(sem, val)
                    stats[ename][1] += 1
                fn = rec["fn"]
                if fn is None:
                    if not rec["sig"]:
                        continue
                    bi = eng.nop()
                else:
                    bi = fn(eng)
                stats[ename][0] += 1
                if rec["dma"] is not None:
                    bi.then_inc(dsem[rec["dma"]], 16)
                elif rec["sig"]:
                    bi.then_inc(esem[ename], 1)

        with nc.Block() as block:
            @block.tensor
            def _(eng):
                replay("pe", eng)

            @block.scalar
            def _(eng):
                replay("act", eng)

            @block.vector
            def _(eng):
                replay("dve", eng)

            @block.gpsimd
            def _(eng):
                replay("pool", eng)

            @block.sync
            def _(eng):
                replay("sp", eng)
        return stats


class Arena:
    def __init__(self, ap, nwords):
        self.ap = ap
        self.n = nwords
        self.off = 0

    def alloc(self, shape, dtype):
        n = 1
        for s in shape:
            n *= s
        words = n if dtype == F32 else (n + 1) // 2
        words = (words + 7) // 8 * 8
        assert self.off + words <= self.n, f"arena overflow {self.off}+{words}>{self.n}"
        sl = self.ap[:, self.off:self.off + words]
        self.off += words
        if dtype != F32:
            sl = sl.bitcast(dtype)
        sl = sl[:, 0:n]
        if len(shape) == 2:
            return sl.rearrange("p (a b) -> p a b", a=shape[0], b=shape[1])
        if len(shape) == 3:
            return sl.rearrange("p (a b c) -> p a b c", a=shape[0], b=shape[1], c=shape[2])
        return sl

    def alloc_top(self, shape, dtype):
        n = 1
        for s_ in shape:
            n *= s_
        words = n if dtype == F32 else (n + 1) // 2
        words = (words + 7) // 8 * 8
        assert self.off + words <= self.n, "arena overflow (top)"
        self.n -= words
        sl = self.ap[:, self.n:self.n + words]
        if dtype != F32:
            sl = sl.bitcast(dtype)
        sl = sl[:, 0:n]
        if len(shape) == 2:
            return sl.rearrange("p (a b) -> p a b", a=shape[0], b=shape[1])
        return sl

    def free_top(self, total):
        self.n = total

    def mark(self):
        return self.off

    def release(self, m):
        self.off = m


class Rot:
    def __init__(self, items):
        self.items = items
        self.i = 0

    def next(self):
        it = self.items[self.i % len(self.items)]
        self.i += 1
        return it


def build_program(S, depth=DEPTH, nseq=NSEQ, stop_after=None):
    NTB = S // 128
    NTT = S // 512
    nc = bass.Bass("TRN2", target_bir_lowering=False)
    fw = FW(nc)

    def din(name, shape):
        return nc.dram_tensor(name, list(shape), F32, kind="ExternalInput").ap()

    x_d = din("x", [nseq, S, D])
    mem_d = din("mem", [nseq, MEM, D])
    out_d = nc.dram_tensor("out", [nseq, S, D], F32, kind="ExternalOutput").ap()
    dbg_d = nc.dram_tensor("dbg", [128, 8, S], BF16, kind="ExternalOutput").ap() if stop_after else None
    norms_d = din("norms", [depth, 7, D])
    w_in_d = din("w_in", [depth, D, D_IN])
    b_row_d = din("b_in", [depth, 1, D_IN])
    wsm_d = din("w_small", [depth, D, 16])
    bsm_d = din("b_small", [depth, 1, 16])
    bcol_d = din("b_col", [depth, 128, 44])
    cw_d = din("conv_col", [depth, 128, 8, 4])
    mlg_d = din("mlstm_norm", [depth, 1, 512])
    gg_d = din("gmlp_norm", [depth, 1, 512])
    wsT_d = din("gmlp_wsT", [depth, 128, 4, 128])
    bs_d = din("gmlp_bs", [depth, 1, 512])
    wbr_d = din("w_branch", [depth, 3, 512, D])
    wout_d = din("w_out", [depth, D, D])
    wxq_d = din("w_xq", [depth, D, D])
    wxkv_d = din("w_xkv", [depth, D, 2 * D])
    wxo_d = din("w_xo", [depth, D, D])
    wff1_d = din("w_ff1", [depth, D, 4 * D])
    wff2_d = din("w_ff2", [depth, 4 * D, D])

    def MM(out, lhsT, rhs, start=True, stop=True, reads=(), writes=(), skip=False):
        fw.op("pe", lambda e: e.matmul(out, lhsT, rhs, start=start, stop=stop, skip_group_check=skip), reads, writes)

    def TR(out, in_, ident, reads=(), writes=()):
        fw.op("pe", lambda e: e.transpose(out, in_, ident), reads, writes)

    def ACT(out, in_, func, reads=(), writes=(), bias=None, scale=None, accum=None):
        kw = {}
        if bias is not None:
            kw["bias"] = bias
        if scale is not None:
            kw["scale"] = scale
        if accum is not None:
            kw["accum_out"] = accum
        fw.op("act", lambda e: e.activation(out=out, in_=in_, func=func, **kw), list(reads) + [tconst], writes)

    def TT(eng, out, in0, in1, op, reads=(), writes=()):
        fw.op(eng, lambda e: e.tensor_tensor(out=out, in0=in0, in1=in1, op=op), reads, writes)

    def TS(eng, out, in0, s1, s2, op0, op1, reads=(), writes=()):
        fw.op(eng, lambda e: e.tensor_scalar(out=out, in0=in0, scalar1=s1, scalar2=s2, op0=op0, op1=op1), reads, writes)

    def TS1(eng, out, in0, s1, op0, reads=(), writes=()):
        fw.op(eng, lambda e: e.tensor_scalar(out=out, in0=in0, scalar1=s1, scalar2=None, op0=op0), reads, writes)

    def STT(eng, out, in0, scalar, in1, op0, op1, reads=(), writes=()):
        fw.op(eng, lambda e: e.scalar_tensor_tensor(out=out, in0=in0, scalar=scalar, in1=in1, op0=op0, op1=op1), reads, writes)

    def CP(eng, out, in_, reads=(), writes=()):
        if eng == "act":
            fw.op("act", lambda e: e.copy(out=out, in_=in_), reads, writes)
        else:
            fw.op(eng, lambda e: e.tensor_copy(out=out, in_=in_), reads, writes)

    def RECIP(out, in_, reads=(), writes=()):
        fw.op("dve", lambda e: e.reciprocal(out=out, in_=in_), reads, writes)

    def MEMSET(eng, ap, val, reads=(), writes=()):
        fw.op(eng, lambda e: e.memset(ap, val), reads, writes)

    def DMA(eng, out, in_, reads=(), writes=()):
        fw.dma(eng, lambda e: e.dma_start(out=out, in_=in_), reads, writes)

    identf = fw.sbuf([128, 128], F32)
    ident = fw.sbuf([128, 128], BF16)
    mask32 = fw.sbuf([128, 128], F32)
    mask16 = fw.sbuf([128, 128], BF16)
    ones32 = fw.sbuf([128, 128], F32)
    sel64 = fw.sbuf([128, 128], F32)
    ones16r = fw.sbuf([1, 128], BF16)
    cst = fw.sbuf([128, 4], F32)
    tconst = Tok()
    MEMSET("pool", identf[:], 0.0, writes=[tconst])
    fw.op("pool", lambda e: e.affine_select(out=identf[:], in_=identf[:], compare_op=ALU.not_equal, fill=1.0,
                                            base=0, pattern=[[-1, 128]], channel_multiplier=1), [tconst], [tconst])
    CP("pool", ident[:], identf[:], [tconst], [tconst])
    MEMSET("pool", mask32[:], 1.0, writes=[tconst])
    fw.op("pool", lambda e: e.affine_select(out=mask32[:], in_=mask32[:], compare_op=ALU.is_ge, fill=0.0,
                                            base=0, pattern=[[1, 128]], channel_multiplier=-1), [tconst], [tconst])
    CP("pool", mask16[:], mask32[:], [tconst], [tconst])
    MEMSET("pool", ones32[:], 1.0, writes=[tconst])
    MEMSET("pool", ones16r[:], 1.0, writes=[tconst])
    CP("pool", sel64[:], identf[:, 0:1].to_broadcast([128, 128]), [tconst], [tconst])
    MEMSET("pool", cst[:, 0:1], EPS, writes=[tconst])
    MEMSET("pool", cst[:, 1:2], -0.5 * math.log(128.0), writes=[tconst])
    MEMSET("pool", cst[:, 2:3], 1.0, writes=[tconst])
    eps_c = cst[:, 0:1]
    lnsc_c = cst[:, 1:2]
    one_c = cst[:, 2:3]

    PS = [fw.psum([128, 1024], F32) for _ in range(4)]
    ptok = toks(8)

    def bank(b):
        return PS[b // 2][:, (b % 2) * 512:(b % 2) * 512 + 512]

    def bank16(b):
        return bank(b).bitcast(BF16)

    AW = (nc.sbuf_bytes_remaining - 2048) // 4
    AW = AW // 8 * 8
    arena_t = fw.sbuf([128, AW], F32)
    AR = Arena(arena_t[:], AW)

    xtok = [[Tok() for _ in range(NTB)] for _ in range(nseq)]

    def xrows(src, sq, tb):
        return src[sq, tb * 128:(tb + 1) * 128, :]

    def wblock(w_ap, c0, ncols):
        return w_ap[:, c0:c0 + ncols].rearrange("(kc p) c -> p kc c", p=128)

    def load_bc(dst, row_ap, tok):
        DMA("sp", dst, row_ap.partition_broadcast(128), writes=[tok])

    def rstd_from_ss(ss, tmp, rstd, inv_n, tk):
        ACT(tmp, ss, AF.Ln, [tk], [tk], bias=eps_c, scale=inv_n)
        ACT(rstd, tmp, AF.Exp, [tk], [tk], scale=-0.5)

    class NormT:
        def __init__(self, pb=(6, 7), nx=3, nh=2):
            self.xt = Rot([(AR.alloc([1024], F32), Tok()) for _ in range(nx)])
            self.hn = Rot([(AR.alloc([1024], BF16), Tok()) for _ in range(nh)])
            self.junk = AR.alloc([1024], BF16)
            self.sm = Rot([(AR.alloc([4], F32), Tok()) for _ in range(max(nx, 4))])
            self.pbr = Rot(list(pb))
            self.tj = Tok()

        def part1(self, src_ap, src_tok, g_bc, g_tok, xbuf=None):
            x_ap, x_tk = xbuf if xbuf is not None else self.xt.next()
            DMA("sp", x_ap, src_ap, reads=[src_tok], writes=[x_tk])
            s_ap, s_tk = self.sm.next()
            ACT(self.junk, x_ap, AF.Square, [x_tk], [self.tj, s_tk], accum=s_ap[:, 0:1])
            rstd_from_ss(s_ap[:, 0:1], s_ap[:, 1:2], s_ap[:, 2:3], 1.0 / D, s_tk)
            h_ap, h_tk = self.hn.next()
            STT("dve", h_ap, x_ap, s_ap[:, 2:3], g_bc, ALU.mult, ALU.mult, [x_tk, s_tk, g_tok], [h_tk])
            return h_ap, h_tk

        def part2(self, h_ap, h_tk, dstT, dst_tok, col, alt=0):
            pbk = self.pbr.next()
            pv = bank16(pbk)
            for k in range(8):
                TR(pv[:, k * 128:(k + 1) * 128], h_ap[:, k * 128:(k + 1) * 128], ident[:], [h_tk, tconst], [ptok[pbk]])
            CP("dve", dstT[:, :, col:col + 128],
               pv.rearrange("p (k t) -> p k t", k=8), [ptok[pbk]], [dst_tok])

        def run(self, src_fn, src_toks, nblk, g_bc, g_tok, dstT, dst_tok_fn, col0=0):
            for b in range(nblk):
                h_ap, h_tk = self.part1(src_fn(b), src_toks[b], g_bc, g_tok)
                self.part2(h_ap, h_tk, dstT, dst_tok_fn(b), col0 + b * 128, alt=b)

    class PostRes:
        def __init__(self, g_bc, g_tok, nxt=4, nt1=2):
            self.g_bc, self.g_tok = g_bc, g_tok
            self.xt = Rot([(AR.alloc([1024], F32), Tok()) for _ in range(nxt)])
            self.t1 = Rot([(AR.alloc([1024], F32), Tok()) for _ in range(nt1)])
            self.junk = AR.alloc([1024], BF16)
            self.tj = Tok()
            self.sm = Rot([(AR.alloc([4], F32), Tok()) for _ in range(3)])
            self.q = {}
            self.todo = []

        def plan(self, src, sq, tbs):
            self.src, self.sq = src, sq
            self.todo = list(tbs)
            self._fill()

        def _fill(self):
            while len(self.q) < 2 and self.todo:
                tb = self.todo.pop(0)
                x_ap, x_tk = self.xt.next()
                DMA("sp", x_ap, xrows(self.src, self.sq, tb), reads=[xtok[self.sq][tb]], writes=[x_tk])
                self.q[tb] = (x_ap, x_tk)

        def run(self, pp, sq, tb, xbuf=None):
            if xbuf is not None:
                x_ap, x_tk = xbuf
            else:
                x_ap, x_tk = self.q.pop(tb)
                self._fill()
            y = PS[pp][:, :]
            ytk = [ptok[2 * pp], ptok[2 * pp + 1]]
            s_ap, s_tk = self.sm.next()
            ACT(self.junk, y, AF.Square, ytk, [self.tj, s_tk], accum=s_ap[:, 0:1])
            rstd_from_ss(s_ap[:, 0:1], s_ap[:, 1:2], s_ap[:, 2:3], 1.0 / D, s_tk)
            t_ap, t_tk = self.t1.next()
            STT("dve", t_ap, y, s_ap[:, 2:3], self.g_bc, ALU.mult, ALU.mult, ytk + [s_tk, self.g_tok], [t_tk])
            TT("dve", x_ap, x_ap, t_ap, ALU.add, [x_tk, t_tk], [x_tk])
            DMA("sp", xrows(out_d, sq, tb), x_ap, reads=[x_tk], writes=[xtok[sq][tb]])

    def out_proj(actT, act_tok_fn, nk, W, w_toks, g_bc, g_tok, src, sq):
        pr = PostRes(g_bc, g_tok)
        pr.plan(src, sq, range(NTB))
        for tb in range(NTB):
            pp = tb % 2
            for half in range(2):
                bk = 2 * pp + half
                for k in range(nk):
                    MM(bank(bk), actT[:, k, tb * 128:(tb + 1) * 128], W[:, k, half * 512:(half + 1) * 512],
                       start=(k == 0), stop=(k == nk - 1), reads=[act_tok_fn(tb), w_toks[k]], writes=[ptok[bk]])
            pr.run(pp, sq, tb)

    for sq in range(nseq):
        for l in range(depth):
            xsrc = x_d if l == 0 else out_d
            AR.release(0)
            hT = AR.alloc([8, S], BF16)
            hT_tok = toks(NTT)
            yfT = AR.alloc([4, S], BF16)
            ymT = AR.alloc([4, S], BF16)
            ygT = AR.alloc([4, S], BF16)
            yfT_tok, ymT_tok, ygT_tok = toks(NTT), toks(NTT), toks(NTT)
            apr = AR.alloc([NTB, 4], F32)
            ebt = AR.alloc([NTB, 4], F32)
            wsc = AR.alloc([NTB, 4], F32)
            emb = AR.alloc([NTB, 4], F32)
            ml_tok = Tok()
            bcol = AR.alloc([44], F32)
            bcol_tok = Tok()
            DMA("sp", bcol, bcol_d[l], writes=[bcol_tok])
            m_mix = AR.mark()
            fbT = AR.alloc([8, NTB, NTB // 2], F32)
            fb_tok = Tok()

            g_bc = AR.alloc([1024], F32)
            g_tok = Tok()
            load_bc(g_bc, norms_d[l, 0:1, :], g_tok)
            NormT(nx=4).run(lambda b: xrows(xsrc, sq, b), xtok[sq], NTB, g_bc, g_tok, hT, lambda b: hT_tok[b // 4])

            def emit_PA():
                wsm = AR.alloc([8, 16], BF16)
                t_wsm = Tok()
                DMA("pool", wsm, wsm_d[l].rearrange("(kc p) c -> p kc c", p=128), writes=[t_wsm])
                bsm = AR.alloc([16], F32)
                t_bsm = Tok()
                load_bc(bsm, bsm_d[l], t_bsm)
                gt = AR.alloc([NTB, 16], F32)
                nl = AR.alloc([NTB, 16], F32)
                gw = AR.alloc([NTB, 16], F32)
                tot = AR.alloc([NTB, 16], F32)
                off = AR.alloc([NTB, 16], F32)
                Gc = AR.alloc([NTB, 16], F32)
                Gref = AR.alloc([NTB, 16], F32)
                tg = Tok()
                pg = bank(0)[:, 0:NTB * 16].rearrange("p (a b) -> p a b", a=NTB, b=16)
                for tb in range(NTB):
                    for k in range(8):
                        MM(bank(0)[:, tb * 16:(tb + 1) * 16], hT[:, k, tb * 128:(tb + 1) * 128], wsm[:, k, :],
                           start=(k == 0), stop=(k == 7), reads=[hT_tok[tb // 4], t_wsm], writes=[ptok[0]])
                TT("dve", gt, pg, bsm.unsqueeze(1).to_broadcast([128, NTB, 16]), ALU.add, [ptok[0], t_bsm], [tg])
                ACT(nl, gt, AF.Exp, [tg], [tg], scale=-1.0)
                ACT(nl, nl, AF.Ln, [tg], [tg], bias=one_c, scale=1.0)
                nlf = nl.rearrange("p a b -> p (a b)")
                MM(bank(1)[:, 0:NTB * 16], mask32[:], nlf, reads=[tg, tconst], writes=[ptok[1]])
                MM(bank(2)[:, 0:NTB * 16], ones32[:], nlf, reads=[tg, tconst], writes=[ptok[2]])
                CP("dve", gw.rearrange("p a b -> p (a b)"), bank(1)[:, 0:NTB * 16], [ptok[1]], [tg])
                CP("dve", tot.rearrange("p a b -> p (a b)"), bank(2)[:, 0:NTB * 16], [ptok[2]], [tg])
                MEMSET("dve", off[:, 0, :], 0.0, writes=[tg])
                for tb in range(1, NTB):
                    TT("dve", off[:, tb, :], off[:, tb - 1, :], tot[:, tb - 1, :], ALU.add, [tg], [tg])
                TT("dve", Gc, gw, off, ALU.add, [tg], [tg])
                MM(bank(3)[:, 0:NTB * 16], sel64[:], Gc.rearrange("p a b -> p (a b)"), reads=[tg, tconst], writes=[ptok[3]])
                CP("dve", Gref.rearrange("p a b -> p (a b)"), bank(3)[:, 0:NTB * 16], [ptok[3]], [tg])
                Gref_odd = Gref.rearrange("p (i two) c -> p i two c", two=2)[:, :, 1, :]
                for h in range(8):
                    TT("dve", fbT[:, h, :, :], Gc[:, :, h].unsqueeze(2).to_broadcast([128, NTB, NTB // 2]),
                       Gref_odd[:, :, h].unsqueeze(1).to_broadcast([128, NTB, NTB // 2]), ALU.subtract, [tg], [fb_tok])
                TT("dve", apr, gt[:, :, 8:12], gw[:, :, 12:16], ALU.add, [tg], [ml_tok])
                ACT(apr, apr, AF.Exp, [ml_tok], [ml_tok], bias=lnsc_c, scale=1.0)
                ACT(ebt, tot[:, :, 12:16], AF.Exp, [tg], [ml_tok], scale=-1.0)
                TT("dve", wsc, apr, ebt, ALU.mult, [ml_tok], [ml_tok])
                ACT(emb, gw[:, :, 12:16], AF.Exp, [tg], [ml_tok])


            Wv = AR.alloc([8, 512], BF16)
            t_wv = Tok()
            DMA("pool", Wv, wblock(w_in_d[l], FOX_V, 512), writes=[t_wv])
            bv = AR.alloc([512], F32)
            t_bv = Tok()
            load_bc(bv, b_row_d[l, :, FOX_V:FOX_V + 512], t_bv)
            Vaug = AR.alloc([NTB, 8 * 65], BF16)
            Vaug4 = Vaug.rearrange("p a (h c) -> p a h c", h=8, c=65)
            V_tok = toks(NTB)
            yf_tm = AR.alloc([NTB, 512], BF16)
            yf_tok = toks(NTT)
            for tb in range(NTB):
                MEMSET("pool", Vaug4[:, tb, :, 64:65], 1.0, writes=[V_tok[tb]])
                bk = 6 + (tb % 2)
                for k in range(8):
                    MM(bank(bk), hT[:, k, tb * 128:(tb + 1) * 128], Wv[:, k, :], start=(k == 0), stop=(k == 7),
                       reads=[hT_tok[tb // 4], t_wv], writes=[ptok[bk]])
                TT("dve", Vaug4[:, tb, :, 0:64], bank(bk).rearrange("p (h c) -> p h c", h=8, c=64),
                   bv.rearrange("p (h c) -> p h c", h=8, c=64), ALU.add, [ptok[bk], t_bv], [V_tok[tb]])
            Wqk = Rot([(AR.alloc([8, 128], BF16), Tok()) for _ in range(4)])
            qkT = Rot([(AR.alloc([S], BF16), toks(NTT)) for _ in range(4)])
            PTs = Rot([(AR.alloc([512], BF16), toks(4)) for _ in range(6)])
            rinv = Rot([(AR.alloc([4], F32), Tok()) for _ in range(2)])
            sbk = Rot([0, 1, 2, 3])
            obk = Rot([4, 5, 6, 7])
            pjb = sbk
            def fox_proj_setup(hp):
                items = []
                for which in range(2):
                    w_ap, w_tk = Wqk.next()
                    c0 = (FOX_Q if which == 0 else FOX_K) + hp * 128
                    DMA("pool", w_ap, wblock(w_in_d[l], c0, 128), writes=[w_tk])
                    t_ap, t_tks = qkT.next()
                    bcolk = hp if which == 0 else 4 + hp
                    items.append((w_ap, w_tk, t_ap, t_tks, bcolk))
                return items

            def fox_proj_part(items, tt):
                for (w_ap, w_tk, t_ap, t_tks, bcolk) in items:
                    bk = pjb.next()
                    for k in range(8):
                        MM(bank(bk), w_ap[:, k, :], hT[:, k, tt * 512:(tt + 1) * 512], start=(k == 0), stop=(k == 7),
                           reads=[w_tk, hT_tok[tt]], writes=[ptok[bk]])
                    TS1("dve", t_ap[:, tt * 512:(tt + 1) * 512], bank(bk), bcol[:, bcolk:bcolk + 1], ALU.add, [ptok[bk], bcol_tok], [t_tks[tt]])

            cur_items = fox_proj_setup(0)
            for tt in range(NTT):
                fox_proj_part(cur_items, tt)
            emit_PA()
            for hp in range(4):
                (_, _, qT, q_tks, _), (_, _, kT, k_tks, _) = cur_items
                nxt_items = fox_proj_setup(hp + 1) if hp + 1 < 4 else None
                for g in range(NTT):
                    i0 = 4 * g
                    nsteps = i0 + 4
                    obs = [obk.next(), obk.next()]
                    firsts = [True, True]
                    pend = {}

                    def emit_S(hh, j):
                        r0 = 64 * hh
                        ilo = max(j, i0)
                        ncol = (i0 + 4 - ilo) * 128
                        sb = sbk.next()
                        MM(bank(sb)[:, 0:ncol], kT[r0:r0 + 64, j * 128:(j + 1) * 128], qT[r0:r0 + 64, ilo * 128:(i0 + 4) * 128],
                           reads=[k_tks[j // 4], q_tks[g]], writes=[ptok[sb]])
                        return sb, ilo

                    def emit_EP(hh, j, info):
                        sb, ilo = info
                        h = 2 * hp + hh
                        ob = obs[hh]
                        obv = bank(ob)[:, 0:260].rearrange("p (a c) -> p a c", a=4, c=65)
                        p_ap, p_tks = PTs.next()
                        for I in (2 * g, 2 * g + 1):
                            i_lo = max(j, 2 * I)
                            if i_lo > 2 * I + 1:
                                continue
                            c0 = (i_lo - ilo) * 128
                            wd = (2 * I + 2 - i_lo) * 128
                            pt = p_tks[I - 2 * g]
                            ACT(p_ap[:, c0:c0 + wd], bank(sb)[:, c0:c0 + wd], AF.Exp, [ptok[sb], fb_tok], [pt],
                                bias=fbT[:, h, j, I:I + 1], scale=0.125)
                            if i_lo == j:
                                TT("dve", p_ap[:, c0:c0 + 128], p_ap[:, c0:c0 + 128], mask16[:], ALU.mult, [pt, tconst], [pt])
                        for i in range(ilo, i0 + 4):
                            c0 = (i - ilo) * 128
                            MM(obv[:, i - i0, :], p_ap[:, c0:c0 + 128], Vaug4[:, j, h, :], start=firsts[hh], stop=(j == i),
                               reads=[p_tks[i // 2 - 2 * g], V_tok[j]], writes=[ptok[ob]], skip=True)
                            firsts[hh] = False

                    LOOK = 1
                    for j in range(nsteps + LOOK):
                        if j < nsteps:
                            for hh in range(2):
                                pend[(hh, j)] = emit_S(hh, j)
                        if j - LOOK >= 0:
                            for hh in range(2):
                                emit_EP(hh, j - LOOK, pend.pop((hh, j - LOOK)))
                    for hh in range(2):
                        h = 2 * hp + hh
                        ob = obs[hh]
                        obv = bank(ob)[:, 0:260].rearrange("p (a c) -> p a c", a=4, c=65)
                        r_ap, r_tk = rinv.next()
                        RECIP(r_ap, obv[:, :, 64], [ptok[ob]], [r_tk])
                        TT("dve", yf_tm[:, i0:i0 + 4, h * 64:(h + 1) * 64], obv[:, :, 0:64],
                           r_ap.unsqueeze(2).to_broadcast([128, 4, 64]), ALU.mult, [ptok[ob], r_tk], [yf_tok[g]])
                    if nxt_items is not None:
                        fox_proj_part(nxt_items, g)
                if nxt_items is not None:
                    cur_items = nxt_items
            for tb in range(NTB):
                bk = pjb.next()
                pv = bank16(bk)
                for cc in range(4):
                    TR(pv[:, cc * 128:(cc + 1) * 128], yf_tm[:, tb, cc * 128:(cc + 1) * 128], ident[:], [yf_tok[tb // 4], tconst], [ptok[bk]])
                CP("act" if tb % 2 else "dve", yfT[:, :, tb * 128:(tb + 1) * 128], pv[:, 0:512].rearrange("p (k t) -> p k t", k=4),
                   [ptok[bk]], [yfT_tok[tb // 4]])
            fw.barrier()
            AR.release(m_mix)
            if stop_after == "fox":
                DMA("sp", dbg_d[:, 0:4, :], yfT, reads=yfT_tok)
                break

            pjb = Rot([6, 7])
            Wvm = AR.alloc([8, 512], BF16)
            Wo = AR.alloc([8, 512], BF16)
            t_wvm, t_wo = Tok(), Tok()
            DMA("pool", Wvm, wblock(w_in_d[l], ML_V, 512), writes=[t_wvm])
            DMA("pool", Wo, wblock(w_in_d[l], ML_O, 512), writes=[t_wo])
            bvm = AR.alloc([512], F32)
            bo = AR.alloc([512], F32)
            gml = AR.alloc([512], F32)
            cw = AR.alloc([8, 4], F32)
            t_small = Tok()
            load_bc(bvm, b_row_d[l, :, ML_V:ML_V + 512], t_small)
            load_bc(bo, b_row_d[l, :, ML_O:ML_O + 512], t_small)
            load_bc(gml, mlg_d[l], t_small)
            DMA("sp", cw, cw_d[l], writes=[t_small])
            qkm = [[(AR.alloc([S], BF16), Tok()) for _ in range(4)] for _ in range(2)]
            zps = Rot([(AR.alloc([S + 8], F32), Tok()) for _ in range(2)])
            accs = Rot([(AR.alloc([S], F32), Tok()) for _ in range(2)])
            Wqk = Rot([(AR.alloc([8, 128], BF16), Tok()) for _ in range(2)])
            for zp_, tz_ in zps.items:
                MEMSET("dve", zp_[:, 0:3], 0.0, writes=[tz_])
            def ml_proj(which, h):
                w_ap, w_tk = Wqk.next()
                c0 = (ML_Q if which == 0 else ML_K) + h * 128
                DMA("pool", w_ap, wblock(w_in_d[l], c0, 128), writes=[w_tk])
                blk = which * 4 + h
                zp, t_zp = zps.next()
                for tt in range(NTT):
                    bk = pjb.next()
                    for k in range(8):
                        MM(bank(bk), w_ap[:, k, :], hT[:, k, tt * 512:(tt + 1) * 512], start=(k == 0), stop=(k == 7),
                           reads=[w_tk, hT_tok[tt]], writes=[ptok[bk]])
                    ACT(zp[:, 3 + tt * 512:3 + (tt + 1) * 512], bank(bk), AF.Identity, [ptok[bk], bcol_tok], [t_zp],
                        bias=bcol[:, 8 + blk:9 + blk], scale=1.0)
                return zp, t_zp, blk

            def ml_conv(which, h, zp, t_zp, blk):
                acc, t_acc = accs.next()
                TS1("dve", acc, zp[:, 0:S], cw[:, blk, 0:1], ALU.mult, [t_zp, t_small], [t_acc])
                for j in range(1, 4):
                    STT("dve", acc, zp[:, j:j + S], cw[:, blk, j:j + 1], acc, ALU.mult, ALU.add, [t_zp, t_small, t_acc], [t_acc])
                d_ap, d_tk = qkm[which][h]
                ACT(d_ap, acc, AF.Silu, [t_acc], [d_tk])

            order = [(w_, h_) for w_ in range(2) for h_ in range(4)]
            nxt_p = ml_proj(*order[0])
            for n_, (w_, h_) in enumerate(order):
                cur_p = nxt_p
                if n_ + 1 < len(order):
                    nxt_p = ml_proj(*order[n_ + 1])
                ml_conv(w_, h_, *cur_p)
            Vm = Rot([(AR.alloc([4, 129], BF16), Tok()) for _ in range(3)])
            for v_ap, v_tk in Vm.items:
                MEMSET("pool", v_ap[:, :, 128:129], 1.0, writes=[v_tk])
            gsig = Rot([(AR.alloc([512], F32), Tok()) for _ in range(6)])
            gs_map = {}
            otmp = Rot([(AR.alloc([512], F32), Tok()) for _ in range(1)])
            ndS = Rot([(AR.alloc([4, 129], F32), Tok()) for _ in range(2)])
            PTm = Rot([(AR.alloc([128], BF16), Tok()) for _ in range(8)])
            K2 = Rot([(AR.alloc([128], BF16), Tok()) for _ in range(8)])
            C32 = [(AR.alloc([129], F32), Tok()) for _ in range(4)]
            Cb = [(AR.alloc([129], BF16), Tok()) for _ in range(4)]
            smr = Rot([(AR.alloc([8, 4], F32), Tok()) for _ in range(4)])
            junkm = AR.alloc([128], BF16)
            tjm = Tok()
            ymc = Rot([(AR.alloc([512], BF16), Tok()) for _ in range(2)])
            sbk = Rot([0, 1])
            tbk = Rot([2])
            ndk = Rot([3, 4])
            dbk = Rot([5])

            def ml_stage1(c):
                cs = slice(c * 128, (c + 1) * 128)
                v_ap, v_tk = Vm.next()
                bk = pjb.next()
                for k in range(8):
                    MM(bank(bk), hT[:, k, cs], Wvm[:, k, :], start=(k == 0), stop=(k == 7), reads=[hT_tok[c // 4], t_wvm], writes=[ptok[bk]])
                TT("dve", v_ap[:, :, 0:128], bank(bk).rearrange("p (h c) -> p h c", h=4, c=128),
                   bvm.rearrange("p (h c) -> p h c", h=4, c=128), ALU.add, [ptok[bk], t_small], [v_tk])
                if c % 4 == 0:
                    for c2 in range(c, min(c + 4, NTB)):
                        cs2 = slice(c2 * 128, (c2 + 1) * 128)
                        bk = pjb.next()
                        for k in range(8):
                            MM(bank(bk), hT[:, k, cs2], Wo[:, k, :], start=(k == 0), stop=(k == 7), reads=[hT_tok[c2 // 4], t_wo], writes=[ptok[bk]])
                        o_ap, o_tk = otmp.next()
                        TT("dve", o_ap, bank(bk), bo, ALU.add, [ptok[bk], t_small], [o_tk])
                        gg_ap, gg_tk = gsig.next()
                        ACT(gg_ap, o_ap, AF.Sigmoid, [o_tk], [gg_tk])
                        TT("dve", gg_ap, gg_ap, gml, ALU.mult, [gg_tk, t_small], [gg_tk])
                        gs_map[c2] = (gg_ap, gg_tk)
                g_ap, g_tk = gs_map.pop(c)
                sb = sbk.next()
                for h in range(4):
                    MM(bank(sb)[:, h * 128:(h + 1) * 128], qkm[1][h][0][:, cs], qkm[0][h][0][:, cs],
                       reads=[qkm[1][h][1], qkm[0][h][1]], writes=[ptok[sb]])
                tb_ = tbk.next()
                for h in range(4):
                    TR(bank16(tb_)[:, h * 128:(h + 1) * 128], qkm[1][h][0][:, cs], ident[:], [qkm[1][h][1], tconst], [ptok[tb_]])
                pts, k2s = [], []
                for h in range(4):
                    p_ap, p_tk = PTm.next()
                    STT("dve", p_ap, bank(sb)[:, h * 128:(h + 1) * 128], apr[:, c, h:h + 1], mask32[:], ALU.mult, ALU.mult,
                        [ptok[sb], ml_tok, tconst], [p_tk])
                    pts.append((p_ap, p_tk))
                for h in range(4):
                    k2_ap, k2_tk = K2.next()
                    ACT(k2_ap, bank16(tb_)[:, h * 128:(h + 1) * 128], AF.Identity, [ptok[tb_], ml_tok], [k2_tk], scale=wsc[:, c, h:h + 1])
                    k2s.append((k2_ap, k2_tk))
                return dict(v=(v_ap, v_tk), g=(g_ap, g_tk), pts=pts, k2s=k2s)

            def ml_stage2(c, st, nxt_holder):
                cs = slice(c * 128, (c + 1) * 128)
                v_ap, v_tk = st["v"]
                g_ap, g_tk = st["g"]
                y_ap, y_tk = ymc.next()
                nds = [ndk.next(), ndk.next()]
                ndv = []
                for h in range(4):
                    nd = nds[h // 2]
                    v_ = bank(nd)[:, (h % 2) * 129:(h % 2) * 129 + 129]
                    ndv.append((nd, v_))
                    MM(v_, st["pts"][h][0], v_ap[:, h, :], start=True, stop=(c == 0), reads=[st["pts"][h][1], v_tk], writes=[ptok[nd]])
                    if c > 0:
                        MM(v_, qkm[0][h][0][:, cs], Cb[h][0], start=False, stop=True, reads=[qkm[0][h][1], Cb[h][1]], writes=[ptok[nd]])
                nS_ap, nS_tk = ndS.next()
                for hp_ in range(2):
                    CP("act", nS_ap[:, 2 * hp_:2 * hp_ + 2, :], bank(nds[hp_])[:, 0:258].rearrange("p (h c) -> p h c", h=2, c=129),
                       [ptok[nds[hp_]]], [nS_tk])
                if c + 1 < NTB:
                    for hp_ in range(2):
                        db = dbk.next()
                        for hh in range(2):
                            h = 2 * hp_ + hh
                            MM(bank(db)[:, hh * 129:hh * 129 + 129], st["k2s"][h][0], v_ap[:, h, :], reads=[st["k2s"][h][1], v_tk], writes=[ptok[db]])
                        for hh in range(2):
                            h = 2 * hp_ + hh
                            c_ap, c_tk = C32[h]
                            dv = bank(db)[:, hh * 129:hh * 129 + 129]
                            if c == 0:
                                CP("dve", c_ap, dv, [ptok[db]], [c_tk])
                            else:
                                STT("dve", c_ap, c_ap, ebt[:, c, h:h + 1], dv, ALU.mult, ALU.add, [c_tk, ml_tok, ptok[db]], [c_tk])
                        for hh in range(2):
                            h = 2 * hp_ + hh
                            CP("pool", Cb[h][0], C32[h][0], [C32[h][1]], [Cb[h][1]])
                while ml_pending:
                    ml_pending.pop(0)()
                s_ap, s_tk = smr.next()
                STT("dve", s_ap[:, 1, :], nS_ap[:, :, 128], -1.0, nS_ap[:, :, 128], ALU.mult, ALU.max, [nS_tk], [s_tk])
                TT("dve", s_ap[:, 2, :], s_ap[:, 1, :], emb[:, c, :], ALU.max, [s_tk, ml_tok], [s_tk])
                RECIP(s_ap[:, 3, :], s_ap[:, 2, :], [s_tk], [s_tk])
                for h in range(4):
                    ACT(junkm, nS_ap[:, h, 0:128], AF.Square, [nS_tk, s_tk], [tjm, s_tk], scale=s_ap[:, 3, h:h + 1], accum=s_ap[:, 4, h:h + 1])
                if c + 1 < NTB:
                    nxt_holder.append(ml_stage1(c + 1))
                ACT(s_ap[:, 5, :], s_ap[:, 4, :], AF.Ln, [s_tk], [s_tk], bias=eps_c, scale=1.0 / 128)
                ACT(s_ap[:, 6, :], s_ap[:, 5, :], AF.Exp, [s_tk], [s_tk], scale=-0.5)
                TT("dve", s_ap[:, 7, :], s_ap[:, 3, :], s_ap[:, 6, :], ALU.mult, [s_tk], [s_tk])
                for h in range(4):
                    STT("dve", y_ap[:, h * 128:(h + 1) * 128], nS_ap[:, h, 0:128], s_ap[:, 7, h:h + 1], g_ap[:, h * 128:(h + 1) * 128],
                        ALU.mult, ALU.mult, [nS_tk, s_tk, g_tk], [y_tk])
                def finish():
                    bk = pjb.next()
                    pv = bank16(bk)
                    for cc in range(4):
                        TR(pv[:, cc * 128:(cc + 1) * 128], y_ap[:, cc * 128:(cc + 1) * 128], ident[:], [y_tk, tconst], [ptok[bk]])
                    CP("act", ymT[:, :, cs], pv[:, 0:512].rearrange("p (k t) -> p k t", k=4), [ptok[bk]], [ymT_tok[c // 4]])
                return finish

            ml_pending = []
            st_cur = ml_stage1(0)
            for c in range(NTB):
                holder = []
                ml_pending.append(ml_stage2(c, st_cur, holder))
                if holder:
                    st_cur = holder[0]
            for f in ml_pending:
                f()
            fw.barrier()
            AR.release(m_mix)
            if stop_after == "mlstm":
                DMA("sp", dbg_d[:, 0:4, :], ymT, reads=ymT_tok)
                break

            mgT = AR.alloc([8, S], BF16)
            mg_tok = toks(NTT)
            m_mg = AR.mark()
            Wu = AR.alloc([8, 512], BF16)
            Wgv = AR.alloc([8, 512], BF16)
            t_wu, t_wgv = Tok(), Tok()
            DMA("pool", Wu, wblock(w_in_d[l], G_U, 512), writes=[t_wu])
            DMA("pool", Wgv, wblock(w_in_d[l], G_V, 512), writes=[t_wgv])
            bgv = AR.alloc([512], F32)
            ggb = AR.alloc([512], F32)
            ws32 = AR.alloc([4, 128], F32)
            ws16 = AR.alloc([4, 128], BF16)
            bs16 = AR.alloc([512], BF16)
            t_g = Tok()
            load_bc(bgv, b_row_d[l, :, G_V:G_V + 512], t_g)
            load_bc(ggb, gg_d[l], t_g)
            DMA("sp", ws32, wsT_d[l], writes=[t_g])
            DMA("pool", bs16[0:1, :], bs_d[l], writes=[t_g])
            MEMSET("dve", ws32[64:128, :, 0:64], 0.0, [t_g], [t_g])
            CP("dve", ws16, ws32, [t_g], [t_g])
            uT = Rot([(AR.alloc([4, 512], BF16), Tok()) for _ in range(2)])
            vtmp = Rot([(AR.alloc([512], F32), Tok()) for _ in range(8)])
            vn = Rot([(AR.alloc([512], BF16), Tok()) for _ in range(4)])
            smr = Rot([(AR.alloc([8, 4], F32), Tok()) for _ in range(3)])
            junkg = AR.alloc([512], BF16)
            tjg = Tok()
            mxb = Rot([0, 1, 2, 3])
            gpb = Rot([4, 5, 6, 7])
            def gm_front(tt):
                u_ap, u_tk = uT.next()
                for g in range(4):
                    bk = gpb.next()
                    for k in range(8):
                        MM(bank(bk), Wu[:, k, g * 128:(g + 1) * 128], hT[:, k, tt * 512:(tt + 1) * 512], start=(k == 0), stop=(k == 7),
                           reads=[t_wu, hT_tok[tt]], writes=[ptok[bk]])
                    ACT(u_ap[:, g, :], bank(bk), AF.Gelu, [ptok[bk], bcol_tok], [u_tk], bias=bcol[:, 16 + g:17 + g], scale=1.0)
                s_ap, s_tk = smr.next()
                vts = []
                for sp_ in range(4):
                    tb = tt * 4 + sp_
                    bk = gpb.next()
                    for k in range(8):
                        MM(bank(bk), hT[:, k, tb * 128:(tb + 1) * 128], Wgv[:, k, :], start=(k == 0), stop=(k == 7),
                           reads=[hT_tok[tt], t_wgv], writes=[ptok[bk]])
                    v_ap, v_tk = vtmp.next()
                    TT("dve", v_ap, bank(bk), bgv, ALU.add, [ptok[bk], t_g], [v_tk])
                    vts.append((v_ap, v_tk))
                for sp_ in range(4):
                    v_ap, v_tk = vts[sp_]
                    ACT(v_ap, v_ap, AF.Gelu, [v_tk], [v_tk, s_tk], accum=s_ap[:, 0, sp_:sp_ + 1])
                for sp_ in range(4):
                    v_ap, v_tk = vts[sp_]
                    ACT(junkg, v_ap, AF.Square, [v_tk], [tjg, s_tk], accum=s_ap[:, 1, sp_:sp_ + 1])
                return u_ap, u_tk, s_ap, s_tk, vts

            def gm_back(tt, u_ap, u_tk, s_ap, s_tk, vts):
                TS1("dve", s_ap[:, 2, :], s_ap[:, 0, :], 1.0 / 512, ALU.mult, [s_tk], [s_tk])
                TT("dve", s_ap[:, 3, :], s_ap[:, 2, :], s_ap[:, 2, :], ALU.mult, [s_tk], [s_tk])
                STT("dve", s_ap[:, 4, :], s_ap[:, 1, :], 1.0 / 512, s_ap[:, 3, :], ALU.mult, ALU.subtract, [s_tk], [s_tk])
                ACT(s_ap[:, 5, :], s_ap[:, 4, :], AF.Ln, [s_tk], [s_tk], bias=eps_c, scale=1.0)
                ACT(s_ap[:, 6, :], s_ap[:, 5, :], AF.Exp, [s_tk], [s_tk], scale=-0.5)
                vns = []
                for sp_ in range(4):
                    v_ap, v_tk = vts[sp_]
                    TS("dve", v_ap, v_ap, s_ap[:, 2, sp_:sp_ + 1], s_ap[:, 6, sp_:sp_ + 1], ALU.subtract, ALU.mult, [v_tk, s_tk], [v_tk])
                    n_ap, n_tk = vn.next()
                    TT("dve", n_ap, v_ap, ggb, ALU.mult, [v_tk, t_g], [n_tk])
                    vns.append((n_ap, n_tk))
                mbs = []
                for sp_ in range(4):
                    n_ap, n_tk = vns[sp_]
                    mb = mxb.next()
                    for g in range(4):
                        MM(bank(mb)[:, g * 128:(g + 1) * 128], n_ap[:, g * 128:(g + 1) * 128], ws16[:, g, :], start=(g == 0), stop=False,
                           reads=[n_tk, t_g], writes=[ptok[mb]], skip=True)
                        MM(bank(mb)[:, g * 128:(g + 1) * 128], ones16r[:], bs16[0:1, g * 128:(g + 1) * 128], start=False, stop=True,
                           reads=[t_g, tconst], writes=[ptok[mb]], skip=True)
                    mbs.append(mb)
                for sp_ in range(4):
                    tb = tt * 4 + sp_
                    mb = mbs[sp_]
                    TT("dve", ygT[:, :, tb * 128:(tb + 1) * 128], bank(mb).rearrange("p (g t) -> p g t", g=4),
                       u_ap[:, :, sp_ * 128:(sp_ + 1) * 128], ALU.mult, [ptok[mb], u_tk], [ygT_tok[tt]])

            gm_next = gm_front(0)
            for tt in range(NTT):
                gm_cur = gm_next
                if tt + 1 < NTT:
                    gm_next = gm_front(tt + 1)
                gm_back(tt, *gm_cur)
            if stop_after == "gmlp":
                DMA("sp", dbg_d[:, 0:4, :], ygT, reads=ygT_tok)
                break

            Wg = Rot([(AR.alloc([8, 128], BF16), Tok()) for _ in range(3)])
            Wb = Rot([(AR.alloc([4, 128], BF16), Tok()) for _ in range(3)])
            sgt = Rot([(AR.alloc([512], F32), Tok()) for _ in range(3)])
            macc = AR.alloc([NTT, 512], F32)
            macc_tok = toks(NTT)
            ysrc = [(yfT, yfT_tok), (ymT, ymT_tok), (ygT, ygT_tok)]
            gbk = Rot([0, 1, 2])
            bbk = Rot([3, 4, 5])
            for dc in range(8):
                for n in range(3):
                    wg_ap, wg_tk = Wg.next()
                    DMA("pool", wg_ap, wblock(w_in_d[l], GATE + n * 1024 + dc * 128, 128), writes=[wg_tk])
                    wb_ap, wb_tk = Wb.next()
                    DMA("pool", wb_ap, wbr_d[l, n, :, dc * 128:(dc + 1) * 128].rearrange("(kc p) c -> p kc c", p=128), writes=[wb_tk])
                    yT, y_tks = ysrc[n]
                    for tt in range(NTT):
                        ts_ = slice(tt * 512, (tt + 1) * 512)
                        gb = gbk.next()
                        for k in range(8):
                            MM(bank(gb), wg_ap[:, k, :], hT[:, k, ts_], start=(k == 0), stop=(k == 7), reads=[wg_tk, hT_tok[tt]], writes=[ptok[gb]])
                        s_ap, s_tk = sgt.next()
                        ACT(s_ap, bank(gb), AF.Sigmoid, [ptok[gb], bcol_tok], [s_tk], bias=bcol[:, 20 + n * 8 + dc:21 + n * 8 + dc], scale=1.0)
                        bb = bbk.next()
                        for k in range(4):
                            MM(bank(bb), wb_ap[:, k, :], yT[:, k, ts_], start=(k == 0), stop=(k == 3), reads=[wb_tk, y_tks[tt]], writes=[ptok[bb]])
                        if n == 0:
                            TT("dve", macc[:, tt, :], s_ap, bank(bb), ALU.mult, [s_tk, ptok[bb]], [macc_tok[tt]])
                        else:
                            TT("dve", s_ap, s_ap, bank(bb), ALU.mult, [s_tk, ptok[bb]], [s_tk])
                            if n == 1:
                                TT("dve", macc[:, tt, :], macc[:, tt, :], s_ap, ALU.add, [s_tk, macc_tok[tt]], [macc_tok[tt]])
                            else:
                                TT("dve", mgT[:, dc, ts_], macc[:, tt, :], s_ap, ALU.add, [s_tk, macc_tok[tt]], [mg_tok[tt]])
            fw.barrier()
            AR.release(m_mg)
            if stop_after == "merge":
                DMA("sp", dbg_d[:, :, :], mgT, reads=mg_tok)
                break

            Wout = AR.alloc([8, 1024], BF16)
            w_toks = toks(8)
            for k in range(8):
                DMA("pool", Wout[:, k, :], wout_d[l, k * 128:(k + 1) * 128, :], writes=[w_toks[k]])
            g_bc = AR.alloc([1024], F32)
            g_tok = Tok()
            load_bc(g_bc, norms_d[l, 1:2, :], g_tok)
            Wxq = AR.alloc_top([8, 1024], BF16)
            Wxo = AR.alloc_top([8, 1024], BF16)
            wq_toks, wo_toks = toks(8), toks(8)
            for k in range(8):
                DMA("pool", Wxq[:, k, :], wxq_d[l, k * 128:(k + 1) * 128, :], writes=[wq_toks[k]])
            for k in range(8):
                DMA("pool", Wxo[:, k, :], wxo_d[l, k * 128:(k + 1) * 128, :], writes=[wo_toks[k]])
            out_proj(mgT, lambda tb: mg_tok[tb // 4], 8, Wout, w_toks, g_bc, g_tok, xsrc, sq)
            fw.barrier()
            if stop_after == "mixer":
                break

            AR.release(0)
            memT = AR.alloc([8, MEM], BF16)
            memT_tok = toks(2)
            KT = AR.alloc([8, MEM], BF16)
            KT_tok = Tok()
            Vx = AR.alloc([2, 4 * 257], BF16)
            Vx4 = Vx.rearrange("p a (h c) -> p a h c", h=4, c=257)
            Vx_tok = toks(2)
            gpre = AR.alloc([1024], F32)
            gpost = AR.alloc([1024], F32)
            gmem = AR.alloc([1024], F32)
            t_gx = Tok()
            load_bc(gpre, norms_d[l, 2:3, :], t_gx)
            load_bc(gpost, norms_d[l, 3:4, :], t_gx)
            load_bc(gmem, norms_d[l, 4:5, :], t_gx)
            m_x = AR.mark()
            mem_toks = toks(2)
            nrm = NormT(nx=2, nh=4)
            nrm.run(lambda b: mem_d[sq, b * 128:(b + 1) * 128, :], mem_toks, 2, gmem, t_gx, memT, lambda b: memT_tok[b])
            Wk = Rot([(AR.alloc([8, 128], BF16), Tok()) for _ in range(2)])
            for blk in range(8):
                w_ap, w_tk = Wk.next()
                DMA("pool", w_ap, wblock(wxkv_d[l], blk * 128, 128), writes=[w_tk])
                bk = pjb.next()
                for k in range(8):
                    MM(bank(bk)[:, 0:MEM], w_ap[:, k, :], memT[:, k, :], start=(k == 0), stop=(k == 7), reads=[w_tk] + memT_tok, writes=[ptok[bk]])
                CP("act", KT[:, blk, :], bank(bk)[:, 0:MEM], [ptok[bk]], [KT_tok])
            Wvx = AR.alloc([8, 1024], BF16)
            wv_toks = toks(8)
            for k in range(8):
                DMA("pool", Wvx[:, k, :], wxkv_d[l, k * 128:(k + 1) * 128, D:2 * D], writes=[wv_toks[k]])
            for mb in range(2):
                MEMSET("pool", Vx4[:, mb, :, 256:257], 1.0, writes=[Vx_tok[mb]])
                for half in range(2):
                    bk = pjb.next()
                    for k in range(8):
                        MM(bank(bk), memT[:, k, mb * 128:(mb + 1) * 128], Wvx[:, k, half * 512:(half + 1) * 512], start=(k == 0), stop=(k == 7),
                           reads=[memT_tok[mb], wv_toks[k]], writes=[ptok[bk]])
                    CP("dve", Vx4[:, mb, 2 * half:2 * half + 2, 0:256], bank(bk).rearrange("p (h c) -> p h c", h=2, c=256), [ptok[bk]], [Vx_tok[mb]])
            hx = Rot([(AR.alloc([8, 512], BF16), toks(1)) for _ in range(2)])
            qx = Rot([(AR.alloc([8, 512], BF16), Tok()) for _ in range(2)])
            PTx = Rot([(AR.alloc([512], BF16), Tok()) for _ in range(4)])
            o_tm = Rot([(AR.alloc([4, 1024], BF16), Tok()) for _ in range(2)])
            oT = Rot([(AR.alloc([8, 512], BF16), Tok()) for _ in range(2)])
            rx = Rot([(AR.alloc([4], F32), Tok()) for _ in range(4)])
            pr = PostRes(gpost, t_gx, nxt=0, nt1=1)
            xs_pool = Rot([(AR.alloc([1024], F32), Tok()) for _ in range(8)])
            xs_of = {}
            sxb = Rot([0, 1, 2, 3])
            oxb = Rot([4, 5])
            def xa_stageA1(tt):
                res = []
                for b in range(4):
                    xb = xs_pool.next()
                    xs_of[tt * 4 + b] = xb
                    res.append(nrm.part1(xrows(out_d, sq, tt * 4 + b), xtok[sq][tt * 4 + b], gpre, t_gx, xbuf=xb))
                return res

            def xa_stageA2(tt, hns):
                h_ap, h_tks = hx.next()
                for b in range(4):
                    nrm.part2(hns[b][0], hns[b][1], h_ap, h_tks[0], b * 128, alt=b)
                q_ap, q_tk = qx.next()
                for blk in range(8):
                    bk = pjb.next()
                    for k in range(8):
                        MM(bank(bk), Wxq[:, k, blk * 128:(blk + 1) * 128], h_ap[:, k, :], start=(k == 0), stop=(k == 7),
                           reads=[wq_toks[k], h_tks[0]], writes=[ptok[bk]])
                    CP("dve", q_ap[:, blk, :], bank(bk), [ptok[bk]], [q_tk])
                return q_ap, q_tk

            def xa_stageBC(tt, q_ap, q_tk, mid_hook):
                ot_ap, ot_tk = o_tm.next()

                def x_scores(h):
                    pts = []
                    for mb in range(2):
                        sb = sxb.next()
                        for cc in range(2):
                            MM(bank(sb), KT[:, 2 * h + cc, mb * 128:(mb + 1) * 128], q_ap[:, 2 * h + cc, :], start=(cc == 0), stop=(cc == 1),
                               reads=[KT_tok, q_tk], writes=[ptok[sb]])
                        p_ap, p_tk = PTx.next()
                        ACT(p_ap, bank(sb), AF.Exp, [ptok[sb]], [p_tk], scale=1.0 / 16.0)
                        pts.append((p_ap, p_tk))
                    return pts

                nxt_pts = x_scores(0)
                for h in range(4):
                    pts = nxt_pts
                    if h + 1 < 4:
                        nxt_pts = x_scores(h + 1)
                    for qb in range(4):
                        ob = oxb.next()
                        ov = bank(ob)[:, 0:257]
                        for mb in range(2):
                            MM(ov, pts[mb][0][:, qb * 128:(qb + 1) * 128], Vx4[:, mb, h, :], start=(mb == 0), stop=(mb == 1),
                               reads=[pts[mb][1], Vx_tok[mb]], writes=[ptok[ob]])
                        r_ap, r_tk = rx.next()
                        RECIP(r_ap[:, 0:1], bank(ob)[:, 256:257], [ptok[ob]], [r_tk])
                        TS1("dve", ot_ap[:, qb, h * 256:(h + 1) * 256], bank(ob)[:, 0:256], r_ap[:, 0:1], ALU.mult, [ptok[ob], r_tk], [ot_tk])
                oT_ap, oT_tk = oT.next()
                for qb in range(4):
                    bk = pjb.next()
                    pv = bank16(bk)
                    for cc in range(8):
                        TR(pv[:, cc * 128:(cc + 1) * 128], ot_ap[:, qb, cc * 128:(cc + 1) * 128], ident[:], [ot_tk, tconst], [ptok[bk]])
                    CP("dve", oT_ap[:, :, qb * 128:(qb + 1) * 128], pv.rearrange("p (k t) -> p k t", k=8), [ptok[bk]], [oT_tk])
                mid_hook()
                for qb in range(4):
                    tb = tt * 4 + qb
                    pp = qb % 2
                    for half in range(2):
                        bk = 2 * pp + half
                        for k in range(8):
                            MM(bank(bk), oT_ap[:, k, qb * 128:(qb + 1) * 128], Wxo[:, k, half * 512:(half + 1) * 512], start=(k == 0), stop=(k == 7),
                               reads=[oT_tk, wo_toks[k]], writes=[ptok[bk]])
                    pr.run(pp, sq, tb, xbuf=xs_of.pop(tb))

            xa_state = {"next": xa_stageA2(0, xa_stageA1(0))}
            for tt in range(NTT):
                xa_cur = xa_state["next"]
                hns_ = xa_stageA1(tt + 1) if tt + 1 < NTT else None

                def mid(tt=tt, hns_=hns_):
                    if hns_ is not None:
                        xa_state["next"] = xa_stageA2(tt + 1, hns_)
                xa_stageBC(tt, xa_cur[0], xa_cur[1], mid)
            fw.barrier()
            AR.free_top(AW)
            if stop_after == "xattn":
                break

            AR.release(0)
            GT = min(1024, S)
            NG = S // GT
            NGB = GT // 128
            W2 = AR.alloc([32, 1024], BF16)
            w2_toks = toks(32)
            gpre = AR.alloc([1024], F32)
            gpost = AR.alloc([1024], F32)
            t_gf = Tok()
            load_bc(gpre, norms_d[l, 5:6, :], t_gf)
            load_bc(gpost, norms_d[l, 6:7, :], t_gf)
            hF = AR.alloc([8, GT], BF16)
            hF_tok = toks(GT // 512)
            aT = AR.alloc([32, GT], BF16)
            aT_tok = toks(32)
            W1 = Rot([(AR.alloc([8, 256], BF16), Tok()) for _ in range(2)])
            rl = Rot([(AR.alloc([512], F32), Tok()) for _ in range(2)])
            pr = PostRes(gpost, t_gf, nxt=3, nt1=1)
            pr.plan(out_d, sq, range(NTB))
            nrm = NormT(pb=(4, 5))
            fbk = Rot([0, 1, 2, 3])
            def ffn_norm(gi):
                t0_ = gi * NGB
                nrm.run(lambda b: xrows(out_d, sq, t0_ + b), xtok[sq][t0_:t0_ + NGB], NGB, gpre, t_gf, hF, lambda b: hF_tok[b // 4])

            ffn_norm(0)
            for gi in range(NG):
                t0 = gi * NGB
                for fc2 in range(16):
                    w_ap, w_tk = W1.next()
                    DMA("pool", w_ap, wblock(wff1_d[l], fc2 * 256, 256), writes=[w_tk])
                    if gi == 0 and fc2 >= 1:
                        for k in ((2 * (fc2 - 1), 2 * (fc2 - 1) + 1) if fc2 < 15 else (28, 29, 30, 31)):
                            DMA("pool", W2[:, k, :], wff2_d[l, k * 128:(k + 1) * 128, :], writes=[w2_toks[k]])
                    for sub in range(2):
                        fc = 2 * fc2 + sub
                        for tt in range(GT // 512):
                            bk = fbk.next()
                            for k in range(8):
                                MM(bank(bk), w_ap[:, k, sub * 128:(sub + 1) * 128], hF[:, k, tt * 512:(tt + 1) * 512], start=(k == 0), stop=(k == 7),
                                   reads=[w_tk, hF_tok[tt]], writes=[ptok[bk]])
                            r_ap, r_tk = rl.next()
                            ACT(r_ap, bank(bk), AF.Relu, [ptok[bk]], [r_tk])
                            TT("dve", aT[:, fc, tt * 512:(tt + 1) * 512], r_ap, r_ap, ALU.mult, [r_tk], [aT_tok[fc]])
                for b in range(NGB):
                    tb = t0 + b
                    pp = 2 + (b % 2)
                    if gi + 1 < NG:
                        t1_ = (gi + 1) * NGB + b
                        nh = nrm.part1(xrows(out_d, sq, t1_), xtok[sq][t1_], gpre, t_gf)
                    for half in range(2):
                        bk = 2 * pp + half
                        for k in range(32):
                            MM(bank(bk), aT[:, k, b * 128:(b + 1) * 128], W2[:, k, half * 512:(half + 1) * 512], start=(k == 0), stop=(k == 31),
                               reads=[aT_tok[k], w2_toks[k]], writes=[ptok[bk]])
                    pr.run(pp, sq, tb)
                    if gi + 1 < NG:
                        nrm.part2(nh[0], nh[1], hF, hF_tok[b // 4], b * 128, alt=b)
            fw.barrier()
        else:
            continue
        break

    fw.barrier()
    stats = fw.emit()
    fw.es.close()
    return nc, stats


def _host_layout(inp, depth=DEPTH):
    w_in = np.ascontiguousarray(inp["w_in"][:depth], dtype=np.float32)
    b_in = np.ascontiguousarray(inp["b_in"][:depth], dtype=np.float32)
    small_cols = list(range(FOX_F, FOX_F + 8)) + list(range(ML_I, ML_I + 4)) + list(range(ML_F, ML_F + 4))
    w_small = np.ascontiguousarray(w_in[:, :, small_cols])
    b_small = np.ascontiguousarray(b_in[:, None, small_cols])
    starts = ([FOX_Q + 128 * i for i in range(4)] + [FOX_K + 128 * i for i in range(4)] +
              [ML_Q + 128 * i for i in range(4)] + [ML_K + 128 * i for i in range(4)] +
              [G_U + 128 * i for i in range(4)] + [GATE + 128 * i for i in range(24)])
    b_col = np.stack([np.stack([b_in[l, s:s + 128] for s in starts], axis=1) for l in range(depth)], axis=0)
    cw = inp["conv_w"][:depth]
    conv_col = np.ascontiguousarray(cw.reshape(depth, 4, 8, 128).transpose(0, 3, 2, 1))
    wsT = np.ascontiguousarray(inp["gmlp_ws"][:depth].transpose(0, 3, 1, 2))
    d = {
        "norms": inp["norms"][:depth], "w_in": w_in, "b_in": b_in[:, None, :], "w_small": w_small, "b_small": b_small,
        "b_col": np.ascontiguousarray(b_col), "conv_col": conv_col,
        "mlstm_norm": inp["mlstm_norm"][:depth, None, :], "gmlp_norm": inp["gmlp_norm"][:depth, None, :],
        "gmlp_wsT": wsT, "gmlp_bs": inp["gmlp_bs"][:depth].reshape(depth, 1, 512),
        "w_branch": inp["w_branch"][:depth], "w_out": inp["w_out"][:depth], "w_xq": inp["w_xq"][:depth],
        "w_xkv": inp["w_xkv"][:depth], "w_xo": inp["w_xo"][:depth], "w_ff1": inp["w_ff1"][:depth], "w_ff2": inp["w_ff2"][:depth],
    }
    return {k: np.ascontiguousarray(v, dtype=np.float32) for k, v in d.items()}


_CACHE = {}


def kernel(**inputs):
    x = np.asarray(inputs["x"], dtype=np.float32)
    mem = np.asarray(inputs["mem"], dtype=np.float32)
    B, S, _ = x.shape
    n_cores = 8
    per = B // n_cores
    key = (S, per)
    if key not in _CACHE:
        _CACHE[key] = build_program(S, DEPTH, per)[0]
    nc = _CACHE[key]
    params = _host_layout(inputs)
    in_maps = []
    for c in range(n_cores):
        m = dict(params)
        m["x"] = np.ascontiguousarray(x[c * per:(c + 1) * per])
        m["mem"] = np.ascontiguousarray(mem[c * per:(c + 1) * per])
        in_maps.append(m)
    res = run_bass_kernel_spmd(nc, in_maps, core_ids=list(range(n_cores)))
    out = np.concatenate([np.asarray(r["out"]) for r in res.results], axis=0)
    return out.astype(np.float32)
```

```python
import math
import numpy as np
import concourse.bass as bass
import concourse.mybir as mybir
from concourse.bass_utils import run_bass_kernel_spmd
from contextlib import ExitStack

F32 = mybir.dt.float32
BF16 = mybir.dt.bfloat16
AF = mybir.ActivationFunctionType
ALU = mybir.AluOpType

ENGS = ["pe", "act", "dve", "pool", "sp"]

D = 1024
DEPTH = 2
NSEQ = 2
MEM = 256
D_IN = 7696
FOX_Q, FOX_K, FOX_V, FOX_F = 0, 512, 1024, 1536
ML_Q, ML_K, ML_V, ML_I, ML_F, ML_O = 1544, 2056, 2568, 3080, 3084, 3088
G_U, G_V, GATE = 3600, 4112, 4624
EPS = 1e-6


class Tok:
    __slots__ = ("w", "r")

    def __init__(self):
        self.w = None
        self.r = []


def toks(n):
    return [Tok() for _ in range(n)]


class FW:
    def __init__(self, nc, n_dma_sems=32):
        self.nc = nc
        self.ins = {e: [] for e in ENGS}
        self.n_dma_sems = n_dma_sems
        self.dma_count = [0] * n_dma_sems
        self.dma_rr = 0
        self.es = ExitStack()
        self.uid = 0

    def sbuf(self, shape, dtype):
        self.uid += 1
        return self.es.enter_context(self.nc.sbuf_tensor(f"sb{self.uid}", list(shape), dtype))

    def psum(self, shape, dtype):
        self.uid += 1
        return self.es.enter_context(self.nc.psum_tensor(f"ps{self.uid}", list(shape), dtype))

    def _deps(self, reads, writes):
        deps = set()
        for t in reads:
            if t.w is not None:
                deps.add(t.w)
        for t in writes:
            if t.w is not None:
                deps.add(t.w)
            deps.update(t.r)
        return deps

    def _mark(self, me, reads, writes):
        for t in reads:
            if me[0] != "dmasem":
                t.r = [x for x in t.r if x[0] != me[0]]
            t.r.append(me)
        for t in writes:
            t.w = me
            t.r = []

    def op(self, eng, fn, reads=(), writes=()):
        idx = len(self.ins[eng])
        deps = self._deps(reads, writes)
        me = (eng, idx)
        self.ins[eng].append(dict(fn=fn, deps=deps, dma=None, sig=False))
        self._mark(me, reads, writes)
        return me

    def dma(self, eng, fn, reads=(), writes=()):
        half = self.n_dma_sems // 2
        if not hasattr(self, "dma_rr2"):
            self.dma_rr2 = {"sp": 0, "pool": 0, "act": 0}
        base = half if eng == "pool" else 0
        s = base + self.dma_rr2[eng] % half
        self.dma_rr2[eng] += 1
        deps = self._deps(reads, writes)
        if self.dma_count[s] > 0:
            deps.add(("dmasem", s, 16 * self.dma_count[s]))
        self.dma_count[s] += 1
        me = ("dmasem", s, 16 * self.dma_count[s])
        self.ins[eng].append(dict(fn=fn, deps=deps, dma=s, sig=False))
        self._mark(me, reads, writes)
        return me

    def barrier(self):
        deps = set()
        for e in ENGS:
            for i in range(len(self.ins[e]) - 1, -1, -1):
                if self.ins[e][i]["dma"] is None:
                    deps.add((e, i))
                    break
        for s in range(self.n_dma_sems):
            if self.dma_count[s] > 0:
                deps.add(("dmasem", s, 16 * self.dma_count[s]))
        for e in ENGS:
            self.ins[e].append(dict(fn=None, deps=set(deps), dma=None, sig=False))

    def emit(self):
        nc = self.nc
        for e in ENGS:
            for rec in self.ins[e]:
                for d in rec["deps"]:
                    if d[0] != "dmasem" and not (d[0] == "pe" and e == "pe"):
                        self.ins[d[0]][d[1]]["sig"] = True
        for e in ENGS:
            c = 0
            for rec in self.ins[e]:
                if rec["sig"]:
                    c += 1
                rec["cnt"] = c
        es = self.es
        esem = {e: es.enter_context(nc.semaphore(f"sem_{e}")) for e in ENGS}
        dsem = [es.enter_context(nc.semaphore(f"sem_dma{i}")) for i in range(self.n_dma_sems)]
        stats = {e: [0, 0] for e in ENGS}

        def replay(ename, eng):
            seen = {}
            for rec in self.ins[ename]:
                need = {}
                for d in rec["deps"]:
                    if d[0] == "dmasem":
                        key = ("d", d[1])
                        val = d[2]
                    else:
                        if d[0] == "pe" and ename == "pe":
                            continue
                        key = ("e", d[0])
                        val = self.ins[d[0]][d[1]]["cnt"]
                    if val > need.get(key, 0):
                        need[key] = val
                for key, val in need.items():
                    if seen.get(key, 0) >= val:
                        continue
                    seen[key] = val
                    sem = dsem[key[1]] if key[0] == "d" else esem[key[1]]
                    eng.wait_ge(sem, val)
                    stats[ename][1] += 1
                fn = rec["fn"]
                if fn is None:
                    if not rec["sig"]:
                        continue
                    bi = eng.nop()
                else:
                    bi = fn(eng)
                stats[ename][0] += 1
                if rec["dma"] is not None:
                    bi.then_inc(dsem[rec["dma"]], 16)
                elif rec["sig"]:
                    bi.then_inc(esem[ename], 1)

        with nc.Block() as block:
            @block.tensor
            def _(eng):
                replay("pe", eng)

            @block.scalar
            def _(eng):
                replay("act", eng)

            @block.vector
            def _(eng):
                replay("dve", eng)

            @block.gpsimd
            def _(eng):
                replay("pool", eng)

            @block.sync
            def _(eng):
                replay("sp", eng)
        return stats


class Arena:
    def __init__(self, ap, nwords):
        self.ap = ap
        self.n = nwords
        self.off = 0

    def alloc(self, shape, dtype):
        n = 1
        for s in shape:
            n *= s
        words = n if dtype == F32 else (n + 1) // 2
        words = (words + 7) // 8 * 8
        assert self.off + words <= self.n, f"arena overflow {self.off}+{words}>{self.n}"
        sl = self.ap[:, self.off:self.off + words]
        self.off += words
        if dtype != F32:
            sl = sl.bitcast(dtype)
        sl = sl[:, 0:n]
        if len(shape) == 2:
            return sl.rearrange("p (a b) -> p a b", a=shape[0], b=shape[1])
        if len(shape) == 3:
            return sl.rearrange("p (a b c) -> p a b c", a=shape[0], b=shape[1], c=shape[2])
        return sl

    def alloc_top(self, shape, dtype):
        n = 1
        for s_ in shape:
            n *= s_
        words = n if dtype == F32 else (n + 1) // 2
        words = (words + 7) // 8 * 8
        assert self.off + words <= self.n, "arena overflow (top)"
        self.n -= words
        sl = self.ap[:, self.n:self.n + words]
        if dtype != F32:
            sl = sl.bitcast(dtype)
        sl = sl[:, 0:n]
        if len(shape) == 2:
            return sl.rearrange("p (a b) -> p a b", a=shape[0], b=shape[1])
        return sl

    def free_top(self, total):
        self.n = total

    def mark(self):
        return self.off

    def release(self, m):
        self.off = m


class Rot:
    def __init__(self, items):
        self.items = items
        self.i = 0

    def next(self):
        it = self.items[self.i % len(self.items)]
        self.i += 1
        return it


def build_program(S, depth=DEPTH, nseq=NSEQ, stop_after=None):
    NTB = S // 128
    NTT = S // 512
    nc = bass.Bass("TRN2", target_bir_lowering=False)
    fw = FW(nc)

    def din(name, shape):
        return nc.dram_tensor(name, list(shape), F32, kind="ExternalInput").ap()

    x_d = din("x", [nseq, S, D])
    mem_d = din("mem", [nseq, MEM, D])
    out_d = nc.dram_tensor("out", [nseq, S, D], F32, kind="ExternalOutput").ap()
    dbg_d = nc.dram_tensor("dbg", [128, 8, S], BF16, kind="ExternalOutput").ap() if stop_after else None
    norms_d = din("norms", [depth, 7, D])
    w_in_d = din("w_in", [depth, D, D_IN])
    b_row_d = din("b_in", [depth, 1, D_IN])
    wsm_d = din("w_small", [depth, D, 16])
    bsm_d = din("b_small", [depth, 1, 16])
    bcol_d = din("b_col", [depth, 128, 44])
    cw_d = din("conv_col", [depth, 128, 8, 4])
    mlg_d = din("mlstm_norm", [depth, 1, 512])
    gg_d = din("gmlp_norm", [depth, 1, 512])
    wsT_d = din("gmlp_wsT", [depth, 128, 4, 128])
    bs_d = din("gmlp_bs", [depth, 1, 512])
    wbr_d = din("w_branch", [depth, 3, 512, D])
    wout_d = din("w_out", [depth, D, D])
    wxq_d = din("w_xq", [depth, D, D])
    wxkv_d = din("w_xkv", [depth, D, 2 * D])
    wxo_d = din("w_xo", [depth, D, D])
    wff1_d = din("w_ff1", [depth, D, 4 * D])
    wff2_d = din("w_ff2", [depth, 4 * D, D])

    def MM(out, lhsT, rhs, start=True, stop=True, reads=(), writes=(), skip=False):
        fw.op("pe", lambda e: e.matmul(out, lhsT, rhs, start=start, stop=stop, skip_group_check=skip), reads, writes)

    def TR(out, in_, ident, reads=(), writes=()):
        fw.op("pe", lambda e: e.transpose(out, in_, ident), reads, writes)

    def ACT(out, in_, func, reads=(), writes=(), bias=None, scale=None, accum=None):
        kw = {}
        if bias is not None:
            kw["bias"] = bias
        if scale is not None:
            kw["scale"] = scale
        if accum is not None:
            kw["accum_out"] = accum
        fw.op("act", lambda e: e.activation(out=out, in_=in_, func=func, **kw), list(reads) + [tconst], writes)

    def TT(eng, out, in0, in1, op, reads=(), writes=()):
        fw.op(eng, lambda e: e.tensor_tensor(out=out, in0=in0, in1=in1, op=op), reads, writes)

    def TS(eng, out, in0, s1, s2, op0, op1, reads=(), writes=()):
        fw.op(eng, lambda e: e.tensor_scalar(out=out, in0=in0, scalar1=s1, scalar2=s2, op0=op0, op1=op1), reads, writes)

    def TS1(eng, out, in0, s1, op0, reads=(), writes=()):
        fw.op(eng, lambda e: e.tensor_scalar(out=out, in0=in0, scalar1=s1, scalar2=None, op0=op0), reads, writes)

    def STT(eng, out, in0, scalar, in1, op0, op1, reads=(), writes=()):
        fw.op(eng, lambda e: e.scalar_tensor_tensor(out=out, in0=in0, scalar=scalar, in1=in1, op0=op0, op1=op1), reads, writes)

    def CP(eng, out, in_, reads=(), writes=()):
        if eng == "act":
            fw.op("act", lambda e: e.copy(out=out, in_=in_), reads, writes)
        else:
            fw.op(eng, lambda e: e.tensor_copy(out=out, in_=in_), reads, writes)

    def RECIP(out, in_, reads=(), writes=()):
        fw.op("dve", lambda e: e.reciprocal(out=out, in_=in_), reads, writes)

    def MEMSET(eng, ap, val, reads=(), writes=()):
        fw.op(eng, lambda e: e.memset(ap, val), reads, writes)

    def DMA(eng, out, in_, reads=(), writes=()):
        fw.dma(eng, lambda e: e.dma_start(out=out, in_=in_), reads, writes)

    identf = fw.sbuf([128, 128], F32)
    ident = fw.sbuf([128, 128], BF16)
    mask32 = fw.sbuf([128, 128], F32)
    mask16 = fw.sbuf([128, 128], BF16)
    ones32 = fw.sbuf([128, 128], F32)
    sel64 = fw.sbuf([128, 128], F32)
    ones16r = fw.sbuf([1, 128], BF16)
    cst = fw.sbuf([128, 4], F32)
    tconst = Tok()
    MEMSET("pool", identf[:], 0.0, writes=[tconst])
    fw.op("pool", lambda e: e.affine_select(out=identf[:], in_=identf[:], compare_op=ALU.not_equal, fill=1.0,
                                            base=0, pattern=[[-1, 128]], channel_multiplier=1), [tconst], [tconst])
    CP("pool", ident[:], identf[:], [tconst], [tconst])
    MEMSET("pool", mask32[:], 1.0, writes=[tconst])
    fw.op("pool", lambda e: e.affine_select(out=mask32[:], in_=mask32[:], compare_op=ALU.is_ge, fill=0.0,
                                            base=0, pattern=[[1, 128]], channel_multiplier=-1), [tconst], [tconst])
    CP("pool", mask16[:], mask32[:], [tconst], [tconst])
    MEMSET("pool", ones32[:], 1.0, writes=[tconst])
    MEMSET("pool", ones16r[:], 1.0, writes=[tconst])
    CP("pool", sel64[:], identf[:, 0:1].to_broadcast([128, 128]), [tconst], [tconst])
    MEMSET("pool", cst[:, 0:1], EPS, writes=[tconst])
    MEMSET("pool", cst[:, 1:2], -0.5 * math.log(128.0), writes=[tconst])
    MEMSET("pool", cst[:, 2:3], 1.0, writes=[tconst])
    eps_c = cst[:, 0:1]
    lnsc_c = cst[:, 1:2]
    one_c = cst[:, 2:3]

    PS = [fw.psum([128, 1024], F32) for _ in range(4)]
    ptok = toks(8)

    def bank(b):
        return PS[b // 2][:, (b % 2) * 512:(b % 2) * 512 + 512]

    def bank16(b):
        return bank(b).bitcast(BF16)

    AW = (nc.sbuf_bytes_remaining - 2048) // 4
    AW = AW // 8 * 8
    arena_t = fw.sbuf([128, AW], F32)
    AR = Arena(arena_t[:], AW)

    xtok = [[Tok() for _ in range(NTB)] for _ in range(nseq)]

    def xrows(src, sq, tb):
        return src[sq, tb * 128:(tb + 1) * 128, :]

    def wblock(w_ap, c0, ncols):
        return w_ap[:, c0:c0 + ncols].rearrange("(kc p) c -> p kc c", p=128)

    def load_bc(dst, row_ap, tok):
        DMA("sp", dst, row_ap.partition_broadcast(128), writes=[tok])

    def rstd_from_ss(ss, tmp, rstd, inv_n, tk):
        ACT(tmp, ss, AF.Ln, [tk], [tk], bias=eps_c, scale=inv_n)
        ACT(rstd, tmp, AF.Exp, [tk], [tk], scale=-0.5)

    class NormT:
        def __init__(self, pb=(6, 7), nx=3, nh=2):
            self.xt = Rot([(AR.alloc([1024], F32), Tok()) for _ in range(nx)])
            self.hn = Rot([(AR.alloc([1024], BF16), Tok()) for _ in range(nh)])
            self.junk = AR.alloc([1024], BF16)
            self.sm = Rot([(AR.alloc([4], F32), Tok()) for _ in range(max(nx, 4))])
            self.pbr = Rot(list(pb))
            self.tj = Tok()

        def part1(self, src_ap, src_tok, g_bc, g_tok, xbuf=None):
            x_ap, x_tk = xbuf if xbuf is not None else self.xt.next()
            DMA("sp", x_ap, src_ap, reads=[src_tok], writes=[x_tk])
            s_ap, s_tk = self.sm.next()
            ACT(self.junk, x_ap, AF.Square, [x_tk], [self.tj, s_tk], accum=s_ap[:, 0:1])
            rstd_from_ss(s_ap[:, 0:1], s_ap[:, 1:2], s_ap[:, 2:3], 1.0 / D, s_tk)
            h_ap, h_tk = self.hn.next()
            STT("dve", h_ap, x_ap, s_ap[:, 2:3], g_bc, ALU.mult, ALU.mult, [x_tk, s_tk, g_tok], [h_tk])
            return h_ap, h_tk

        def part2(self, h_ap, h_tk, dstT, dst_tok, col, alt=0):
            pbk = self.pbr.next()
            pv = bank16(pbk)
            for k in range(8):
                TR(pv[:, k * 128:(k + 1) * 128], h_ap[:, k * 128:(k + 1) * 128], ident[:], [h_tk, tconst], [ptok[pbk]])
            CP("dve", dstT[:, :, col:col + 128],
               pv.rearrange("p (k t) -> p k t", k=8), [ptok[pbk]], [dst_tok])

        def run(self, src_fn, src_toks, nblk, g_bc, g_tok, dstT, dst_tok_fn, col0=0):
            for b in range(nblk):
                h_ap, h_tk = self.part1(src_fn(b), src_toks[b], g_bc, g_tok)
                self.part2(h_ap, h_tk, dstT, dst_tok_fn(b), col0 + b * 128, alt=b)

    class PostRes:
        def __init__(self, g_bc, g_tok, nxt=4, nt1=2):
            self.g_bc, self.g_tok = g_bc, g_tok
            self.xt = Rot([(AR.alloc([1024], F32), Tok()) for _ in range(nxt)])
            self.t1 = Rot([(AR.alloc([1024], F32), Tok()) for _ in range(nt1)])
            self.junk = AR.alloc([1024], BF16)
            self.tj = Tok()
            self.sm = Rot([(AR.alloc([4], F32), Tok()) for _ in range(3)])
            self.q = {}
            self.todo = []

        def plan(self, src, sq, tbs):
            self.src, self.sq = src, sq
            self.todo = list(tbs)
            self._fill()

        def _fill(self):
            while len(self.q) < 2 and self.todo:
                tb = self.todo.pop(0)
                x_ap, x_tk = self.xt.next()
                DMA("sp", x_ap, xrows(self.src, self.sq, tb), reads=[xtok[self.sq][tb]], writes=[x_tk])
                self.q[tb] = (x_ap, x_tk)

        def run(self, pp, sq, tb, xbuf=None):
            if xbuf is not None:
                x_ap, x_tk = xbuf
            else:
                x_ap, x_tk = self.q.pop(tb)
                self._fill()
            y = PS[pp][:, :]
            ytk = [ptok[2 * pp], ptok[2 * pp + 1]]
            s_ap, s_tk = self.sm.next()
            ACT(self.junk, y, AF.Square, ytk, [self.tj, s_tk], accum=s_ap[:, 0:1])
            rstd_from_ss(s_ap[:, 0:1], s_ap[:, 1:2], s_ap[:, 2:3], 1.0 / D, s_tk)
            t_ap, t_tk = self.t1.next()
            STT("dve", t_ap, y, s_ap[:, 2:3], self.g_bc, ALU.mult, ALU.mult, ytk + [s_tk, self.g_tok], [t_tk])
            TT("dve", x_ap, x_ap, t_ap, ALU.add, [x_tk, t_tk], [x_tk])
            DMA("sp", xrows(out_d, sq, tb), x_ap, reads=[x_tk], writes=[xtok[sq][tb]])

    def out_proj(actT, act_tok_fn, nk, W, w_toks, g_bc, g_tok, src, sq):
        pr = PostRes(g_bc, g_tok)
        pr.plan(src, sq, range(NTB))
        for tb in range(NTB):
            pp = tb % 2
            for half in range(2):
                bk = 2 * pp + half
                for k in range(nk):
                    MM(bank(bk), actT[:, k, tb * 128:(tb + 1) * 128], W[:, k, half * 512:(half + 1) * 512],
                       start=(k == 0), stop=(k == nk - 1), reads=[act_tok_fn(tb), w_toks[k]], writes=[ptok[bk]])
            pr.run(pp, sq, tb)

    for sq in range(nseq):
        for l in range(depth):
            xsrc = x_d if l == 0 else out_d
            AR.release(0)
            hT = AR.alloc([8, S], BF16)
            hT_tok = toks(NTT)
            yfT = AR.alloc([4, S], BF16)
            ymT = AR.alloc([4, S], BF16)
            ygT = AR.alloc([4, S], BF16)
            yfT_tok, ymT_tok, ygT_tok = toks(NTT), toks(NTT), toks(NTT)
            apr = AR.alloc([NTB, 4], F32)
            ebt = AR.alloc([NTB, 4], F32)
            wsc = AR.alloc([NTB, 4], F32)
            emb = AR.alloc([NTB, 4], F32)
            ml_tok = Tok()
            bcol = AR.alloc([44], F32)
            bcol_tok = Tok()
            DMA("sp", bcol, bcol_d[l], writes=[bcol_tok])
            m_mix = AR.mark()
            fbT = AR.alloc([8, NTB, NTB // 2], F32)
            fb_tok = Tok()

            g_bc = AR.alloc([1024], F32)
            g_tok = Tok()
            load_bc(g_bc, norms_d[l, 0:1, :], g_tok)
            NormT(nx=4).run(lambda b: xrows(xsrc, sq, b), xtok[sq], NTB, g_bc, g_tok, hT, lambda b: hT_tok[b // 4])

            def emit_PA():
                wsm = AR.alloc([8, 16], BF16)
                t_wsm = Tok()
                DMA("pool", wsm, wsm_d[l].rearrange("(kc p) c -> p kc c", p=128), writes=[t_wsm])
                bsm = AR.alloc([16], F32)
                t_bsm = Tok()
                load_bc(bsm, bsm_d[l], t_bsm)
                gt = AR.alloc([NTB, 16], F32)
                nl = AR.alloc([NTB, 16], F32)
                gw = AR.alloc([NTB, 16], F32)
                tot = AR.alloc([NTB, 16], F32)
                off = AR.alloc([NTB, 16], F32)
                Gc = AR.alloc([NTB, 16], F32)
                Gref = AR.alloc([NTB, 16], F32)
                tg = Tok()
                pg = bank(0)[:, 0:NTB * 16].rearrange("p (a b) -> p a b", a=NTB, b=16)
                for tb in range(NTB):
                    for k in range(8):
                        MM(bank(0)[:, tb * 16:(tb + 1) * 16], hT[:, k, tb * 128:(tb + 1) * 128], wsm[:, k, :],
                           start=(k == 0), stop=(k == 7), reads=[hT_tok[tb // 4], t_wsm], writes=[ptok[0]])
                TT("dve", gt, pg, bsm.unsqueeze(1).to_broadcast([128, NTB, 16]), ALU.add, [ptok[0], t_bsm], [tg])
                ACT(nl, gt, AF.Exp, [tg], [tg], scale=-1.0)
                ACT(nl, nl, AF.Ln, [tg], [tg], bias=one_c, scale=1.0)
                nlf = nl.rearrange("p a b -> p (a b)")
                MM(bank(1)[:, 0:NTB * 16], mask32[:], nlf, reads=[tg, tconst], writes=[ptok[1]])
                MM(bank(2)[:, 0:NTB * 16], ones32[:], nlf, reads=[tg, tconst], writes=[ptok[2]])
                CP("dve", gw.rearrange("p a b -> p (a b)"), bank(1)[:, 0:NTB * 16], [ptok[1]], [tg])
                CP("dve", tot.rearrange("p a b -> p (a b)"), bank(2)[:, 0:NTB * 16], [ptok[2]], [tg])
                MEMSET("dve", off[:, 0, :], 0.0, writes=[tg])
                for tb in range(1, NTB):
                    TT("dve", off[:, tb, :], off[:, tb - 1, :], tot[:, tb - 1, :], ALU.add, [tg], [tg])
                TT("dve", Gc, gw, off, ALU.add, [tg], [tg])
                MM(bank(3)[:, 0:NTB * 16], sel64[:], Gc.rearrange("p a b -> p (a b)"), reads=[tg, tconst], writes=[ptok[3]])
                CP("dve", Gref.rearrange("p a b -> p (a b)"), bank(3)[:, 0:NTB * 16], [ptok[3]], [tg])
                Gref_odd = Gref.rearrange("p (i two) c -> p i two c", two=2)[:, :, 1, :]
                for h in range(8):
                    TT("dve", fbT[:, h, :, :], Gc[:, :, h].unsqueeze(2).to_broadcast([128, NTB, NTB // 2]),
                       Gref_odd[:, :, h].unsqueeze(1).to_broadcast([128, NTB, NTB // 2]), ALU.subtract, [tg], [fb_tok])
                TT("dve", apr, gt[:, :, 8:12], gw[:, :, 12:16], ALU.add, [tg], [ml_tok])
                ACT(apr, apr, AF.Exp, [ml_tok], [ml_tok], bias=lnsc_c, scale=1.0)
                ACT(ebt, tot[:, :, 12:16], AF.Exp, [tg], [ml_tok], scale=-1.0)
                TT("dve", wsc, apr, ebt, ALU.mult, [ml_tok], [ml_tok])
                ACT(emb, gw[:, :, 12:16], AF.Exp, [tg], [ml_tok])


            Wv = AR.alloc([8, 512], BF16)
            t_wv = Tok()
            DMA("pool", Wv, wblock(w_in_d[l], FOX_V, 512), writes=[t_wv])
            bv = AR.alloc([512], F32)
            t_bv = Tok()
            load_bc(bv, b_row_d[l, :, FOX_V:FOX_V + 512], t_bv)
            Vaug = AR.alloc([NTB, 8 * 65], BF16)
            Vaug4 = Vaug.rearrange("p a (h c) -> p a h c", h=8, c=65)
            V_tok = toks(NTB)
            yf_tm = AR.alloc([NTB, 512], BF16)
            yf_tok = toks(NTT)
            for tb in range(NTB):
                MEMSET("pool", Vaug4[:, tb, :, 64:65], 1.0, writes=[V_tok[tb]])
                bk = 6 + (tb % 2)
                for k in range(8):
                    MM(bank(bk), hT[:, k, tb * 128:(tb + 1) * 128], Wv[:, k, :], start=(k == 0), stop=(k == 7),
                       reads=[hT_tok[tb // 4], t_wv], writes=[ptok[bk]])
                TT("dve", Vaug4[:, tb, :, 0:64], bank(bk).rearrange("p (h c) -> p h c", h=8, c=64),
                   bv.rearrange("p (h c) -> p h c", h=8, c=64), ALU.add, [ptok[bk], t_bv], [V_tok[tb]])
            Wqk = Rot([(AR.alloc([8, 128], BF16), Tok()) for _ in range(4)])
            qkT = Rot([(AR.alloc([S], BF16), toks(NTT)) for _ in range(4)])
            PTs = Rot([(AR.alloc([512], BF16), toks(4)) for _ in range(6)])
            rinv = Rot([(AR.alloc([4], F32), Tok()) for _ in range(2)])
            sbk = Rot([0, 1, 2, 3])
            obk = Rot([4, 5, 6, 7])
            pjb = sbk
            def fox_proj_setup(hp):
                items = []
                for which in range(2):
                    w_ap, w_tk = Wqk.next()
                    c0 = (FOX_Q if which == 0 else FOX_K) + hp * 128
                    DMA("pool", w_ap, wblock(w_in_d[l], c0, 128), writes=[w_tk])
                    t_ap, t_tks = qkT.next()
                    bcolk = hp if which == 0 else 4 + hp
                    items.append((w_ap, w_tk, t_ap, t_tks, bcolk))
                return items

            def fox_proj_part(items, tt):
                for (w_ap, w_tk, t_ap, t_tks, bcolk) in items:
                    bk = pjb.next()
                    for k in range(8):
                        MM(bank(bk), w_ap[:, k, :], hT[:, k, tt * 512:(tt + 1) * 512], start=(k == 0), stop=(k == 7),
                           reads=[w_tk, hT_tok[tt]], writes=[ptok[bk]])
                    TS1("dve", t_ap[:, tt * 512:(tt + 1) * 512], bank(bk), bcol[:, bcolk:bcolk + 1], ALU.add, [ptok[bk], bcol_tok], [t_tks[tt]])

            cur_items = fox_proj_setup(0)
            for tt in range(NTT):
                fox_proj_part(cur_items, tt)
            emit_PA()
            for hp in range(4):
                (_, _, qT, q_tks, _), (_, _, kT, k_tks, _) = cur_items
                nxt_items = fox_proj_setup(hp + 1) if hp + 1 < 4 else None
                for g in range(NTT):
                    i0 = 4 * g
                    nsteps = i0 + 4
                    obs = [obk.next(), obk.next()]
                    firsts = [True, True]
                    pend = {}

                    def emit_S(hh, j):
                        r0 = 64 * hh
                        ilo = max(j, i0)
                        ncol = (i0 + 4 - ilo) * 128
                        sb = sbk.next()
                        MM(bank(sb)[:, 0:ncol], kT[r0:r0 + 64, j * 128:(j + 1) * 128], qT[r0:r0 + 64, ilo * 128:(i0 + 4) * 128],
                           reads=[k_tks[j // 4], q_tks[g]], writes=[ptok[sb]])
                        return sb, ilo

                    def emit_EP(hh, j, info):
                        sb, ilo = info
                        h = 2 * hp + hh
                        ob = obs[hh]
                        obv = bank(ob)[:, 0:260].rearrange("p (a c) -> p a c", a=4, c=65)
                        p_ap, p_tks = PTs.next()
                        for I in (2 * g, 2 * g + 1):
                            i_lo = max(j, 2 * I)
                            if i_lo > 2 * I + 1:
                                continue
                            c0 = (i_lo - ilo) * 128
                            wd = (2 * I + 2 - i_lo) * 128
                            pt = p_tks[I - 2 * g]
                            ACT(p_ap[:, c0:c0 + wd], bank(sb)[:, c0:c0 + wd], AF.Exp, [ptok[sb], fb_tok], [pt],
                                bias=fbT[:, h, j, I:I + 1], scale=0.125)
                            if i_lo == j:
                                TT("dve", p_ap[:, c0:c0 + 128], p_ap[:, c0:c0 + 128], mask16[:], ALU.mult, [pt, tconst], [pt])
                        for i in range(ilo, i0 + 4):
                            c0 = (i - ilo) * 128
                            MM(obv[:, i - i0, :], p_ap[:, c0:c0 + 128], Vaug4[:, j, h, :], start=firsts[hh], stop=(j == i),
                               reads=[p_tks[i // 2 - 2 * g], V_tok[j]], writes=[ptok[ob]], skip=True)
                            firsts[hh] = False

                    LOOK = 1
                    for j in range(nsteps + LOOK):
                        if j < nsteps:
                            for hh in range(2):
                                pend[(hh, j)] = emit_S(hh, j)
                        if j - LOOK >= 0:
                            for hh in range(2):
                                emit_EP(hh, j - LOOK, pend.pop((hh, j - LOOK)))
                    for hh in range(2):
                        h = 2 * hp + hh
                        ob = obs[hh]
                        obv = bank(ob)[:, 0:260].rearrange("p (a c) -> p a c", a=4, c=65)
                        r_ap, r_tk = rinv.next()
                        RECIP(r_ap, obv[:, :, 64], [ptok[ob]], [r_tk])
                        TT("dve", yf_tm[:, i0:i0 + 4, h * 64:(h + 1) * 64], obv[:, :, 0:64],
                           r_ap.unsqueeze(2).to_broadcast([128, 4, 64]), ALU.mult, [ptok[ob], r_tk], [yf_tok[g]])
                    if nxt_items is not None:
                        fox_proj_part(nxt_items, g)
                if nxt_items is not None:
                    cur_items = nxt_items
            for tb in range(NTB):
                bk = pjb.next()
                pv = bank16(bk)
                for cc in range(4):
                    TR(pv[:, cc * 128:(cc + 1) * 128], yf_tm[:, tb, cc * 128:(cc + 1) * 128], ident[:], [yf_tok[tb // 4], tconst], [ptok[bk]])
                CP("act" if tb % 2 else "dve", yfT[:, :, tb * 128:(tb + 1) * 128], pv[:, 0:512].rearrange("p (k t) -> p k t", k=4),
                   [ptok[bk]], [yfT_tok[tb // 4]])
            fw.barrier()
            AR.release(m_mix)
            if stop_after == "fox":
                DMA("sp", dbg_d[:, 0:4, :], yfT, reads=yfT_tok)
                break

            pjb = Rot([6, 7])
            Wvm = AR.alloc([8, 512], BF16)
            Wo = AR.alloc([8, 512], BF16)
            t_wvm, t_wo = Tok(), Tok()
            DMA("pool", Wvm, wblock(w_in_d[l], ML_V, 512), writes=[t_wvm])
            DMA("pool", Wo, wblock(w_in_d[l], ML_O, 512), writes=[t_wo])
            bvm = AR.alloc([512], F32)
            bo = AR.alloc([512], F32)
            gml = AR.alloc([512], F32)
            cw = AR.alloc([8, 4], F32)
            t_small = Tok()
            load_bc(bvm, b_row_d[l, :, ML_V:ML_V + 512], t_small)
            load_bc(bo, b_row_d[l, :, ML_O:ML_O + 512], t_small)
            load_bc(gml, mlg_d[l], t_small)
            DMA("sp", cw, cw_d[l], writes=[t_small])
            qkm = [[(AR.alloc([S], BF16), Tok()) for _ in range(4)] for _ in range(2)]
            zps = Rot([(AR.alloc([S + 8], F32), Tok()) for _ in range(2)])
            accs = Rot([(AR.alloc([S], F32), Tok()) for _ in range(2)])
            Wqk = Rot([(AR.alloc([8, 128], BF16), Tok()) for _ in range(2)])
            for zp_, tz_ in zps.items:
                MEMSET("dve", zp_[:, 0:3], 0.0, writes=[tz_])
            def ml_proj(which, h):
                w_ap, w_tk = Wqk.next()
                c0 = (ML_Q if which == 0 else ML_K) + h * 128
                DMA("pool", w_ap, wblock(w_in_d[l], c0, 128), writes=[w_tk])
                blk = which * 4 + h
                zp, t_zp = zps.next()
                for tt in range(NTT):
                    bk = pjb.next()
                    for k in range(8):
                        MM(bank(bk), w_ap[:, k, :], hT[:, k, tt * 512:(tt + 1) * 512], start=(k == 0), stop=(k == 7),
                           reads=[w_tk, hT_tok[tt]], writes=[ptok[bk]])
                    ACT(zp[:, 3 + tt * 512:3 + (tt + 1) * 512], bank(bk), AF.Identity, [ptok[bk], bcol_tok], [t_zp],
                        bias=bcol[:, 8 + blk:9 + blk], scale=1.0)
                return zp, t_zp, blk

            def ml_conv(which, h, zp, t_zp, blk):
                acc, t_acc = accs.next()
                TS1("dve", acc, zp[:, 0:S], cw[:, blk, 0:1], ALU.mult, [t_zp, t_small], [t_acc])
                for j in range(1, 4):
                    STT("dve", acc, zp[:, j:j + S], cw[:, blk, j:j + 1], acc, ALU.mult, ALU.add, [t_zp, t_small, t_acc], [t_acc])
                d_ap, d_tk = qkm[which][h]
                ACT(d_ap, acc, AF.Silu, [t_acc], [d_tk])

            order = [(w_, h_) for w_ in range(2) for h_ in range(4)]
            nxt_p = ml_proj(*order[0])
            for n_, (w_, h_) in enumerate(order):
                cur_p = nxt_p
                if n_ + 1 < len(order):
                    nxt_p = ml_proj(*order[n_ + 1])
                ml_conv(w_, h_, *cur_p)
            Vm = Rot([(AR.alloc([4, 129], BF16), Tok()) for _ in range(3)])
            for v_ap, v_tk in Vm.items:
                MEMSET("pool", v_ap[:, :, 128:129], 1.0, writes=[v_tk])
            gsig = Rot([(AR.alloc([512], F32), Tok()) for _ in range(6)])
            gs_map = {}
            otmp = Rot([(AR.alloc([512], F32), Tok()) for _ in range(1)])
            ndS = Rot([(AR.alloc([4, 129], F32), Tok()) for _ in range(2)])
            PTm = Rot([(AR.alloc([128], BF16), Tok()) for _ in range(8)])
            K2 = Rot([(AR.alloc([128], BF16), Tok()) for _ in range(8)])
            C32 = [(AR.alloc([129], F32), Tok()) for _ in range(4)]
            Cb = [(AR.alloc([129], BF16), Tok()) for _ in range(4)]
            smr = Rot([(AR.alloc([8, 4], F32), Tok()) for _ in range(4)])
            junkm = AR.alloc([128], BF16)
            tjm = Tok()
            ymc = Rot([(AR.alloc([512], BF16), Tok()) for _ in range(2)])
            sbk = Rot([0, 1])
            tbk = Rot([2])
            ndk = Rot([3, 4])
            dbk = Rot([5])

            def ml_stage1(c):
                cs = slice(c * 128, (c + 1) * 128)
                v_ap, v_tk = Vm.next()
                bk = pjb.next()
                for k in range(8):
                    MM(bank(bk), hT[:, k, cs], Wvm[:, k, :], start=(k == 0), stop=(k == 7), reads=[hT_tok[c // 4], t_wvm], writes=[ptok[bk]])
                TT("dve", v_ap[:, :, 0:128], bank(bk).rearrange("p (h c) -> p h c", h=4, c=128),
                   bvm.rearrange("p (h c) -> p h c", h=4, c=128), ALU.add, [ptok[bk], t_small], [v_tk])
                if c % 4 == 0:
                    for c2 in range(c, min(c + 4, NTB)):
                        cs2 = slice(c2 * 128, (c2 + 1) * 128)
                        bk = pjb.next()
                        for k in range(8):
                            MM(bank(bk), hT[:, k, cs2], Wo[:, k, :], start=(k == 0), stop=(k == 7), reads=[hT_tok[c2 // 4], t_wo], writes=[ptok[bk]])
                        o_ap, o_tk = otmp.next()
                        TT("dve", o_ap, bank(bk), bo, ALU.add, [ptok[bk], t_small], [o_tk])
                        gg_ap, gg_tk = gsig.next()
                        ACT(gg_ap, o_ap, AF.Sigmoid, [o_tk], [gg_tk])
                        TT("dve", gg_ap, gg_ap, gml, ALU.mult, [gg_tk, t_small], [gg_tk])
                        gs_map[c2] = (gg_ap, gg_tk)
                g_ap, g_tk = gs_map.pop(c)
                sb = sbk.next()
                for h in range(4):
                    MM(bank(sb)[:, h * 128:(h + 1) * 128], qkm[1][h][0][:, cs], qkm[0][h][0][:, cs],
                       reads=[qkm[1][h][1], qkm[0][h][1]], writes=[ptok[sb]])
                tb_ = tbk.next()
                for h in range(4):
                    TR(bank16(tb_)[:, h * 128:(h + 1) * 128], qkm[1][h][0][:, cs], ident[:], [qkm[1][h][1], tconst], [ptok[tb_]])
                pts, k2s = [], []
                for h in range(4):
                    p_ap, p_tk = PTm.next()
                    STT("dve", p_ap, bank(sb)[:, h * 128:(h + 1) * 128], apr[:, c, h:h + 1], mask32[:], ALU.mult, ALU.mult,
                        [ptok[sb], ml_tok, tconst], [p_tk])
                    pts.append((p_ap, p_tk))
                for h in range(4):
                    k2_ap, k2_tk = K2.next()
                    ACT(k2_ap, bank16(tb_)[:, h * 128:(h + 1) * 128], AF.Identity, [ptok[tb_], ml_tok], [k2_tk], scale=wsc[:, c, h:h + 1])
                    k2s.append((k2_ap, k2_tk))
                return dict(v=(v_ap, v_tk), g=(g_ap, g_tk), pts=pts, k2s=k2s)

            def ml_stage2(c, st, nxt_holder):
                cs = slice(c * 128, (c + 1) * 128)
                v_ap, v_tk = st["v"]
                g_ap, g_tk = st["g"]
                y_ap, y_tk = ymc.next()
                nds = [ndk.next(), ndk.next()]
                ndv = []
                for h in range(4):
                    nd = nds[h // 2]
                    v_ = bank(nd)[:, (h % 2) * 129:(h % 2) * 129 + 129]
                    ndv.append((nd, v_))
                    MM(v_, st["pts"][h][0], v_ap[:, h, :], start=True, stop=(c == 0), reads=[st["pts"][h][1], v_tk], writes=[ptok[nd]])
                    if c > 0:
                        MM(v_, qkm[0][h][0][:, cs], Cb[h][0], start=False, stop=True, reads=[qkm[0][h][1], Cb[h][1]], writes=[ptok[nd]])
                nS_ap, nS_tk = ndS.next()
                for hp_ in range(2):
                    CP("act", nS_ap[:, 2 * hp_:2 * hp_ + 2, :], bank(nds[hp_])[:, 0:258].rearrange("p (h c) -> p h c", h=2, c=129),
                       [ptok[nds[hp_]]], [nS_tk])
                if c + 1 < NTB:
                    for hp_ in range(2):
                        db = dbk.next()
                        for hh in range(2):
                            h = 2 * hp_ + hh
                            MM(bank(db)[:, hh * 129:hh * 129 + 129], st["k2s"][h][0], v_ap[:, h, :], reads=[st["k2s"][h][1], v_tk], writes=[ptok[db]])
                        for hh in range(2):
                            h = 2 * hp_ + hh
                            c_ap, c_tk = C32[h]
                            dv = bank(db)[:, hh * 129:hh * 129 + 129]
                            if c == 0:
                                CP("dve", c_ap, dv, [ptok[db]], [c_tk])
                            else:
                                STT("dve", c_ap, c_ap, ebt[:, c, h:h + 1], dv, ALU.mult, ALU.add, [c_tk, ml_tok, ptok[db]], [c_tk])
                        for hh in range(2):
                            h = 2 * hp_ + hh
                            CP("pool", Cb[h][0], C32[h][0], [C32[h][1]], [Cb[h][1]])
                while ml_pending:
                    ml_pending.pop(0)()
                s_ap, s_tk = smr.next()
                STT("dve", s_ap[:, 1, :], nS_ap[:, :, 128], -1.0, nS_ap[:, :, 128], ALU.mult, ALU.max, [nS_tk], [s_tk])
                TT("dve", s_ap[:, 2, :], s_ap[:, 1, :], emb[:, c, :], ALU.max, [s_tk, ml_tok], [s_tk])
                RECIP(s_ap[:, 3, :], s_ap[:, 2, :], [s_tk], [s_tk])
                for h in range(4):
                    ACT(junkm, nS_ap[:, h, 0:128], AF.Square, [nS_tk, s_tk], [tjm, s_tk], scale=s_ap[:, 3, h:h + 1], accum=s_ap[:, 4, h:h + 1])
                if c + 1 < NTB:
                    nxt_holder.append(ml_stage1(c + 1))
                ACT(s_ap[:, 5, :], s_ap[:, 4, :], AF.Ln, [s_tk], [s_tk], bias=eps_c, scale=1.0 / 128)
                ACT(s_ap[:, 6, :], s_ap[:, 5, :], AF.Exp, [s_tk], [s_tk], scale=-0.5)
                TT("dve", s_ap[:, 7, :], s_ap[:, 3, :], s_ap[:, 6, :], ALU.mult, [s_tk], [s_tk])
                for h in range(4):
                    STT("dve", y_ap[:, h * 128:(h + 1) * 128], nS_ap[:, h, 0:128], s_ap[:, 7, h:h + 1], g_ap[:, h * 128:(h + 1) * 128],
                        ALU.mult, ALU.mult, [nS_tk, s_tk, g_tk], [y_tk])
                def finish():
                    bk = pjb.next()
                    pv = bank16(bk)
                    for cc in range(4):
                        TR(pv[:, cc * 128:(cc + 1) * 128], y_ap[:, cc * 128:(cc + 1) * 128], ident[:], [y_tk, tconst], [ptok[bk]])
                    CP("act", ymT[:, :, cs], pv[:, 0:512].rearrange("p (k t) -> p k t", k=4), [ptok[bk]], [ymT_tok[c // 4]])
                return finish

            ml_pending = []
            st_cur = ml_stage1(0)
            for c in range(NTB):
                holder = []
                ml_pending.append(ml_stage2(c, st_cur, holder))
                if holder:
                    st_cur = holder[0]
            for f in ml_pending:
                f()
            fw.barrier()
            AR.release(m_mix)
            if stop_after == "mlstm":
                DMA("sp", dbg_d[:, 0:4, :], ymT, reads=ymT_tok)
                break

            mgT = AR.alloc([8, S], BF16)
            mg_tok = toks(NTT)
            m_mg = AR.mark()
            Wu = AR.alloc([8, 512], BF16)
            Wgv = AR.alloc([8, 512], BF16)
            t_wu, t_wgv = Tok(), Tok()
            DMA("pool", Wu, wblock(w_in_d[l], G_U, 512), writes=[t_wu])
            DMA("pool", Wgv, wblock(w_in_d[l], G_V, 512), writes=[t_wgv])
            bgv = AR.alloc([512], F32)
            ggb = AR.alloc([512], F32)
            ws32 = AR.alloc([4, 128], F32)
            ws16 = AR.alloc([4, 128], BF16)
            bs16 = AR.alloc([512], BF16)
            t_g = Tok()
            load_bc(bgv, b_row_d[l, :, G_V:G_V + 512], t_g)
            load_bc(ggb, gg_d[l], t_g)
            DMA("sp", ws32, wsT_d[l], writes=[t_g])
            DMA("pool", bs16[0:1, :], bs_d[l], writes=[t_g])
            MEMSET("dve", ws32[64:128, :, 0:64], 0.0, [t_g], [t_g])
            CP("dve", ws16, ws32, [t_g], [t_g])
            uT = Rot([(AR.alloc([4, 512], BF16), Tok()) for _ in range(2)])
            vtmp = Rot([(AR.alloc([512], F32), Tok()) for _ in range(8)])
            vn = Rot([(AR.alloc([512], BF16), Tok()) for _ in range(4)])
            smr = Rot([(AR.alloc([8, 4], F32), Tok()) for _ in range(3)])
            junkg = AR.alloc([512], BF16)
            tjg = Tok()
            mxb = Rot([0, 1, 2, 3])
            gpb = Rot([4, 5, 6, 7])
            def gm_front(tt):
                u_ap, u_tk = uT.next()
                for g in range(4):
                    bk = gpb.next()
                    for k in range(8):
                        MM(bank(bk), Wu[:, k, g * 128:(g + 1) * 128], hT[:, k, tt * 512:(tt + 1) * 512], start=(k == 0), stop=(k == 7),
                           reads=[t_wu, hT_tok[tt]], writes=[ptok[bk]])
                    ACT(u_ap[:, g, :], bank(bk), AF.Gelu, [ptok[bk], bcol_tok], [u_tk], bias=bcol[:, 16 + g:17 + g], scale=1.0)
                s_ap, s_tk = smr.next()
                vts = []
                for sp_ in range(4):
                    tb = tt * 4 + sp_
                    bk = gpb.next()
                    for k in range(8):
                        MM(bank(bk), hT[:, k, tb * 128:(tb + 1) * 128], Wgv[:, k, :], start=(k == 0), stop=(k == 7),
                           reads=[hT_tok[tt], t_wgv], writes=[ptok[bk]])
                    v_ap, v_tk = vtmp.next()
                    TT("dve", v_ap, bank(bk), bgv, ALU.add, [ptok[bk], t_g], [v_tk])
                    vts.append((v_ap, v_tk))
                for sp_ in range(4):
                    v_ap, v_tk = vts[sp_]
                    ACT(v_ap, v_ap, AF.Gelu, [v_tk], [v_tk, s_tk], accum=s_ap[:, 0, sp_:sp_ + 1])
                for sp_ in range(4):
                    v_ap, v_tk = vts[sp_]
                    ACT(junkg, v_ap, AF.Square, [v_tk], [tjg, s_tk], accum=s_ap[:, 1, sp_:sp_ + 1])
                return u_ap, u_tk, s_ap, s_tk, vts

            def gm_back(tt, u_ap, u_tk, s_ap, s_tk, vts):
                TS1("dve", s_ap[:, 2, :], s_ap[:, 0, :], 1.0 / 512, ALU.mult, [s_tk], [s_tk])
                TT("dve", s_ap[:, 3, :], s_ap[:, 2, :], s_ap[:, 2, :], ALU.mult, [s_tk], [s_tk])
                STT("dve", s_ap[:, 4, :], s_ap[:, 1, :], 1.0 / 512, s_ap[:, 3, :], ALU.mult, ALU.subtract, [s_tk], [s_tk])
                ACT(s_ap[:, 5, :], s_ap[:, 4, :], AF.Ln, [s_tk], [s_tk], bias=eps_c, scale=1.0)
                ACT(s_ap[:, 6, :], s_ap[:, 5, :], AF.Exp, [s_tk], [s_tk], scale=-0.5)
                vns = []
                for sp_ in range(4):
                    v_ap, v_tk = vts[sp_]
                    TS("dve", v_ap, v_ap, s_ap[:, 2, sp_:sp_ + 1], s_ap[:, 6, sp_:sp_ + 1], ALU.subtract, ALU.mult, [v_tk, s_tk], [v_tk])
                    n_ap, n_tk = vn.next()
                    TT("dve", n_ap, v_ap, ggb, ALU.mult, [v_tk, t_g], [n_tk])
                    vns.append((n_ap, n_tk))
                mbs = []
                for sp_ in range(4):
                    n_ap, n_tk = vns[sp_]
                    mb = mxb.next()
                    for g in range(4):
                        MM(bank(mb)[:, g * 128:(g + 1) * 128], n_ap[:, g * 128:(g + 1) * 128], ws16[:, g, :], start=(g == 0), stop=False,
                           reads=[n_tk, t_g], writes=[ptok[mb]], skip=True)
                        MM(bank(mb)[:, g * 128:(g + 1) * 128], ones16r[:], bs16[0:1, g * 128:(g + 1) * 128], start=False, stop=True,
                           reads=[t_g, tconst], writes=[ptok[mb]], skip=True)
                    mbs.append(mb)
                for sp_ in range(4):
                    tb = tt * 4 + sp_
                    mb = mbs[sp_]
                    TT("dve", ygT[:, :, tb * 128:(tb + 1) * 128], bank(mb).rearrange("p (g t) -> p g t", g=4),
                       u_ap[:, :, sp_ * 128:(sp_ + 1) * 128], ALU.mult, [ptok[mb], u_tk], [ygT_tok[tt]])

            gm_next = gm_front(0)
            for tt in range(NTT):
                gm_cur = gm_next
                if tt + 1 < NTT:
                    gm_next = gm_front(tt + 1)
                gm_back(tt, *gm_cur)
            if stop_after == "gmlp":
                DMA("sp", dbg_d[:, 0:4, :], ygT, reads=ygT_tok)
                break

            Wg = Rot([(AR.alloc([8, 128], BF16), Tok()) for _ in range(3)])
            Wb = Rot([(AR.alloc([4, 128], BF16), Tok()) for _ in range(3)])
            sgt = Rot([(AR.alloc([512], F32), Tok()) for _ in range(3)])
            macc = AR.alloc([NTT, 512], F32)
            macc_tok = toks(NTT)
            ysrc = [(yfT, yfT_tok), (ymT, ymT_tok), (ygT, ygT_tok)]
            gbk = Rot([0, 1, 2])
            bbk = Rot([3, 4, 5])
            for dc in range(8):
                for n in range(3):
                    wg_ap, wg_tk = Wg.next()
                    DMA("pool", wg_ap, wblock(w_in_d[l], GATE + n * 1024 + dc * 128, 128), writes=[wg_tk])
                    wb_ap, wb_tk = Wb.next()
                    DMA("pool", wb_ap, wbr_d[l, n, :, dc * 128:(dc + 1) * 128].rearrange("(kc p) c -> p kc c", p=128), writes=[wb_tk])
                    yT, y_tks = ysrc[n]
                    for tt in range(NTT):
                        ts_ = slice(tt * 512, (tt + 1) * 512)
                        gb = gbk.next()
                        for k in range(8):
                            MM(bank(gb), wg_ap[:, k, :], hT[:, k, ts_], start=(k == 0), stop=(k == 7), reads=[wg_tk, hT_tok[tt]], writes=[ptok[gb]])
                        s_ap, s_tk = sgt.next()
                        ACT(s_ap, bank(gb), AF.Sigmoid, [ptok[gb], bcol_tok], [s_tk], bias=bcol[:, 20 + n * 8 + dc:21 + n * 8 + dc], scale=1.0)
                        bb = bbk.next()
                        for k in range(4):
                            MM(bank(bb), wb_ap[:, k, :], yT[:, k, ts_], start=(k == 0), stop=(k == 3), reads=[wb_tk, y_tks[tt]], writes=[ptok[bb]])
                        if n == 0:
                            TT("dve", macc[:, tt, :], s_ap, bank(bb), ALU.mult, [s_tk, ptok[bb]], [macc_tok[tt]])
                        else:
                            TT("dve", s_ap, s_ap, bank(bb), ALU.mult, [s_tk, ptok[bb]], [s_tk])
                            if n == 1:
                                TT("dve", macc[:, tt, :], macc[:, tt, :], s_ap, ALU.add, [s_tk, macc_tok[tt]], [macc_tok[tt]])
                            else:
                                TT("dve", mgT[:, dc, ts_], macc[:, tt, :], s_ap, ALU.add, [s_tk, macc_tok[tt]], [mg_tok[tt]])
            fw.barrier()
            AR.release(m_mg)
            if stop_after == "merge":
                DMA("sp", dbg_d[:, :, :], mgT, reads=mg_tok)
                break

            Wout = AR.alloc([8, 1024], BF16)
            w_toks = toks(8)
            for k in range(8):
                DMA("pool", Wout[:, k, :], wout_d[l, k * 128:(k + 1) * 128, :], writes=[w_toks[k]])
            g_bc = AR.alloc([1024], F32)
            g_tok = Tok()
            load_bc(g_bc, norms_d[l, 1:2, :], g_tok)
            Wxq = AR.alloc_top([8, 1024], BF16)
            Wxo = AR.alloc_top([8, 1024], BF16)
            wq_toks, wo_toks = toks(8), toks(8)
            for k in range(8):
                DMA("pool", Wxq[:, k, :], wxq_d[l, k * 128:(k + 1) * 128, :], writes=[wq_toks[k]])
            for k in range(8):
                DMA("pool", Wxo[:, k, :], wxo_d[l, k * 128:(k + 1) * 128, :], writes=[wo_toks[k]])
            out_proj(mgT, lambda tb: mg_tok[tb // 4], 8, Wout, w_toks, g_bc, g_tok, xsrc, sq)
            fw.barrier()
            if stop_after == "mixer":
                break

            AR.release(0)
            memT = AR.alloc([8, MEM], BF16)
            memT_tok = toks(2)
            KT = AR.alloc([8, MEM], BF16)
            KT_tok = Tok()
            Vx = AR.alloc([2, 4 * 257], BF16)
            Vx4 = Vx.rearrange("p a (h c) -> p a h c", h=4, c=257)
            Vx_tok = toks(2)
            gpre = AR.alloc([1024], F32)
            gpost = AR.alloc([1024], F32)
            gmem = AR.alloc([1024], F32)
            t_gx = Tok()
            load_bc(gpre, norms_d[l, 2:3, :], t_gx)
            load_bc(gpost, norms_d[l, 3:4, :], t_gx)
            load_bc(gmem, norms_d[l, 4:5, :], t_gx)
            m_x = AR.mark()
            mem_toks = toks(2)
            nrm = NormT(nx=2, nh=4)
            nrm.run(lambda b: mem_d[sq, b * 128:(b + 1) * 128, :], mem_toks, 2, gmem, t_gx, memT, lambda b: memT_tok[b])
            Wk = Rot([(AR.alloc([8, 128], BF16), Tok()) for _ in range(2)])
            for blk in range(8):
                w_ap, w_tk = Wk.next()
                DMA("pool", w_ap, wblock(wxkv_d[l], blk * 128, 128), writes=[w_tk])
                bk = pjb.next()
                for k in range(8):
                    MM(bank(bk)[:, 0:MEM], w_ap[:, k, :], memT[:, k, :], start=(k == 0), stop=(k == 7), reads=[w_tk] + memT_tok, writes=[ptok[bk]])
                CP("act", KT[:, blk, :], bank(bk)[:, 0:MEM], [ptok[bk]], [KT_tok])
            Wvx = AR.alloc([8, 1024], BF16)
            wv_toks = toks(8)
            for k in range(8):
                DMA("pool", Wvx[:, k, :], wxkv_d[l, k * 128:(k + 1) * 128, D:2 * D], writes=[wv_toks[k]])
            for mb in range(2):
                MEMSET("pool", Vx4[:, mb, :, 256:257], 1.0, writes=[Vx_tok[mb]])
                for half in range(2):
                    bk = pjb.next()
                    for k in range(8):
                        MM(bank(bk), memT[:, k, mb * 128:(mb + 1) * 128], Wvx[:, k, half * 512:(half + 1) * 512], start=(k == 0), stop=(k == 7),
                           reads=[memT_tok[mb], wv_toks[k]], writes=[ptok[bk]])
                    CP("dve", Vx4[:, mb, 2 * half:2 * half + 2, 0:256], bank(bk).rearrange("p (h c) -> p h c", h=2, c=256), [ptok[bk]], [Vx_tok[mb]])
            hx = Rot([(AR.alloc([8, 512], BF16), toks(1)) for _ in range(2)])
            qx = Rot([(AR.alloc([8, 512], BF16), Tok()) for _ in range(2)])
            PTx = Rot([(AR.alloc([512], BF16), Tok()) for _ in range(4)])
            o_tm = Rot([(AR.alloc([4, 1024], BF16), Tok()) for _ in range(2)])
            oT = Rot([(AR.alloc([8, 512], BF16), Tok()) for _ in range(2)])
            rx = Rot([(AR.alloc([4], F32), Tok()) for _ in range(4)])
            pr = PostRes(gpost, t_gx, nxt=0, nt1=1)
            xs_pool = Rot([(AR.alloc([1024], F32), Tok()) for _ in range(8)])
            xs_of = {}
            sxb = Rot([0, 1, 2, 3])
            oxb = Rot([4, 5, 6, 7])
            pjb = sxb
            def xa_stageA1(tt):
                res = []
                for b in range(4):
                    xb = xs_pool.next()
                    xs_of[tt * 4 + b] = xb
                    res.append(nrm.part1(xrows(out_d, sq, tt * 4 + b), xtok[sq][tt * 4 + b], gpre, t_gx, xbuf=xb))
                return res

            def xa_stageA2(tt, hns):
                h_ap, h_tks = hx.next()
                for b in range(4):
                    nrm.part2(hns[b][0], hns[b][1], h_ap, h_tks[0], b * 128, alt=b)
                q_ap, q_tk = qx.next()
                for blk in range(8):
                    bk = pjb.next()
                    for k in range(8):
                        MM(bank(bk), Wxq[:, k, blk * 128:(blk + 1) * 128], h_ap[:, k, :], start=(k == 0), stop=(k == 7),
                           reads=[wq_toks[k], h_tks[0]], writes=[ptok[bk]])
                    CP("dve", q_ap[:, blk, :], bank(bk), [ptok[bk]], [q_tk])
                return q_ap, q_tk

            def xa_stageBC(tt, q_ap, q_tk, mid_hook):
                ot_ap, ot_tk = o_tm.next()

                def x_scores(h):
                    pts = []
                    for mb in range(2):
                        sb = sxb.next()
                        for cc in range(2):
                            MM(bank(sb), KT[:, 2 * h + cc, mb * 128:(mb + 1) * 128], q_ap[:, 2 * h + cc, :], start=(cc == 0), stop=(cc == 1),
                               reads=[KT_tok, q_tk], writes=[ptok[sb]])
                        p_ap, p_tk = PTx.next()
                        ACT(p_ap, bank(sb), AF.Exp, [ptok[sb]], [p_tk], scale=1.0 / 16.0)
                        pts.append((p_ap, p_tk))
                    return pts

                nxt_pts = x_scores(0)
                for h in range(4):
                    pts = nxt_pts
                    if h + 1 < 4:
                        nxt_pts = x_scores(h + 1)
                    for qb in range(4):
                        ob = oxb.next()
                        ov = bank(ob)[:, 0:257]
                        for mb in range(2):
                            MM(ov, pts[mb][0][:, qb * 128:(qb + 1) * 128], Vx4[:, mb, h, :], start=(mb == 0), stop=(mb == 1),
                               reads=[pts[mb][1], Vx_tok[mb]], writes=[ptok[ob]])
                        r_ap, r_tk = rx.next()
                        RECIP(r_ap[:, 0:1], bank(ob)[:, 256:257], [ptok[ob]], [r_tk])
                        TS1("dve", ot_ap[:, qb, h * 256:(h + 1) * 256], bank(ob)[:, 0:256], r_ap[:, 0:1], ALU.mult, [ptok[ob], r_tk], [ot_tk])
                oT_ap, oT_tk = oT.next()
                for qb in range(4):
                    bk = pjb.next()
                    pv = bank16(bk)
                    for cc in range(8):
                        TR(pv[:, cc * 128:(cc + 1) * 128], ot_ap[:, qb, cc * 128:(cc + 1) * 128], ident[:], [ot_tk, tconst], [ptok[bk]])
                    CP("dve", oT_ap[:, :, qb * 128:(qb + 1) * 128], pv.rearrange("p (k t) -> p k t", k=8), [ptok[bk]], [oT_tk])
                mid_hook()
                for qb in range(4):
                    tb = tt * 4 + qb
                    pp = qb % 2
                    for half in range(2):
                        bk = 2 * pp + half
                        for k in range(8):
                            MM(bank(bk), oT_ap[:, k, qb * 128:(qb + 1) * 128], Wxo[:, k, half * 512:(half + 1) * 512], start=(k == 0), stop=(k == 7),
                               reads=[oT_tk, wo_toks[k]], writes=[ptok[bk]])
                    pr.run(pp, sq, tb, xbuf=xs_of.pop(tb))

            xa_state = {"next": xa_stageA2(0, xa_stageA1(0))}
            for tt in range(NTT):
                xa_cur = xa_state["next"]
                hns_ = xa_stageA1(tt + 1) if tt + 1 < NTT else None

                def mid(tt=tt, hns_=hns_):
                    if hns_ is not None:
                        xa_state["next"] = xa_stageA2(tt + 1, hns_)
                xa_stageBC(tt, xa_cur[0], xa_cur[1], mid)
            fw.barrier()
            AR.free_top(AW)
            pjb = Rot([6, 7])
            if stop_after == "xattn":
                break

            AR.release(0)
            GT = min(1024, S)
            NG = S // GT
            NGB = GT // 128
            W2 = AR.alloc([32, 1024], BF16)
            w2_toks = toks(32)
            gpre = AR.alloc([1024], F32)
            gpost = AR.alloc([1024], F32)
            t_gf = Tok()
            load_bc(gpre, norms_d[l, 5:6, :], t_gf)
            load_bc(gpost, norms_d[l, 6:7, :], t_gf)
            hF = AR.alloc([8, GT], BF16)
            hF_tok = toks(GT // 512)
            aT = AR.alloc([32, GT], BF16)
            aT_tok = toks(32)
            W1 = Rot([(AR.alloc([8, 256], BF16), Tok()) for _ in range(2)])
            rl = Rot([(AR.alloc([512], F32), Tok()) for _ in range(2)])
            pr = PostRes(gpost, t_gf, nxt=3, nt1=1)
            pr.plan(out_d, sq, range(NTB))
            nrm = NormT(pb=(4, 5))
            fbk = Rot([0, 1, 2, 3])
            def ffn_norm(gi):
                t0_ = gi * NGB
                nrm.run(lambda b: xrows(out_d, sq, t0_ + b), xtok[sq][t0_:t0_ + NGB], NGB, gpre, t_gf, hF, lambda b: hF_tok[b // 4])

            ffn_norm(0)
            for gi in range(NG):
                t0 = gi * NGB
                for fc2 in range(16):
                    w_ap, w_tk = W1.next()
                    DMA("pool", w_ap, wblock(wff1_d[l], fc2 * 256, 256), writes=[w_tk])
                    if gi == 0 and fc2 >= 1:
                        for k in ((2 * (fc2 - 1), 2 * (fc2 - 1) + 1) if fc2 < 15 else (28, 29, 30, 31)):
                            DMA("pool", W2[:, k, :], wff2_d[l, k * 128:(k + 1) * 128, :], writes=[w2_toks[k]])
                    for sub in range(2):
                        fc = 2 * fc2 + sub
                        for tt in range(GT // 512):
                            bk = fbk.next()
                            for k in range(8):
                                MM(bank(bk), w_ap[:, k, sub * 128:(sub + 1) * 128], hF[:, k, tt * 512:(tt + 1) * 512], start=(k == 0), stop=(k == 7),
                                   reads=[w_tk, hF_tok[tt]], writes=[ptok[bk]])
                            r_ap, r_tk = rl.next()
                            ACT(r_ap, bank(bk), AF.Relu, [ptok[bk]], [r_tk])
                            TT("dve", aT[:, fc, tt * 512:(tt + 1) * 512], r_ap, r_ap, ALU.mult, [r_tk], [aT_tok[fc]])
                for b in range(NGB):
                    tb = t0 + b
                    pp = 2 + (b % 2)
                    if gi + 1 < NG:
                        t1_ = (gi + 1) * NGB + b
                        nh = nrm.part1(xrows(out_d, sq, t1_), xtok[sq][t1_], gpre, t_gf)
                    for half in range(2):
                        bk = 2 * pp + half
                        for k in range(32):
                            MM(bank(bk), aT[:, k, b * 128:(b + 1) * 128], W2[:, k, half * 512:(half + 1) * 512], start=(k == 0), stop=(k == 31),
                               reads=[aT_tok[k], w2_toks[k]], writes=[ptok[bk]])
                    pr.run(pp, sq, tb)
                    if gi + 1 < NG:
                        nrm.part2(nh[0], nh[1], hF, hF_tok[b // 4], b * 128, alt=b)
            fw.barrier()
        else:
            continue
        break

    fw.barrier()
    stats = fw.emit()
    fw.es.close()
    return nc, stats


def _host_layout(inp, depth=DEPTH):
    w_in = np.ascontiguousarray(inp["w_in"][:depth], dtype=np.float32)
    b_in = np.ascontiguousarray(inp["b_in"][:depth], dtype=np.float32)
    small_cols = list(range(FOX_F, FOX_F + 8)) + list(range(ML_I, ML_I + 4)) + list(range(ML_F, ML_F + 4))
    w_small = np.ascontiguousarray(w_in[:, :, small_cols])
    b_small = np.ascontiguousarray(b_in[:, None, small_cols])
    starts = ([FOX_Q + 128 * i for i in range(4)] + [FOX_K + 128 * i for i in range(4)] +
              [ML_Q + 128 * i for i in range(4)] + [ML_K + 128 * i for i in range(4)] +
              [G_U + 128 * i for i in range(4)] + [GATE + 128 * i for i in range(24)])
    b_col = np.stack([np.stack([b_in[l, s:s + 128] for s in starts], axis=1) for l in range(depth)], axis=0)
    cw = inp["conv_w"][:depth]
    conv_col = np.ascontiguousarray(cw.reshape(depth, 4, 8, 128).transpose(0, 3, 2, 1))
    wsT = np.ascontiguousarray(inp["gmlp_ws"][:depth].transpose(0, 3, 1, 2))
    d = {
        "norms": inp["norms"][:depth], "w_in": w_in, "b_in": b_in[:, None, :], "w_small": w_small, "b_small": b_small,
        "b_col": np.ascontiguousarray(b_col), "conv_col": conv_col,
        "mlstm_norm": inp["mlstm_norm"][:depth, None, :], "gmlp_norm": inp["gmlp_norm"][:depth, None, :],
        "gmlp_wsT": wsT, "gmlp_bs": inp["gmlp_bs"][:depth].reshape(depth, 1, 512),
        "w_branch": inp["w_branch"][:depth], "w_out": inp["w_out"][:depth], "w_xq": inp["w_xq"][:depth],
        "w_xkv": inp["w_xkv"][:depth], "w_xo": inp["w_xo"][:depth], "w_ff1": inp["w_ff1"][:depth], "w_ff2": inp["w_ff2"][:depth],
    }
    return {k: np.ascontiguousarray(v, dtype=np.float32) for k, v in d.items()}


_CACHE = {}


def kernel(**inputs):
    x = np.asarray(inputs["x"], dtype=np.float32)
    mem = np.asarray(inputs["mem"], dtype=np.float32)
    B, S, _ = x.shape
    n_cores = 8
    per = B // n_cores
    key = (S, per)
    if key not in _CACHE:
        _CACHE[key] = build_program(S, DEPTH, per)[0]
    nc = _CACHE[key]
    params = _host_layout(inputs)
    in_maps = []
    for c in range(n_cores):
        m = dict(params)
        m["x"] = np.ascontiguousarray(x[c * per:(c + 1) * per])
        m["mem"] = np.ascontiguousarray(mem[c * per:(c + 1) * per])
        in_maps.append(m)
    res = run_bass_kernel_spmd(nc, in_maps, core_ids=list(range(n_cores)))
    out = np.concatenate([np.asarray(r["out"]) for r in res.results], axis=0)
    return out.astype(np.float32)
```

```python
import math
import numpy as np
import concourse.bass as bass
import concourse.mybir as mybir
from concourse.bass_utils import run_bass_kernel_spmd
from contextlib import ExitStack

F32 = mybir.dt.float32
BF16 = mybir.dt.bfloat16
AF = mybir.ActivationFunctionType
ALU = mybir.AluOpType

ENGS = ["pe", "act", "dve", "pool", "sp"]

D = 1024
DEPTH = 2
NSEQ = 2
MEM = 256
D_IN = 7696
FOX_Q, FOX_K, FOX_V, FOX_F = 0, 512, 1024, 1536
ML_Q, ML_K, ML_V, ML_I, ML_F, ML_O = 1544, 2056, 2568, 3080, 3084, 3088
G_U, G_V, GATE = 3600, 4112, 4624
EPS = 1e-6


class Tok:
    __slots__ = ("w", "r")

    def __init__(self):
        self.w = None
        self.r = []


def toks(n):
    return [Tok() for _ in range(n)]


class FW:
    def __init__(self, nc, n_dma_sems=32):
        self.nc = nc
        self.ins = {e: [] for e in ENGS}
        self.n_dma_sems = n_dma_sems
        self.dma_count = [0] * n_dma_sems
        self.dma_rr = 0
        self.es = ExitStack()
        self.uid = 0

    def sbuf(self, shape, dtype):
        self.uid += 1
        return self.es.enter_context(self.nc.sbuf_tensor(f"sb{self.uid}", list(shape), dtype))

    def psum(self, shape, dtype):
        self.uid += 1
        return self.es.enter_context(self.nc.psum_tensor(f"ps{self.uid}", list(shape), dtype))

    def _deps(self, reads, writes):
        deps = set()
        for t in reads:
            if t.w is not None:
                deps.add(t.w)
        for t in writes:
            if t.w is not None:
                deps.add(t.w)
            deps.update(t.r)
        return deps

    def _mark(self, me, reads, writes):
        for t in reads:
            if me[0] != "dmasem":
                t.r = [x for x in t.r if x[0] != me[0]]
            t.r.append(me)
        for t in writes:
            t.w = me
            t.r = []

    def op(self, eng, fn, reads=(), writes=()):
        idx = len(self.ins[eng])
        deps = self._deps(reads, writes)
        me = (eng, idx)
        self.ins[eng].append(dict(fn=fn, deps=deps, dma=None, sig=False))
        self._mark(me, reads, writes)
        return me

    def dma(self, eng, fn, reads=(), writes=()):
        half = self.n_dma_sems // 2
        if not hasattr(self, "dma_rr2"):
            self.dma_rr2 = {"sp": 0, "pool": 0, "act": 0}
        base = half if eng == "pool" else 0
        s = base + self.dma_rr2[eng] % half
        self.dma_rr2[eng] += 1
        deps = self._deps(reads, writes)
        if self.dma_count[s] > 0:
            deps.add(("dmasem", s, 16 * self.dma_count[s]))
        self.dma_count[s] += 1
        me = ("dmasem", s, 16 * self.dma_count[s])
        self.ins[eng].append(dict(fn=fn, deps=deps, dma=s, sig=False))
        self._mark(me, reads, writes)
        return me

    def barrier(self):
        deps = set()
        for e in ENGS:
            for i in range(len(self.ins[e]) - 1, -1, -1):
                if self.ins[e][i]["dma"] is None:
                    deps.add((e, i))
                    break
        for s in range(self.n_dma_sems):
            if self.dma_count[s] > 0:
                deps.add(("dmasem", s, 16 * self.dma_count[s]))
        for e in ENGS:
            self.ins[e].append(dict(fn=None, deps=set(deps), dma=None, sig=False))

    def emit(self):
        nc = self.nc
        for e in ENGS:
            for rec in self.ins[e]:
                for d in rec["deps"]:
                    if d[0] != "dmasem" and not (d[0] == "pe" and e == "pe"):
                        self.ins[d[0]][d[1]]["sig"] = True
        for e in ENGS:
            c = 0
            for rec in self.ins[e]:
                if rec["sig"]:
                    c += 1
                rec["cnt"] = c
        es = self.es
        esem = {e: es.enter_context(nc.semaphore(f"sem_{e}")) for e in ENGS}
        dsem = [es.enter_context(nc.semaphore(f"sem_dma{i}")) for i in range(self.n_dma_sems)]
        stats = {e: [0, 0] for e in ENGS}

        def replay(ename, eng):
            seen = {}
            for rec in self.ins[ename]:
                need = {}
                for d in rec["deps"]:
                    if d[0] == "dmasem":
                        key = ("d", d[1])
                        val = d[2]
                    else:
                        if d[0] == "pe" and ename == "pe":
                            continue
                        key = ("e", d[0])
                        val = self.ins[d[0]][d[1]]["cnt"]
                    if val > need.get(key, 0):
                        need[key] = val
                for key, val in need.items():
                    if seen.get(key, 0) >= val:
                        continue
                    seen[key] = val
                    sem = dsem[key[1]] if key[0] == "d" else esem[key[1]]
                    eng.wait_ge(sem, val)
                    stats[ename][1] += 1
                fn = rec["fn"]
                if fn is None:
                    if not rec["sig"]:
                        continue
                    bi = eng.nop()
                else:
                    bi = fn(eng)
                stats[ename][0] += 1
                if rec["dma"] is not None:
                    bi.then_inc(dsem[rec["dma"]], 16)
                elif rec["sig"]:
                    bi.then_inc(esem[ename], 1)

        with nc.Block() as block:
            @block.tensor
            def _(eng):
                replay("pe", eng)

            @block.scalar
            def _(eng):
                replay("act", eng)

            @block.vector
            def _(eng):
                replay("dve", eng)

            @block.gpsimd
            def _(eng):
                replay("pool", eng)

            @block.sync
            def _(eng):
                replay("sp", eng)
        return stats


class Arena:
    def __init__(self, ap, nwords):
        self.ap = ap
        self.n = nwords
        self.off = 0

    def alloc(self, shape, dtype):
        n = 1
        for s in shape:
            n *= s
        words = n if dtype == F32 else (n + 1) // 2
        words = (words + 7) // 8 * 8
        assert self.off + words <= self.n, f"arena overflow {self.off}+{words}>{self.n}"
        sl = self.ap[:, self.off:self.off + words]
        self.off += words
        if dtype != F32:
            sl = sl.bitcast(dtype)
        sl = sl[:, 0:n]
        if len(shape) == 2:
            return sl.rearrange("p (a b) -> p a b", a=shape[0], b=shape[1])
        if len(shape) == 3:
            return sl.rearrange("p (a b c) -> p a b c", a=shape[0], b=shape[1], c=shape[2])
        return sl

    def alloc_top(self, shape, dtype):
        n = 1
        for s_ in shape:
            n *= s_
        words = n if dtype == F32 else (n + 1) // 2
        words = (words + 7) // 8 * 8
        assert self.off + words <= self.n, "arena overflow (top)"
        self.n -= words
        sl = self.ap[:, self.n:self.n + words]
        if dtype != F32:
            sl = sl.bitcast(dtype)
        sl = sl[:, 0:n]
        if len(shape) == 2:
            return sl.rearrange("p (a b) -> p a b", a=shape[0], b=shape[1])
        return sl

    def free_top(self, total):
        self.n = total

    def mark(self):
        return self.off

    def release(self, m):
        self.off = m


class Rot:
    def __init__(self, items):
        self.items = items
        self.i = 0

    def next(self):
        it = self.items[self.i % len(self.items)]
        self.i += 1
        return it


def build_program(S, depth=DEPTH, nseq=NSEQ, stop_after=None):
    NTB = S // 128
    NTT = S // 512
    nc = bass.Bass("TRN2", target_bir_lowering=False)
    fw = FW(nc)

    def din(name, shape):
        return nc.dram_tensor(name, list(shape), F32, kind="ExternalInput").ap()

    x_d = din("x", [nseq, S, D])
    mem_d = din("mem", [nseq, MEM, D])
    out_d = nc.dram_tensor("out", [nseq, S, D], F32, kind="ExternalOutput").ap()
    dbg_d = nc.dram_tensor("dbg", [128, 8, S], BF16, kind="ExternalOutput").ap() if stop_after else None
    norms_d = din("norms", [depth, 7, D])
    w_in_d = din("w_in", [depth, D, D_IN])
    b_row_d = din("b_in", [depth, 1, D_IN])
    wsm_d = din("w_small", [depth, D, 16])
    bsm_d = din("b_small", [depth, 1, 16])
    bcol_d = din("b_col", [depth, 128, 44])
    cw_d = din("conv_col", [depth, 128, 8, 4])
    mlg_d = din("mlstm_norm", [depth, 1, 512])
    gg_d = din("gmlp_norm", [depth, 1, 512])
    wsT_d = din("gmlp_wsT", [depth, 128, 4, 128])
    bs_d = din("gmlp_bs", [depth, 1, 512])
    wbr_d = din("w_branch", [depth, 3, 512, D])
    wout_d = din("w_out", [depth, D, D])
    wxq_d = din("w_xq", [depth, D, D])
    wxkv_d = din("w_xkv", [depth, D, 2 * D])
    wxo_d = din("w_xo", [depth, D, D])
    wff1_d = din("w_ff1", [depth, D, 4 * D])
    wff2_d = din("w_ff2", [depth, 4 * D, D])

    def MM(out, lhsT, rhs, start=True, stop=True, reads=(), writes=(), skip=False):
        fw.op("pe", lambda e: e.matmul(out, lhsT, rhs, start=start, stop=stop, skip_group_check=skip), reads, writes)

    def TR(out, in_, ident, reads=(), writes=()):
        fw.op("pe", lambda e: e.transpose(out, in_, ident), reads, writes)

    def ACT(out, in_, func, reads=(), writes=(), bias=None, scale=None, accum=None):
        kw = {}
        if bias is not None:
            kw["bias"] = bias
        if scale is not None:
            kw["scale"] = scale
        if accum is not None:
            kw["accum_out"] = accum
        fw.op("act", lambda e: e.activation(out=out, in_=in_, func=func, **kw), list(reads) + [tconst], writes)

    def TT(eng, out, in0, in1, op, reads=(), writes=()):
        fw.op(eng, lambda e: e.tensor_tensor(out=out, in0=in0, in1=in1, op=op), reads, writes)

    def TS(eng, out, in0, s1, s2, op0, op1, reads=(), writes=()):
        fw.op(eng, lambda e: e.tensor_scalar(out=out, in0=in0, scalar1=s1, scalar2=s2, op0=op0, op1=op1), reads, writes)

    def TS1(eng, out, in0, s1, op0, reads=(), writes=()):
        fw.op(eng, lambda e: e.tensor_scalar(out=out, in0=in0, scalar1=s1, scalar2=None, op0=op0), reads, writes)

    def STT(eng, out, in0, scalar, in1, op0, op1, reads=(), writes=()):
        fw.op(eng, lambda e: e.scalar_tensor_tensor(out=out, in0=in0, scalar=scalar, in1=in1, op0=op0, op1=op1), reads, writes)

    def CP(eng, out, in_, reads=(), writes=()):
        if eng == "act":
            fw.op("act", lambda e: e.copy(out=out, in_=in_), reads, writes)
        else:
            fw.op(eng, lambda e: e.tensor_copy(out=out, in_=in_), reads, writes)

    def RECIP(out, in_, reads=(), writes=()):
        fw.op("dve", lambda e: e.reciprocal(out=out, in_=in_), reads, writes)

    def MEMSET(eng, ap, val, reads=(), writes=()):
        fw.op(eng, lambda e: e.memset(ap, val), reads, writes)

    def DMA(eng, out, in_, reads=(), writes=()):
        fw.dma(eng, lambda e: e.dma_start(out=out, in_=in_), reads, writes)

    identf = fw.sbuf([128, 128], F32)
    ident = fw.sbuf([128, 128], BF16)
    mask32 = fw.sbuf([128, 128], F32)
    mask16 = fw.sbuf([128, 128], BF16)
    ones32 = fw.sbuf([128, 128], F32)
    sel64 = fw.sbuf([128, 128], F32)
    ones16r = fw.sbuf([1, 128], BF16)
    cst = fw.sbuf([128, 4], F32)
    tconst = Tok()
    MEMSET("pool", identf[:], 0.0, writes=[tconst])
    fw.op("pool", lambda e: e.affine_select(out=identf[:], in_=identf[:], compare_op=ALU.not_equal, fill=1.0,
                                            base=0, pattern=[[-1, 128]], channel_multiplier=1), [tconst], [tconst])
    CP("pool", ident[:], identf[:], [tconst], [tconst])
    MEMSET("pool", mask32[:], 1.0, writes=[tconst])
    fw.op("pool", lambda e: e.affine_select(out=mask32[:], in_=mask32[:], compare_op=ALU.is_ge, fill=0.0,
                                            base=0, pattern=[[1, 128]], channel_multiplier=-1), [tconst], [tconst])
    CP("pool", mask16[:], mask32[:], [tconst], [tconst])
    MEMSET("pool", ones32[:], 1.0, writes=[tconst])
    MEMSET("pool", ones16r[:], 1.0, writes=[tconst])
    CP("pool", sel64[:], identf[:, 0:1].to_broadcast([128, 128]), [tconst], [tconst])
    MEMSET("pool", cst[:, 0:1], EPS, writes=[tconst])
    MEMSET("pool", cst[:, 1:2], -0.5 * math.log(128.0), writes=[tconst])
    MEMSET("pool", cst[:, 2:3], 1.0, writes=[tconst])
    eps_c = cst[:, 0:1]
    lnsc_c = cst[:, 1:2]
    one_c = cst[:, 2:3]

    PS = [fw.psum([128, 1024], F32) for _ in range(4)]
    ptok = toks(8)

    def bank(b):
        return PS[b // 2][:, (b % 2) * 512:(b % 2) * 512 + 512]

    def bank16(b):
        return bank(b).bitcast(BF16)

    AW = (nc.sbuf_bytes_remaining - 2048) // 4
    AW = AW // 8 * 8
    arena_t = fw.sbuf([128, AW], F32)
    AR = Arena(arena_t[:], AW)

    xtok = [[Tok() for _ in range(NTB)] for _ in range(nseq)]

    def xrows(src, sq, tb):
        return src[sq, tb * 128:(tb + 1) * 128, :]

    def wblock(w_ap, c0, ncols):
        return w_ap[:, c0:c0 + ncols].rearrange("(kc p) c -> p kc c", p=128)

    def load_bc(dst, row_ap, tok):
        DMA("sp", dst, row_ap.partition_broadcast(128), writes=[tok])

    def rstd_from_ss(ss, tmp, rstd, inv_n, tk):
        ACT(tmp, ss, AF.Ln, [tk], [tk], bias=eps_c, scale=inv_n)
        ACT(rstd, tmp, AF.Exp, [tk], [tk], scale=-0.5)

    class NormT:
        def __init__(self, pb=(6, 7), nx=3, nh=2):
            self.xt = Rot([(AR.alloc([1024], F32), Tok()) for _ in range(nx)])
            self.hn = Rot([(AR.alloc([1024], BF16), Tok()) for _ in range(nh)])
            self.junk = AR.alloc([1024], BF16)
            self.sm = Rot([(AR.alloc([4], F32), Tok()) for _ in range(max(nx, 4))])
            self.pbr = Rot(list(pb))
            self.tj = Tok()

        def part1(self, src_ap, src_tok, g_bc, g_tok, xbuf=None):
            x_ap, x_tk = xbuf if xbuf is not None else self.xt.next()
            DMA("sp", x_ap, src_ap, reads=[src_tok], writes=[x_tk])
            s_ap, s_tk = self.sm.next()
            ACT(self.junk, x_ap, AF.Square, [x_tk], [self.tj, s_tk], accum=s_ap[:, 0:1])
            rstd_from_ss(s_ap[:, 0:1], s_ap[:, 1:2], s_ap[:, 2:3], 1.0 / D, s_tk)
            h_ap, h_tk = self.hn.next()
            STT("dve", h_ap, x_ap, s_ap[:, 2:3], g_bc, ALU.mult, ALU.mult, [x_tk, s_tk, g_tok], [h_tk])
            return h_ap, h_tk

        def part2(self, h_ap, h_tk, dstT, dst_tok, col, alt=0):
            pbk = self.pbr.next()
            pv = bank16(pbk)
            for k in range(8):
                TR(pv[:, k * 128:(k + 1) * 128], h_ap[:, k * 128:(k + 1) * 128], ident[:], [h_tk, tconst], [ptok[pbk]])
            CP("dve", dstT[:, :, col:col + 128],
               pv.rearrange("p (k t) -> p k t", k=8), [ptok[pbk]], [dst_tok])

        def run(self, src_fn, src_toks, nblk, g_bc, g_tok, dstT, dst_tok_fn, col0=0):
            for b in range(nblk):
                h_ap, h_tk = self.part1(src_fn(b), src_toks[b], g_bc, g_tok)
                self.part2(h_ap, h_tk, dstT, dst_tok_fn(b), col0 + b * 128, alt=b)

    class PostRes:
        def __init__(self, g_bc, g_tok, nxt=4, nt1=2):
            self.g_bc, self.g_tok = g_bc, g_tok
            self.xt = Rot([(AR.alloc([1024], F32), Tok()) for _ in range(nxt)])
            self.t1 = Rot([(AR.alloc([1024], F32), Tok()) for _ in range(nt1)])
            self.junk = AR.alloc([1024], BF16)
            self.tj = Tok()
            self.sm = Rot([(AR.alloc([4], F32), Tok()) for _ in range(3)])
            self.q = {}
            self.todo = []

        def plan(self, src, sq, tbs):
            self.src, self.sq = src, sq
            self.todo = list(tbs)
            self._fill()

        def _fill(self):
            while len(self.q) < 2 and self.todo:
                tb = self.todo.pop(0)
                x_ap, x_tk = self.xt.next()
                DMA("sp", x_ap, xrows(self.src, self.sq, tb), reads=[xtok[self.sq][tb]], writes=[x_tk])
                self.q[tb] = (x_ap, x_tk)

        def run(self, pp, sq, tb, xbuf=None):
            if xbuf is not None:
                x_ap, x_tk = xbuf
            else:
                x_ap, x_tk = self.q.pop(tb)
                self._fill()
            y = PS[pp][:, :]
            ytk = [ptok[2 * pp], ptok[2 * pp + 1]]
            s_ap, s_tk = self.sm.next()
            ACT(self.junk, y, AF.Square, ytk, [self.tj, s_tk], accum=s_ap[:, 0:1])
            rstd_from_ss(s_ap[:, 0:1], s_ap[:, 1:2], s_ap[:, 2:3], 1.0 / D, s_tk)
            t_ap, t_tk = self.t1.next()
            STT("dve", t_ap, y, s_ap[:, 2:3], self.g_bc, ALU.mult, ALU.mult, ytk + [s_tk, self.g_tok], [t_tk])
            TT("dve", x_ap, x_ap, t_ap, ALU.add, [x_tk, t_tk], [x_tk])
            DMA("sp", xrows(out_d, sq, tb), x_ap, reads=[x_tk], writes=[xtok[sq][tb]])

    def out_proj(actT, act_tok_fn, nk, W, w_toks, g_bc, g_tok, src, sq):
        pr = PostRes(g_bc, g_tok)
        pr.plan(src, sq, range(NTB))
        for tb in range(NTB):
            pp = tb % 2
            for half in range(2):
                bk = 2 * pp + half
                for k in range(nk):
                    MM(bank(bk), actT[:, k, tb * 128:(tb + 1) * 128], W[:, k, half * 512:(half + 1) * 512],
                       start=(k == 0), stop=(k == nk - 1), reads=[act_tok_fn(tb), w_toks[k]], writes=[ptok[bk]])
            pr.run(pp, sq, tb)

    for sq in range(nseq):
        for l in range(depth):
            xsrc = x_d if l == 0 else out_d
            AR.release(0)
            hT = AR.alloc([8, S], BF16)
            hT_tok = toks(NTT)
            yfT = AR.alloc([4, S], BF16)
            ymT = AR.alloc([4, S], BF16)
            ygT = AR.alloc([4, S], BF16)
            yfT_tok, ymT_tok, ygT_tok = toks(NTT), toks(NTT), toks(NTT)
            apr = AR.alloc([NTB, 4], F32)
            ebt = AR.alloc([NTB, 4], F32)
            wsc = AR.alloc([NTB, 4], F32)
            emb = AR.alloc([NTB, 4], F32)
            ml_tok = Tok()
            bcol = AR.alloc([44], F32)
            bcol_tok = Tok()
            DMA("sp", bcol, bcol_d[l], writes=[bcol_tok])
            m_mix = AR.mark()
            fbT = AR.alloc([8, NTB, NTB // 2], F32)
            fb_tok = Tok()

            g_bc = AR.alloc([1024], F32)
            g_tok = Tok()
            load_bc(g_bc, norms_d[l, 0:1, :], g_tok)
            NormT(nx=4).run(lambda b: xrows(xsrc, sq, b), xtok[sq], NTB, g_bc, g_tok, hT, lambda b: hT_tok[b // 4])

            def emit_PA():
                wsm = AR.alloc([8, 16], BF16)
                t_wsm = Tok()
                DMA("pool", wsm, wsm_d[l].rearrange("(kc p) c -> p kc c", p=128), writes=[t_wsm])
                bsm = AR.alloc([16], F32)
                t_bsm = Tok()
                load_bc(bsm, bsm_d[l], t_bsm)
                gt = AR.alloc([NTB, 16], F32)
                nl = AR.alloc([NTB, 16], F32)
                gw = AR.alloc([NTB, 16], F32)
                tot = AR.alloc([NTB, 16], F32)
                off = AR.alloc([NTB, 16], F32)
                Gc = AR.alloc([NTB, 16], F32)
                Gref = AR.alloc([NTB, 16], F32)
                tg = Tok()
                pg = bank(0)[:, 0:NTB * 16].rearrange("p (a b) -> p a b", a=NTB, b=16)
                for tb in range(NTB):
                    for k in range(8):
                        MM(bank(0)[:, tb * 16:(tb + 1) * 16], hT[:, k, tb * 128:(tb + 1) * 128], wsm[:, k, :],
                           start=(k == 0), stop=(k == 7), reads=[hT_tok[tb // 4], t_wsm], writes=[ptok[0]])
                TT("dve", gt, pg, bsm.unsqueeze(1).to_broadcast([128, NTB, 16]), ALU.add, [ptok[0], t_bsm], [tg])
                ACT(nl, gt, AF.Exp, [tg], [tg], scale=-1.0)
                ACT(nl, nl, AF.Ln, [tg], [tg], bias=one_c, scale=1.0)
                nlf = nl.rearrange("p a b -> p (a b)")
                MM(bank(1)[:, 0:NTB * 16], mask32[:], nlf, reads=[tg, tconst], writes=[ptok[1]])
                MM(bank(2)[:, 0:NTB * 16], ones32[:], nlf, reads=[tg, tconst], writes=[ptok[2]])
                CP("dve", gw.rearrange("p a b -> p (a b)"), bank(1)[:, 0:NTB * 16], [ptok[1]], [tg])
                CP("dve", tot.rearrange("p a b -> p (a b)"), bank(2)[:, 0:NTB * 16], [ptok[2]], [tg])
                MEMSET("dve", off[:, 0, :], 0.0, writes=[tg])
                for tb in range(1, NTB):
                    TT("dve", off[:, tb, :], off[:, tb - 1, :], tot[:, tb - 1, :], ALU.add, [tg], [tg])
                TT("dve", Gc, gw, off, ALU.add, [tg], [tg])
                MM(bank(3)[:, 0:NTB * 16], sel64[:], Gc.rearrange("p a b -> p (a b)"), reads=[tg, tconst], writes=[ptok[3]])
                CP("dve", Gref.rearrange("p a b -> p (a b)"), bank(3)[:, 0:NTB * 16], [ptok[3]], [tg])
                Gref_odd = Gref.rearrange("p (i two) c -> p i two c", two=2)[:, :, 1, :]
                for h in range(8):
                    TT("dve", fbT[:, h, :, :], Gc[:, :, h].unsqueeze(2).to_broadcast([128, NTB, NTB // 2]),
                       Gref_odd[:, :, h].unsqueeze(1).to_broadcast([128, NTB, NTB // 2]), ALU.subtract, [tg], [fb_tok])
                TT("dve", apr, gt[:, :, 8:12], gw[:, :, 12:16], ALU.add, [tg], [ml_tok])
                ACT(apr, apr, AF.Exp, [ml_tok], [ml_tok], bias=lnsc_c, scale=1.0)
                ACT(ebt, tot[:, :, 12:16], AF.Exp, [tg], [ml_tok], scale=-1.0)
                TT("dve", wsc, apr, ebt, ALU.mult, [ml_tok], [ml_tok])
                ACT(emb, gw[:, :, 12:16], AF.Exp, [tg], [ml_tok])


            Wv = AR.alloc([8, 512], BF16)
            t_wv = Tok()
            DMA("pool", Wv, wblock(w_in_d[l], FOX_V, 512), writes=[t_wv])
            bv = AR.alloc([512], F32)
            t_bv = Tok()
            load_bc(bv, b_row_d[l, :, FOX_V:FOX_V + 512], t_bv)
            Vaug = AR.alloc([NTB, 8 * 65], BF16)
            Vaug4 = Vaug.rearrange("p a (h c) -> p a h c", h=8, c=65)
            V_tok = toks(NTB)
            yf_tm = AR.alloc([NTB, 512], BF16)
            yf_tok = toks(NTT)
            for tb in range(NTB):
                MEMSET("pool", Vaug4[:, tb, :, 64:65], 1.0, writes=[V_tok[tb]])
                bk = 6 + (tb % 2)
                for k in range(8):
                    MM(bank(bk), hT[:, k, tb * 128:(tb + 1) * 128], Wv[:, k, :], start=(k == 0), stop=(k == 7),
                       reads=[hT_tok[tb // 4], t_wv], writes=[ptok[bk]])
                TT("dve", Vaug4[:, tb, :, 0:64], bank(bk).rearrange("p (h c) -> p h c", h=8, c=64),
                   bv.rearrange("p (h c) -> p h c", h=8, c=64), ALU.add, [ptok[bk], t_bv], [V_tok[tb]])
            Wqk = Rot([(AR.alloc([8, 128], BF16), Tok()) for _ in range(4)])
            qkT = Rot([(AR.alloc([S], BF16), toks(NTT)) for _ in range(4)])
            PTs = Rot([(AR.alloc([512], BF16), toks(4)) for _ in range(6)])
            rinv = Rot([(AR.alloc([4], F32), Tok()) for _ in range(2)])
            sbk = Rot([0, 1, 2, 3])
            obk = Rot([4, 5, 6, 7])
            pjb = sbk
            def fox_proj_setup(hp):
                items = []
                for which in range(2):
                    w_ap, w_tk = Wqk.next()
                    c0 = (FOX_Q if which == 0 else FOX_K) + hp * 128
                    DMA("pool", w_ap, wblock(w_in_d[l], c0, 128), writes=[w_tk])
                    t_ap, t_tks = qkT.next()
                    bcolk = hp if which == 0 else 4 + hp
                    items.append((w_ap, w_tk, t_ap, t_tks, bcolk))
                return items

            def fox_proj_part(items, tt):
                for (w_ap, w_tk, t_ap, t_tks, bcolk) in items:
                    bk = pjb.next()
                    for k in range(8):
                        MM(bank(bk), w_ap[:, k, :], hT[:, k, tt * 512:(tt + 1) * 512], start=(k == 0), stop=(k == 7),
                           reads=[w_tk, hT_tok[tt]], writes=[ptok[bk]])
                    TS1("dve", t_ap[:, tt * 512:(tt + 1) * 512], bank(bk), bcol[:, bcolk:bcolk + 1], ALU.add, [ptok[bk], bcol_tok], [t_tks[tt]])

            cur_items = fox_proj_setup(0)
            for tt in range(NTT):
                fox_proj_part(cur_items, tt)
            emit_PA()
            for hp in range(4):
                (_, _, qT, q_tks, _), (_, _, kT, k_tks, _) = cur_items
                nxt_items = fox_proj_setup(hp + 1) if hp + 1 < 4 else None
                for g in range(NTT):
                    i0 = 4 * g
                    nsteps = i0 + 4
                    obs = [obk.next(), obk.next()]
                    firsts = [True, True]
                    pend = {}

                    def emit_S(hh, j):
                        r0 = 64 * hh
                        ilo = max(j, i0)
                        ncol = (i0 + 4 - ilo) * 128
                        sb = sbk.next()
                        MM(bank(sb)[:, 0:ncol], kT[r0:r0 + 64, j * 128:(j + 1) * 128], qT[r0:r0 + 64, ilo * 128:(i0 + 4) * 128],
                           reads=[k_tks[j // 4], q_tks[g]], writes=[ptok[sb]])
                        return sb, ilo

                    def emit_EP(hh, j, info):
                        sb, ilo = info
                        h = 2 * hp + hh
                        ob = obs[hh]
                        obv = bank(ob)[:, 0:260].rearrange("p (a c) -> p a c", a=4, c=65)
                        p_ap, p_tks = PTs.next()
                        for I in (2 * g, 2 * g + 1):
                            i_lo = max(j, 2 * I)
                            if i_lo > 2 * I + 1:
                                continue
                            c0 = (i_lo - ilo) * 128
                            wd = (2 * I + 2 - i_lo) * 128
                            pt = p_tks[I - 2 * g]
                            ACT(p_ap[:, c0:c0 + wd], bank(sb)[:, c0:c0 + wd], AF.Exp, [ptok[sb], fb_tok], [pt],
                                bias=fbT[:, h, j, I:I + 1], scale=0.125)
                            if i_lo == j:
                                TT("dve", p_ap[:, c0:c0 + 128], p_ap[:, c0:c0 + 128], mask16[:], ALU.mult, [pt, tconst], [pt])
                        for i in range(ilo, i0 + 4):
                            c0 = (i - ilo) * 128
                            MM(obv[:, i - i0, :], p_ap[:, c0:c0 + 128], Vaug4[:, j, h, :], start=firsts[hh], stop=(j == i),
                               reads=[p_tks[i // 2 - 2 * g], V_tok[j]], writes=[ptok[ob]], skip=True)
                            firsts[hh] = False

                    LOOK = 1
                    for j in range(nsteps + LOOK):
                        if j < nsteps:
                            for hh in range(2):
                                pend[(hh, j)] = emit_S(hh, j)
                        if j - LOOK >= 0:
                            for hh in range(2):
                                emit_EP(hh, j - LOOK, pend.pop((hh, j - LOOK)))
                    for hh in range(2):
                        h = 2 * hp + hh
                        ob = obs[hh]
                        obv = bank(ob)[:, 0:260].rearrange("p (a c) -> p a c", a=4, c=65)
                        r_ap, r_tk = rinv.next()
                        RECIP(r_ap, obv[:, :, 64], [ptok[ob]], [r_tk])
                        TT("dve", yf_tm[:, i0:i0 + 4, h * 64:(h + 1) * 64], obv[:, :, 0:64],
                           r_ap.unsqueeze(2).to_broadcast([128, 4, 64]), ALU.mult, [ptok[ob], r_tk], [yf_tok[g]])
                    if nxt_items is not None:
                        fox_proj_part(nxt_items, g)
                if nxt_items is not None:
                    cur_items = nxt_items
            for tb in range(NTB):
                bk = pjb.next()
                pv = bank16(bk)
                for cc in range(4):
                    TR(pv[:, cc * 128:(cc + 1) * 128], yf_tm[:, tb, cc * 128:(cc + 1) * 128], ident[:], [yf_tok[tb // 4], tconst], [ptok[bk]])
                CP("act" if tb % 2 else "dve", yfT[:, :, tb * 128:(tb + 1) * 128], pv[:, 0:512].rearrange("p (k t) -> p k t", k=4),
                   [ptok[bk]], [yfT_tok[tb // 4]])
            fw.barrier()
            AR.release(m_mix)
            if stop_after == "fox":
                DMA("sp", dbg_d[:, 0:4, :], yfT, reads=yfT_tok)
                break

            pjb = Rot([6, 7])
            Wvm = AR.alloc([8, 512], BF16)
            Wo = AR.alloc([8, 512], BF16)
            t_wvm, t_wo = Tok(), Tok()
            DMA("pool", Wvm, wblock(w_in_d[l], ML_V, 512), writes=[t_wvm])
            DMA("pool", Wo, wblock(w_in_d[l], ML_O, 512), writes=[t_wo])
            bvm = AR.alloc([512], F32)
            bo = AR.alloc([512], F32)
            gml = AR.alloc([512], F32)
            cw = AR.alloc([8, 4], F32)
            t_small = Tok()
            load_bc(bvm, b_row_d[l, :, ML_V:ML_V + 512], t_small)
            load_bc(bo, b_row_d[l, :, ML_O:ML_O + 512], t_small)
            load_bc(gml, mlg_d[l], t_small)
            DMA("sp", cw, cw_d[l], writes=[t_small])
            qkm = [[(AR.alloc([S], BF16), Tok()) for _ in range(4)] for _ in range(2)]
            zps = Rot([(AR.alloc([S + 8], F32), Tok()) for _ in range(2)])
            accs = Rot([(AR.alloc([S], F32), Tok()) for _ in range(2)])
            Wqk = Rot([(AR.alloc([8, 128], BF16), Tok()) for _ in range(2)])
            for zp_, tz_ in zps.items:
                MEMSET("dve", zp_[:, 0:3], 0.0, writes=[tz_])
            def ml_proj(which, h):
                w_ap, w_tk = Wqk.next()
                c0 = (ML_Q if which == 0 else ML_K) + h * 128
                DMA("pool", w_ap, wblock(w_in_d[l], c0, 128), writes=[w_tk])
                blk = which * 4 + h
                zp, t_zp = zps.next()
                for tt in range(NTT):
                    bk = pjb.next()
                    for k in range(8):
                        MM(bank(bk), w_ap[:, k, :], hT[:, k, tt * 512:(tt + 1) * 512], start=(k == 0), stop=(k == 7),
                           reads=[w_tk, hT_tok[tt]], writes=[ptok[bk]])
                    ACT(zp[:, 3 + tt * 512:3 + (tt + 1) * 512], bank(bk), AF.Identity, [ptok[bk], bcol_tok], [t_zp],
                        bias=bcol[:, 8 + blk:9 + blk], scale=1.0)
                return zp, t_zp, blk

            def ml_conv(which, h, zp, t_zp, blk):
                acc, t_acc = accs.next()
                TS1("dve", acc, zp[:, 0:S], cw[:, blk, 0:1], ALU.mult, [t_zp, t_small], [t_acc])
                for j in range(1, 4):
                    STT("dve", acc, zp[:, j:j + S], cw[:, blk, j:j + 1], acc, ALU.mult, ALU.add, [t_zp, t_small, t_acc], [t_acc])
                d_ap, d_tk = qkm[which][h]
                ACT(d_ap, acc, AF.Silu, [t_acc], [d_tk])

            order = [(w_, h_) for w_ in range(2) for h_ in range(4)]
            nxt_p = ml_proj(*order[0])
            for n_, (w_, h_) in enumerate(order):
                cur_p = nxt_p
                if n_ + 1 < len(order):
                    nxt_p = ml_proj(*order[n_ + 1])
                ml_conv(w_, h_, *cur_p)
            Vm = Rot([(AR.alloc([4, 129], BF16), Tok()) for _ in range(3)])
            for v_ap, v_tk in Vm.items:
                MEMSET("pool", v_ap[:, :, 128:129], 1.0, writes=[v_tk])
            gsig = Rot([(AR.alloc([512], F32), Tok()) for _ in range(6)])
            gs_map = {}
            otmp = Rot([(AR.alloc([512], F32), Tok()) for _ in range(1)])
            ndS = Rot([(AR.alloc([4, 129], F32), Tok()) for _ in range(2)])
            PTm = Rot([(AR.alloc([128], BF16), Tok()) for _ in range(8)])
            K2 = Rot([(AR.alloc([128], BF16), Tok()) for _ in range(8)])
            C32 = [(AR.alloc([129], F32), Tok()) for _ in range(4)]
            Cb = [(AR.alloc([129], BF16), Tok()) for _ in range(4)]
            smr = Rot([(AR.alloc([8, 4], F32), Tok()) for _ in range(4)])
            junkm = AR.alloc([128], BF16)
            tjm = Tok()
            ymc = Rot([(AR.alloc([512], BF16), Tok()) for _ in range(2)])
            sbk = Rot([0, 1])
            tbk = Rot([2])
            ndk = Rot([3, 4])
            dbk = Rot([5])

            def ml_stage1(c):
                cs = slice(c * 128, (c + 1) * 128)
                v_ap, v_tk = Vm.next()
                bk = pjb.next()
                for k in range(8):
                    MM(bank(bk), hT[:, k, cs], Wvm[:, k, :], start=(k == 0), stop=(k == 7), reads=[hT_tok[c // 4], t_wvm], writes=[ptok[bk]])
                TT("dve", v_ap[:, :, 0:128], bank(bk).rearrange("p (h c) -> p h c", h=4, c=128),
                   bvm.rearrange("p (h c) -> p h c", h=4, c=128), ALU.add, [ptok[bk], t_small], [v_tk])
                if c % 4 == 0:
                    for c2 in range(c, min(c + 4, NTB)):
                        cs2 = slice(c2 * 128, (c2 + 1) * 128)
                        bk = pjb.next()
                        for k in range(8):
                            MM(bank(bk), hT[:, k, cs2], Wo[:, k, :], start=(k == 0), stop=(k == 7), reads=[hT_tok[c2 // 4], t_wo], writes=[ptok[bk]])
                        o_ap, o_tk = otmp.next()
                        TT("dve", o_ap, bank(bk), bo, ALU.add, [ptok[bk], t_small], [o_tk])
                        gg_ap, gg_tk = gsig.next()
                        ACT(gg_ap, o_ap, AF.Sigmoid, [o_tk], [gg_tk])
                        TT("dve", gg_ap, gg_ap, gml, ALU.mult, [gg_tk, t_small], [gg_tk])
                        gs_map[c2] = (gg_ap, gg_tk)
                g_ap, g_tk = gs_map.pop(c)
                sb = sbk.next()
                for h in range(4):
                    MM(bank(sb)[:, h * 128:(h + 1) * 128], qkm[1][h][0][:, cs], qkm[0][h][0][:, cs],
                       reads=[qkm[1][h][1], qkm[0][h][1]], writes=[ptok[sb]])
                tb_ = tbk.next()
                for h in range(4):
                    TR(bank16(tb_)[:, h * 128:(h + 1) * 128], qkm[1][h][0][:, cs], ident[:], [qkm[1][h][1], tconst], [ptok[tb_]])
                pts, k2s = [], []
                for h in range(4):
                    p_ap, p_tk = PTm.next()
                    STT("dve", p_ap, bank(sb)[:, h * 128:(h + 1) * 128], apr[:, c, h:h + 1], mask32[:], ALU.mult, ALU.mult,
                        [ptok[sb], ml_tok, tconst], [p_tk])
                    pts.append((p_ap, p_tk))
                for h in range(4):
                    k2_ap, k2_tk = K2.next()
                    ACT(k2_ap, bank16(tb_)[:, h * 128:(h + 1) * 128], AF.Identity, [ptok[tb_], ml_tok], [k2_tk], scale=wsc[:, c, h:h + 1])
                    k2s.append((k2_ap, k2_tk))
                return dict(v=(v_ap, v_tk), g=(g_ap, g_tk), pts=pts, k2s=k2s)

            def ml_stage2(c, st, nxt_holder):
                cs = slice(c * 128, (c + 1) * 128)
                v_ap, v_tk = st["v"]
                g_ap, g_tk = st["g"]
                y_ap, y_tk = ymc.next()
                nds = [ndk.next(), ndk.next()]
                ndv = []
                for h in range(4):
                    nd = nds[h // 2]
                    v_ = bank(nd)[:, (h % 2) * 129:(h % 2) * 129 + 129]
                    ndv.append((nd, v_))
                    MM(v_, st["pts"][h][0], v_ap[:, h, :], start=True, stop=(c == 0), reads=[st["pts"][h][1], v_tk], writes=[ptok[nd]])
                    if c > 0:
                        MM(v_, qkm[0][h][0][:, cs], Cb[h][0], start=False, stop=True, reads=[qkm[0][h][1], Cb[h][1]], writes=[ptok[nd]])
                nS_ap, nS_tk = ndS.next()
                for hp_ in range(2):
                    CP("act", nS_ap[:, 2 * hp_:2 * hp_ + 2, :], bank(nds[hp_])[:, 0:258].rearrange("p (h c) -> p h c", h=2, c=129),
                       [ptok[nds[hp_]]], [nS_tk])
                if c + 1 < NTB:
                    for hp_ in range(2):
                        db = dbk.next()
                        for hh in range(2):
                            h = 2 * hp_ + hh
                            MM(bank(db)[:, hh * 129:hh * 129 + 129], st["k2s"][h][0], v_ap[:, h, :], reads=[st["k2s"][h][1], v_tk], writes=[ptok[db]])
                        for hh in range(2):
                            h = 2 * hp_ + hh
                            c_ap, c_tk = C32[h]
                            dv = bank(db)[:, hh * 129:hh * 129 + 129]
                            if c == 0:
                                CP("dve", c_ap, dv, [ptok[db]], [c_tk])
                            else:
                                STT("dve", c_ap, c_ap, ebt[:, c, h:h + 1], dv, ALU.mult, ALU.add, [c_tk, ml_tok, ptok[db]], [c_tk])
                        for hh in range(2):
                            h = 2 * hp_ + hh
                            CP("pool", Cb[h][0], C32[h][0], [C32[h][1]], [Cb[h][1]])
                while ml_pending:
                    ml_pending.pop(0)()
                s_ap, s_tk = smr.next()
                STT("dve", s_ap[:, 1, :], nS_ap[:, :, 128], -1.0, nS_ap[:, :, 128], ALU.mult, ALU.max, [nS_tk], [s_tk])
                TT("dve", s_ap[:, 2, :], s_ap[:, 1, :], emb[:, c, :], ALU.max, [s_tk, ml_tok], [s_tk])
                RECIP(s_ap[:, 3, :], s_ap[:, 2, :], [s_tk], [s_tk])
                for h in range(4):
                    ACT(junkm, nS_ap[:, h, 0:128], AF.Square, [nS_tk, s_tk], [tjm, s_tk], scale=s_ap[:, 3, h:h + 1], accum=s_ap[:, 4, h:h + 1])
                if c + 1 < NTB:
                    nxt_holder.append(ml_stage1(c + 1))
                ACT(s_ap[:, 5, :], s_ap[:, 4, :], AF.Ln, [s_tk], [s_tk], bias=eps_c, scale=1.0 / 128)
                ACT(s_ap[:, 6, :], s_ap[:, 5, :], AF.Exp, [s_tk], [s_tk], scale=-0.5)
                TT("dve", s_ap[:, 7, :], s_ap[:, 3, :], s_ap[:, 6, :], ALU.mult, [s_tk], [s_tk])
                for h in range(4):
                    STT("dve", y_ap[:, h * 128:(h + 1) * 128], nS_ap[:, h, 0:128], s_ap[:, 7, h:h + 1], g_ap[:, h * 128:(h + 1) * 128],
                        ALU.mult, ALU.mult, [nS_tk, s_tk, g_tk], [y_tk])
                def finish():
                    bk = pjb.next()
                    pv = bank16(bk)
                    for cc in range(4):
                        TR(pv[:, cc * 128:(cc + 1) * 128], y_ap[:, cc * 128:(cc + 1) * 128], ident[:], [y_tk, tconst], [ptok[bk]])
                    CP("act", ymT[:, :, cs], pv[:, 0:512].rearrange("p (k t) -> p k t", k=4), [ptok[bk]], [ymT_tok[c // 4]])
                return finish

            ml_pending = []
            st_cur = ml_stage1(0)
            for c in range(NTB):
                holder = []
                ml_pending.append(ml_stage2(c, st_cur, holder))
                if holder:
                    st_cur = holder[0]
            for f in ml_pending:
                f()
            fw.barrier()
            AR.release(m_mix)
            if stop_after == "mlstm":
                DMA("sp", dbg_d[:, 0:4, :], ymT, reads=ymT_tok)
                break

            mgT = AR.alloc([8, S], BF16)
            mg_tok = toks(NTT)
            m_mg = AR.mark()
            Wu = AR.alloc([8, 512], BF16)
            Wgv = AR.alloc([8, 512], BF16)
            t_wu, t_wgv = Tok(), Tok()
            DMA("pool", Wu, wblock(w_in_d[l], G_U, 512), writes=[t_wu])
            DMA("pool", Wgv, wblock(w_in_d[l], G_V, 512), writes=[t_wgv])
            bgv = AR.alloc([512], F32)
            ggb = AR.alloc([512], F32)
            ws32 = AR.alloc([4, 128], F32)
            ws16 = AR.alloc([4, 128], BF16)
            bs16 = AR.alloc([512], BF16)
            t_g = Tok()
            load_bc(bgv, b_row_d[l, :, G_V:G_V + 512], t_g)
            load_bc(ggb, gg_d[l], t_g)
            DMA("sp", ws32, wsT_d[l], writes=[t_g])
            DMA("pool", bs16[0:1, :], bs_d[l], writes=[t_g])
            MEMSET("dve", ws32[64:128, :, 0:64], 0.0, [t_g], [t_g])
            CP("dve", ws16, ws32, [t_g], [t_g])
            uT = Rot([(AR.alloc([4, 512], BF16), Tok()) for _ in range(2)])
            vtmp = Rot([(AR.alloc([512], F32), Tok()) for _ in range(8)])
            vn = Rot([(AR.alloc([512], BF16), Tok()) for _ in range(4)])
            smr = Rot([(AR.alloc([8, 4], F32), Tok()) for _ in range(3)])
            junkg = AR.alloc([512], BF16)
            tjg = Tok()
            mxb = Rot([0, 1, 2, 3])
            gpb = Rot([4, 5, 6, 7])
            def gm_front(tt):
                u_ap, u_tk = uT.next()
                for g in range(4):
                    bk = gpb.next()
                    for k in range(8):
                        MM(bank(bk), Wu[:, k, g * 128:(g + 1) * 128], hT[:, k, tt * 512:(tt + 1) * 512], start=(k == 0), stop=(k == 7),
                           reads=[t_wu, hT_tok[tt]], writes=[ptok[bk]])
                    ACT(u_ap[:, g, :], bank(bk), AF.Gelu, [ptok[bk], bcol_tok], [u_tk], bias=bcol[:, 16 + g:17 + g], scale=1.0)
                s_ap, s_tk = smr.next()
                vts = []
                for sp_ in range(4):
                    tb = tt * 4 + sp_
                    bk = gpb.next()
                    for k in range(8):
                        MM(bank(bk), hT[:, k, tb * 128:(tb + 1) * 128], Wgv[:, k, :], start=(k == 0), stop=(k == 7),
                           reads=[hT_tok[tt], t_wgv], writes=[ptok[bk]])
                    v_ap, v_tk = vtmp.next()
                    TT("dve", v_ap, bank(bk), bgv, ALU.add, [ptok[bk], t_g], [v_tk])
                    vts.append((v_ap, v_tk))
                for sp_ in range(4):
                    v_ap, v_tk = vts[sp_]
                    ACT(v_ap, v_ap, AF.Gelu, [v_tk], [v_tk, s_tk], accum=s_ap[:, 0, sp_:sp_ + 1])
                for sp_ in range(4):
                    v_ap, v_tk = vts[sp_]
                    ACT(junkg, v_ap, AF.Square, [v_tk], [tjg, s_tk], accum=s_ap[:, 1, sp_:sp_ + 1])
                return u_ap, u_tk, s_ap, s_tk, vts

            def gm_back(tt, u_ap, u_tk, s_ap, s_tk, vts):
                TS1("dve", s_ap[:, 2, :], s_ap[:, 0, :], 1.0 / 512, ALU.mult, [s_tk], [s_tk])
                TT("dve", s_ap[:, 3, :], s_ap[:, 2, :], s_ap[:, 2, :], ALU.mult, [s_tk], [s_tk])
                STT("dve", s_ap[:, 4, :], s_ap[:, 1, :], 1.0 / 512, s_ap[:, 3, :], ALU.mult, ALU.subtract, [s_tk], [s_tk])
                ACT(s_ap[:, 5, :], s_ap[:, 4, :], AF.Ln, [s_tk], [s_tk], bias=eps_c, scale=1.0)
                ACT(s_ap[:, 6, :], s_ap[:, 5, :], AF.Exp, [s_tk], [s_tk], scale=-0.5)
                vns = []
                for sp_ in range(4):
                    v_ap, v_tk = vts[sp_]
                    TS("dve", v_ap, v_ap, s_ap[:, 2, sp_:sp_ + 1], s_ap[:, 6, sp_:sp_ + 1], ALU.subtract, ALU.mult, [v_tk, s_tk], [v_tk])
                    n_ap, n_tk = vn.next()
                    TT("dve", n_ap, v_ap, ggb, ALU.mult, [v_tk, t_g], [n_tk])
                    vns.append((n_ap, n_tk))
                mbs = []
                for sp_ in range(4):
                    n_ap, n_tk = vns[sp_]
                    mb = mxb.next()
                    for g in range(4):
                        MM(bank(mb)[:, g * 128:(g + 1) * 128], n_ap[:, g * 128:(g + 1) * 128], ws16[:, g, :], start=(g == 0), stop=False,
                           reads=[n_tk, t_g], writes=[ptok[mb]], skip=True)
                        MM(bank(mb)[:, g * 128:(g + 1) * 128], ones16r[:], bs16[0:1, g * 128:(g + 1) * 128], start=False, stop=True,
                           reads=[t_g, tconst], writes=[ptok[mb]], skip=True)
                    mbs.append(mb)
                for sp_ in range(4):
                    tb = tt * 4 + sp_
                    mb = mbs[sp_]
                    TT("dve", ygT[:, :, tb * 128:(tb + 1) * 128], bank(mb).rearrange("p (g t) -> p g t", g=4),
                       u_ap[:, :, sp_ * 128:(sp_ + 1) * 128], ALU.mult, [ptok[mb], u_tk], [ygT_tok[tt]])

            gm_next = gm_front(0)
            for tt in range(NTT):
                gm_cur = gm_next
                if tt + 1 < NTT:
                    gm_next = gm_front(tt + 1)
                gm_back(tt, *gm_cur)
            if stop_after == "gmlp":
                DMA("sp", dbg_d[:, 0:4, :], ygT, reads=ygT_tok)
                break

            Wg = Rot([(AR.alloc([8, 128], BF16), Tok()) for _ in range(3)])
            Wb = Rot([(AR.alloc([4, 128], BF16), Tok()) for _ in range(3)])
            sgt = Rot([(AR.alloc([512], F32), Tok()) for _ in range(3)])
            macc = AR.alloc([NTT, 512], F32)
            macc_tok = toks(NTT)
            ysrc = [(yfT, yfT_tok), (ymT, ymT_tok), (ygT, ygT_tok)]
            gbk = Rot([0, 1, 2])
            bbk = Rot([3, 4, 5])
            for dc in range(8):
                for n in range(3):
                    wg_ap, wg_tk = Wg.next()
                    DMA("pool", wg_ap, wblock(w_in_d[l], GATE + n * 1024 + dc * 128, 128), writes=[wg_tk])
                    wb_ap, wb_tk = Wb.next()
                    DMA("pool", wb_ap, wbr_d[l, n, :, dc * 128:(dc + 1) * 128].rearrange("(kc p) c -> p kc c", p=128), writes=[wb_tk])
                    yT, y_tks = ysrc[n]
                    for tt in range(NTT):
                        ts_ = slice(tt * 512, (tt + 1) * 512)
                        gb = gbk.next()
                        for k in range(8):
                            MM(bank(gb), wg_ap[:, k, :], hT[:, k, ts_], start=(k == 0), stop=(k == 7), reads=[wg_tk, hT_tok[tt]], writes=[ptok[gb]])
                        s_ap, s_tk = sgt.next()
                        ACT(s_ap, bank(gb), AF.Sigmoid, [ptok[gb], bcol_tok], [s_tk], bias=bcol[:, 20 + n * 8 + dc:21 + n * 8 + dc], scale=1.0)
                        bb = bbk.next()
                        for k in range(4):
                            MM(bank(bb), wb_ap[:, k, :], yT[:, k, ts_], start=(k == 0), stop=(k == 3), reads=[wb_tk, y_tks[tt]], writes=[ptok[bb]])
                        if n == 0:
                            TT("dve", macc[:, tt, :], s_ap, bank(bb), ALU.mult, [s_tk, ptok[bb]], [macc_tok[tt]])
                        else:
                            TT("dve", s_ap, s_ap, bank(bb), ALU.mult, [s_tk, ptok[bb]], [s_tk])
                            if n == 1:
                                TT("dve", macc[:, tt, :], macc[:, tt, :], s_ap, ALU.add, [s_tk, macc_tok[tt]], [macc_tok[tt]])
                            else:
                                TT("dve", mgT[:, dc, ts_], macc[:, tt, :], s_ap, ALU.add, [s_tk, macc_tok[tt]], [mg_tok[tt]])
            fw.barrier()
            AR.release(m_mg)
            if stop_after == "merge":
                DMA("sp", dbg_d[:, :, :], mgT, reads=mg_tok)
                break

            Wout = AR.alloc([8, 1024], BF16)
            w_toks = toks(8)
            for k in range(8):
                DMA("pool", Wout[:, k, :], wout_d[l, k * 128:(k + 1) * 128, :], writes=[w_toks[k]])
            g_bc = AR.alloc([1024], F32)
            g_tok = Tok()
            load_bc(g_bc, norms_d[l, 1:2, :], g_tok)
            Wxq = AR.alloc_top([8, 1024], BF16)
            Wxo = AR.alloc_top([8, 1024], BF16)
            wq_toks, wo_toks = toks(8), toks(8)
            for k in range(8):
                DMA("pool", Wxq[:, k, :], wxq_d[l, k * 128:(k + 1) * 128, :], writes=[wq_toks[k]])
            for k in range(8):
                DMA("pool", Wxo[:, k, :], wxo_d[l, k * 128:(k + 1) * 128, :], writes=[wo_toks[k]])
            out_proj(mgT, lambda tb: mg_tok[tb // 4], 8, Wout, w_toks, g_bc, g_tok, xsrc, sq)
            fw.barrier()
            if stop_after == "mixer":
                break

            AR.release(0)
            memT = AR.alloc([8, MEM], BF16)
            memT_tok = toks(2)
            KT = AR.alloc([8, MEM], BF16)
            KT_tok = Tok()
            Vx = AR.alloc([2, 4 * 257], BF16)
            Vx4 = Vx.rearrange("p a (h c) -> p a h c", h=4, c=257)
            Vx_tok = toks(2)
            gpre = AR.alloc([1024], F32)
            gpost = AR.alloc([1024], F32)
            gmem = AR.alloc([1024], F32)
            t_gx = Tok()
            load_bc(gpre, norms_d[l, 2:3, :], t_gx)
            load_bc(gpost, norms_d[l, 3:4, :], t_gx)
            load_bc(gmem, norms_d[l, 4:5, :], t_gx)
            m_x = AR.mark()
            mem_toks = toks(2)
            nrm = NormT(nx=2, nh=4)
            nrm.run(lambda b: mem_d[sq, b * 128:(b + 1) * 128, :], mem_toks, 2, gmem, t_gx, memT, lambda b: memT_tok[b])
            Wk = Rot([(AR.alloc([8, 128], BF16), Tok()) for _ in range(2)])
            for blk in range(8):
                w_ap, w_tk = Wk.next()
                DMA("pool", w_ap, wblock(wxkv_d[l], blk * 128, 128), writes=[w_tk])
                bk = pjb.next()
                for k in range(8):
                    MM(bank(bk)[:, 0:MEM], w_ap[:, k, :], memT[:, k, :], start=(k == 0), stop=(k == 7), reads=[w_tk] + memT_tok, writes=[ptok[bk]])
                CP("act", KT[:, blk, :], bank(bk)[:, 0:MEM], [ptok[bk]], [KT_tok])
            Wvx = AR.alloc([8, 1024], BF16)
            wv_toks = toks(8)
            for k in range(8):
                DMA("pool", Wvx[:, k, :], wxkv_d[l, k * 128:(k + 1) * 128, D:2 * D], writes=[wv_toks[k]])
            for mb in range(2):
                MEMSET("pool", Vx4[:, mb, :, 256:257], 1.0, writes=[Vx_tok[mb]])
                for half in range(2):
                    bk = pjb.next()
                    for k in range(8):
                        MM(bank(bk), memT[:, k, mb * 128:(mb + 1) * 128], Wvx[:, k, half * 512:(half + 1) * 512], start=(k == 0), stop=(k == 7),
                           reads=[memT_tok[mb], wv_toks[k]], writes=[ptok[bk]])
                    CP("dve", Vx4[:, mb, 2 * half:2 * half + 2, 0:256], bank(bk).rearrange("p (h c) -> p h c", h=2, c=256), [ptok[bk]], [Vx_tok[mb]])
            hx = Rot([(AR.alloc([8, 512], BF16), toks(1)) for _ in range(2)])
            qx = Rot([(AR.alloc([8, 512], BF16), Tok()) for _ in range(2)])
            PTx = Rot([(AR.alloc([512], BF16), Tok()) for _ in range(4)])
            o_tm = Rot([(AR.alloc([4, 1024], BF16), Tok()) for _ in range(2)])
            oT = Rot([(AR.alloc([8, 512], BF16), Tok()) for _ in range(2)])
            rx = Rot([(AR.alloc([4], F32), Tok()) for _ in range(4)])
            pr = PostRes(gpost, t_gx, nxt=0, nt1=1)
            xs_pool = Rot([(AR.alloc([1024], F32), Tok()) for _ in range(8)])
            xs_of = {}
            sxb = Rot([0, 1, 2, 3])
            oxb = Rot([4, 5, 6, 7])
            pjb = sxb
            def xa_stageA1(tt):
                res = []
                for b in range(4):
                    xb = xs_pool.next()
                    xs_of[tt * 4 + b] = xb
                    res.append(nrm.part1(xrows(out_d, sq, tt * 4 + b), xtok[sq][tt * 4 + b], gpre, t_gx, xbuf=xb))
                return res

            def xa_stageA2(tt, hns):
                h_ap, h_tks = hx.next()
                for b in range(4):
                    nrm.part2(hns[b][0], hns[b][1], h_ap, h_tks[0], b * 128, alt=b)
                q_ap, q_tk = qx.next()
                for blk in range(8):
                    bk = pjb.next()
                    for k in range(8):
                        MM(bank(bk), Wxq[:, k, blk * 128:(blk + 1) * 128], h_ap[:, k, :], start=(k == 0), stop=(k == 7),
                           reads=[wq_toks[k], h_tks[0]], writes=[ptok[bk]])
                    CP("dve", q_ap[:, blk, :], bank(bk), [ptok[bk]], [q_tk])
                return q_ap, q_tk

            def xa_stageBC(tt, q_ap, q_tk, mid_hook):
                ot_ap, ot_tk = o_tm.next()

                def x_scores(h):
                    pts = []
                    for mb in range(2):
                        sb = sxb.next()
                        for cc in range(2):
                            MM(bank(sb), KT[:, 2 * h + cc, mb * 128:(mb + 1) * 128], q_ap[:, 2 * h + cc, :], start=(cc == 0), stop=(cc == 1),
                               reads=[KT_tok, q_tk], writes=[ptok[sb]])
                        p_ap, p_tk = PTx.next()
                        ACT(p_ap, bank(sb), AF.Exp, [ptok[sb]], [p_tk], scale=1.0 / 16.0)
                        pts.append((p_ap, p_tk))
                    return pts

                nxt_pts = x_scores(0)
                for h in range(4):
                    pts = nxt_pts
                    if h + 1 < 4:
                        nxt_pts = x_scores(h + 1)
                    for qb in range(4):
                        ob = oxb.next()
                        ov = bank(ob)[:, 0:257]
                        for mb in range(2):
                            MM(ov, pts[mb][0][:, qb * 128:(qb + 1) * 128], Vx4[:, mb, h, :], start=(mb == 0), stop=(mb == 1),
                               reads=[pts[mb][1], Vx_tok[mb]], writes=[ptok[ob]])
                        r_ap, r_tk = rx.next()
                        RECIP(r_ap[:, 0:1], bank(ob)[:, 256:257], [ptok[ob]], [r_tk])
                        TS1("dve", ot_ap[:, qb, h * 256:(h + 1) * 256], bank(ob)[:, 0:256], r_ap[:, 0:1], ALU.mult, [ptok[ob], r_tk], [ot_tk])
                oT_ap, oT_tk = oT.next()
                for qb in range(4):
                    bk = pjb.next()
                    pv = bank16(bk)
                    for cc in range(8):
                        TR(pv[:, cc * 128:(cc + 1) * 128], ot_ap[:, qb, cc * 128:(cc + 1) * 128], ident[:], [ot_tk, tconst], [ptok[bk]])
                    CP("dve", oT_ap[:, :, qb * 128:(qb + 1) * 128], pv.rearrange("p (k t) -> p k t", k=8), [ptok[bk]], [oT_tk])
                mid_hook()
                for qb in range(4):
                    tb = tt * 4 + qb
                    pp = qb % 2
                    for half in range(2):
                        bk = 2 * pp + half
                        for k in range(8):
                            MM(bank(bk), oT_ap[:, k, qb * 128:(qb + 1) * 128], Wxo[:, k, half * 512:(half + 1) * 512], start=(k == 0), stop=(k == 7),
                               reads=[oT_tk, wo_toks[k]], writes=[ptok[bk]])
                    pr.run(pp, sq, tb, xbuf=xs_of.pop(tb))

            xa_state = {"next": xa_stageA2(0, xa_stageA1(0))}
            for tt in range(NTT):
                xa_cur = xa_state["next"]
                hns_ = xa_stageA1(tt + 1) if tt + 1 < NTT else None

                def mid(tt=tt, hns_=hns_):
                    if hns_ is not None:
                        xa_state["next"] = xa_stageA2(tt + 1, hns_)
                xa_stageBC(tt, xa_cur[0], xa_cur[1], mid)
            fw.barrier()
            AR.free_top(AW)
            pjb = Rot([6, 7])
            if stop_after == "xattn":
                break

            AR.release(0)
            GT = min(1024, S)
            NG = S // GT
            NGB = GT // 128
            W2 = AR.alloc([32, 1024], BF16)
            w2_toks = toks(32)
            gpre = AR.alloc([1024], F32)
            gpost = AR.alloc([1024], F32)
            t_gf = Tok()
            load_bc(gpre, norms_d[l, 5:6, :], t_gf)
            load_bc(gpost, norms_d[l, 6:7, :], t_gf)
            hF = AR.alloc([8, GT], BF16)
            hF_tok = toks(GT // 512)
            aT = AR.alloc([32, GT], BF16)
            aT_tok = toks(32)
            W1 = Rot([(AR.alloc([8, 256], BF16), Tok()) for _ in range(2)])
            rl = Rot([(AR.alloc([512], F32), Tok()) for _ in range(2)])
            pr = PostRes(gpost, t_gf, nxt=3, nt1=1)
            pr.plan(out_d, sq, range(NTB))
            nrm = NormT(pb=(0, 1))
            fbk = Rot([0, 1, 2, 3])
            def ffn_norm(gi):
                t0_ = gi * NGB
                nrm.run(lambda b: xrows(out_d, sq, t0_ + b), xtok[sq][t0_:t0_ + NGB], NGB, gpre, t_gf, hF, lambda b: hF_tok[b // 4])

            ffn_norm(0)
            for gi in range(NG):
                t0 = gi * NGB
                for fc2 in range(16):
                    w_ap, w_tk = W1.next()
                    DMA("pool", w_ap, wblock(wff1_d[l], fc2 * 256, 256), writes=[w_tk])
                    if gi == 0 and fc2 >= 1:
                        for k in ((2 * (fc2 - 1), 2 * (fc2 - 1) + 1) if fc2 < 15 else (28, 29, 30, 31)):
                            DMA("pool", W2[:, k, :], wff2_d[l, k * 128:(k + 1) * 128, :], writes=[w2_toks[k]])
                    for sub in range(2):
                        fc = 2 * fc2 + sub
                        for tt in range(GT // 512):
                            bk = fbk.next()
                            for k in range(8):
                                MM(bank(bk), w_ap[:, k, sub * 128:(sub + 1) * 128], hF[:, k, tt * 512:(tt + 1) * 512], start=(k == 0), stop=(k == 7),
                                   reads=[w_tk, hF_tok[tt]], writes=[ptok[bk]])
                            r_ap, r_tk = rl.next()
                            ACT(r_ap, bank(bk), AF.Relu, [ptok[bk]], [r_tk])
                            TT("dve", aT[:, fc, tt * 512:(tt + 1) * 512], r_ap, r_ap, ALU.mult, [r_tk], [aT_tok[fc]])
                for b in range(NGB):
                    tb = t0 + b
                    pp = 2 + (b % 2)
                    if gi + 1 < NG:
                        t1_ = (gi + 1) * NGB + b
                        nh = nrm.part1(xrows(out_d, sq, t1_), xtok[sq][t1_], gpre, t_gf)
                    for half in range(2):
                        bk = 2 * pp + half
                        for k in range(32):
                            MM(bank(bk), aT[:, k, b * 128:(b + 1) * 128], W2[:, k, half * 512:(half + 1) * 512], start=(k == 0), stop=(k == 31),
                               reads=[aT_tok[k], w2_toks[k]], writes=[ptok[bk]])
                    pr.run(pp, sq, tb)
                    if gi + 1 < NG:
                        nrm.part2(nh[0], nh[1], hF, hF_tok[b // 4], b * 128, alt=b)
            fw.barrier()
        else:
            continue
        break

    fw.barrier()
    stats = fw.emit()
    fw.es.close()
    return nc, stats


def _host_layout(inp, depth=DEPTH):
    w_in = np.ascontiguousarray(inp["w_in"][:depth], dtype=np.float32)
    b_in = np.ascontiguousarray(inp["b_in"][:depth], dtype=np.float32)
    small_cols = list(range(FOX_F, FOX_F + 8)) + list(range(ML_I, ML_I + 4)) + list(range(ML_F, ML_F + 4))
    w_small = np.ascontiguousarray(w_in[:, :, small_cols])
    b_small = np.ascontiguousarray(b_in[:, None, small_cols])
    starts = ([FOX_Q + 128 * i for i in range(4)] + [FOX_K + 128 * i for i in range(4)] +
              [ML_Q + 128 * i for i in range(4)] + [ML_K + 128 * i for i in range(4)] +
              [G_U + 128 * i for i in range(4)] + [GATE + 128 * i for i in range(24)])
    b_col = np.stack([np.stack([b_in[l, s:s + 128] for s in starts], axis=1) for l in range(depth)], axis=0)
    cw = inp["conv_w"][:depth]
    conv_col = np.ascontiguousarray(cw.reshape(depth, 4, 8, 128).transpose(0, 3, 2, 1))
    wsT = np.ascontiguousarray(inp["gmlp_ws"][:depth].transpose(0, 3, 1, 2))
    d = {
        "norms": inp["norms"][:depth], "w_in": w_in, "b_in": b_in[:, None, :], "w_small": w_small, "b_small": b_small,
        "b_col": np.ascontiguousarray(b_col), "conv_col": conv_col,
        "mlstm_norm": inp["mlstm_norm"][:depth, None, :], "gmlp_norm": inp["gmlp_norm"][:depth, None, :],
        "gmlp_wsT": wsT, "gmlp_bs": inp["gmlp_bs"][:depth].reshape(depth, 1, 512),
        "w_branch": inp["w_branch"][:depth], "w_out": inp["w_out"][:depth], "w_xq": inp["w_xq"][:depth],
        "w_xkv": inp["w_xkv"][:depth], "w_xo": inp["w_xo"][:depth], "w_ff1": inp["w_ff1"][:depth], "w_ff2": inp["w_ff2"][:depth],
    }
    return {k: np.ascontiguousarray(v, dtype=np.float32) for k, v in d.items()}


_CACHE = {}


def kernel(**inputs):
    x = np.asarray(inputs["x"], dtype=np.float32)
    mem = np.asarray(inputs["mem"], dtype=np.float32)
    B, S, _ = x.shape
    n_cores = 8
    per = B // n_cores
    key = (S, per)
    if key not in _CACHE:
        _CACHE[key] = build_program(S, DEPTH, per)[0]
    nc = _CACHE[key]
    params = _host_layout(inputs)
    in_maps = []
    for c in range(n_cores):
        m = dict(params)
        m["x"] = np.ascontiguousarray(x[c * per:(c + 1) * per])
        m["mem"] = np.ascontiguousarray(mem[c * per:(c + 1) * per])
        in_maps.append(m)
    res = run_bass_kernel_spmd(nc, in_maps, core_ids=list(range(n_cores)))
    out = np.concatenate([np.asarray(r["out"]) for r in res.results], axis=0)
    return out.astype(np.float32)
```
